# Optimizing a Trainium2 kernel written in Bass

```python
import math
import numpy as np
import jax
import jax.numpy as jnp
from jax import lax

D_MODEL = 1024
BATCH = 16
SEQ = 4096
DEPTH = 2

GRID_W = 64
CTX_LEN = 256
EPS = 1e-6

GDN_HEADS = 4
GDN_DK = 64
GDN_DV = 64
GDN_CONV = 5
GDN_CHUNK = 64
GDN_QK = GDN_HEADS * GDN_DK
GDN_W = GDN_HEADS * GDN_DV
GDN_QKV = 2 * GDN_QK + GDN_W

MLA_HEADS = 8
MLA_Q_RANK = 384
MLA_KV_RANK = 256
MLA_NOPE = 64
MLA_ROPE = 32
MLA_DV = 64
MLA_W = MLA_HEADS * MLA_DV
MLA_SCALE = (MLA_NOPE + MLA_ROPE) ** -0.5
ROPE_THETA = 10000.0
Q_BLOCK = 128

GLA_HEADS = 4
GLA_DK = 32
GLA_DV = 64
GLA_QK = GLA_HEADS * GLA_DK
GLA_W = GLA_HEADS * GLA_DV
GLA_GATE_RANK = 16
GLA_GATE_NORM = 16.0
GLA_CHUNK = 64

MIX_W = GDN_W + MLA_W + GLA_W
IN_SPLITS = (GDN_QKV, GDN_W, 2 * GDN_HEADS, 2 * GDN_HEADS,
             MLA_Q_RANK, MLA_KV_RANK, MLA_ROPE, MLA_W,
             GLA_QK, GLA_QK, GLA_W, GLA_W, 2 * GLA_GATE_RANK)
N_IN = sum(IN_SPLITS)

kernel_name = 'hybrid_gdn_mla_gla_prefix_dit'


def rmsnorm(x, w=None):
    xf = x.astype(jnp.float32)
    y = xf * lax.rsqrt(jnp.mean(xf * xf, axis=-1, keepdims=True) + EPS)
    if w is not None:
        y = y * w.astype(jnp.float32)
    return y.astype(x.dtype)


def l2norm(x):
    xf = x.astype(jnp.float32)
    return (xf * lax.rsqrt(jnp.sum(xf * xf, axis=-1, keepdims=True) + EPS)).astype(x.dtype)


def split_cols(p, sizes):
    return jnp.split(p, np.cumsum(sizes)[:-1].tolist(), axis=-1)


def dwconv_centred(x, w):
    k, ch = w.shape
    pad = (k - 1) // 2
    return lax.conv_general_dilated(x, w[:, None, :].astype(x.dtype), window_strides=(1,),
                                    padding=[(pad, pad)], dimension_numbers=('NWC', 'WIO', 'NWC'),
                                    feature_group_count=ch)


def axial_rope(n_tokens):
    rows = n_tokens // GRID_W
    row = jnp.broadcast_to(jnp.arange(rows, dtype=jnp.float32)[:, None], (rows, GRID_W)).reshape(-1)
    col = jnp.broadcast_to(jnp.arange(GRID_W, dtype=jnp.float32)[None, :], (rows, GRID_W)).reshape(-1)
    n_freq = MLA_ROPE // 4
    inv = ROPE_THETA ** (-jnp.arange(n_freq, dtype=jnp.float32) / n_freq)
    ang = jnp.concatenate([row[:, None] * inv, col[:, None] * inv], axis=-1)
    return jnp.cos(ang), jnp.sin(ang)


def apply_rope(x, cos, sin):
    x1, x2 = jnp.split(x.astype(jnp.float32), 2, axis=-1)
    return jnp.concatenate([x1 * cos - x2 * sin, x2 * cos + x1 * sin], axis=-1).astype(x.dtype)


def to_chunks(a, size):
    b, t, h = a.shape[:3]
    a = a.reshape((b, t // size, size, h) + a.shape[3:])
    return jnp.moveaxis(a, 3, 1)


def from_chunks(a):
    a = jnp.moveaxis(a, 1, 3)
    return a.reshape((a.shape[0], -1) + a.shape[3:])


def gdn_chunk_scan(q, k, v, g, beta, s0):
    out_dtype = v.dtype
    q, k, v, g, beta = (to_chunks(a.astype(jnp.float32), GDN_CHUNK) for a in (q, k, v, g, beta))
    size = GDN_CHUNK
    incl = jnp.tril(jnp.ones((size, size), bool))
    strict = jnp.tril(jnp.ones((size, size), bool), -1)
    gc = jnp.cumsum(g, axis=-1)
    decay = jnp.where(incl, jnp.exp(jnp.where(incl, gc[..., :, None] - gc[..., None, :], 0.0)), 0.0)
    kb = k * beta[..., None]
    low = jnp.where(strict, jnp.einsum('bhntd,bhnsd->bhnts', kb, k) * decay, 0.0)
    eye = jnp.eye(size, dtype=jnp.float32)
    t_inv = lax.linalg.triangular_solve(eye + low, jnp.broadcast_to(eye, low.shape), left_side=True, lower=True)
    u = t_inv @ (v * beta[..., None])
    w = t_inv @ (kb * jnp.exp(gc)[..., None])
    intra = jnp.where(incl, jnp.einsum('bhntd,bhnsd->bhnts', q, k) * decay, 0.0)

    def step(state, xs):
        qi, ki, ui, wi, gi, ai = xs
        v_new = ui - wi @ state
        o = (qi * jnp.exp(gi)[..., None]) @ state + ai @ v_new
        g_last = gi[..., -1:]
        state = state * jnp.exp(g_last)[..., None] + jnp.einsum(
            'bhcd,bhce->bhde', ki * jnp.exp(g_last - gi)[..., None], v_new)
        return state, o

    xs = tuple(jnp.moveaxis(a, 2, 0) for a in (q, k, u, w, gc, intra))
    state, o = lax.scan(step, s0.astype(jnp.float32), xs)
    return from_chunks(jnp.moveaxis(o, 0, 2)).astype(out_dtype), state


def gla_chunk_scan(q, k, v, la, s0):
    out_dtype = v.dtype
    q, k, v, la = (to_chunks(a.astype(jnp.float32), GLA_CHUNK) for a in (q, k, v, la))
    incl = jnp.tril(jnp.ones((GLA_CHUNK, GLA_CHUNK), bool))[..., None]
    cum = jnp.cumsum(la, axis=3)

    def step(state, xs):
        qi, ki, vi, bi = xs
        diff = bi[:, :, :, None, :] - bi[:, :, None, :, :]
        dec = jnp.where(incl, jnp.exp(jnp.where(incl, diff, 0.0)), 0.0)
        scores = jnp.einsum('bhtd,bhsd,bhtsd->bhts', qi, ki, dec)
        o = (qi * jnp.exp(bi)) @ state + scores @ vi
        b_last = bi[:, :, -1:]
        state = state * jnp.exp(b_last[:, :, 0])[..., None] + jnp.einsum(
            'bhcd,bhce->bhde', ki * jnp.exp(b_last - bi), vi)
        return state, o

    xs = tuple(jnp.moveaxis(a, 2, 0) for a in (q, k, v, cum))
    state, o = lax.scan(step, s0.astype(jnp.float32), xs)
    return from_chunks(jnp.moveaxis(o, 0, 2)).astype(out_dtype), state


def bidirectional_scan(scan_fn, shared_c, shared_l, dirs_c, dirs_l, s0):
    outs_c, outs_l = [], []
    for d in range(2):
        flip = (lambda a: a[:, ::-1]) if d == 1 else (lambda a: a)
        o_c, s_ctx = scan_fn(*[flip(a) for a in shared_c + dirs_c[d]], s0)
        o_l, _ = scan_fn(*[flip(a) for a in shared_l + dirs_l[d]], s_ctx)
        outs_c.append(flip(o_c))
        outs_l.append(flip(o_l))
    return outs_c[0] + outs_c[1], outs_l[0] + outs_l[1]


def gdn_branch(parts_c, parts_l, conv_w, a_log, dt_bias, norm_w):
    def prep(qkv, a, b):
        bsz, t = qkv.shape[:2]
        h = jax.nn.silu(dwconv_centred(qkv, conv_w))
        q, k, v = jnp.split(h, [GDN_QK, 2 * GDN_QK], axis=-1)
        q = l2norm(q.reshape(bsz, t, GDN_HEADS, GDN_DK)) * GDN_DK ** -0.5
        k = l2norm(k.reshape(bsz, t, GDN_HEADS, GDN_DK))
        v = v.reshape(bsz, t, GDN_HEADS, GDN_DV)
        g = -jnp.exp(a_log.astype(jnp.float32)) * jax.nn.softplus(
            a.reshape(bsz, t, 2, GDN_HEADS).astype(jnp.float32) + dt_bias.astype(jnp.float32))
        beta = jax.nn.sigmoid(b.reshape(bsz, t, 2, GDN_HEADS).astype(jnp.float32))
        return (q, k, v), tuple((g[:, :, d], beta[:, :, d]) for d in range(2))

    qkv_c, z_c, a_c, b_c = parts_c
    qkv_l, z_l, a_l, b_l = parts_l
    shared_c, dirs_c = prep(qkv_c, a_c, b_c)
    shared_l, dirs_l = prep(qkv_l, a_l, b_l)
    s0 = jnp.zeros((qkv_l.shape[0], GDN_HEADS, GDN_DK, GDN_DV), jnp.float32)
    o_c, o_l = bidirectional_scan(gdn_chunk_scan, shared_c, shared_l, dirs_c, dirs_l, s0)

    def gate(o, z):
        return rmsnorm(o, norm_w).reshape(z.shape) * jax.nn.silu(z)
    return gate(o_c, z_c), gate(o_l, z_l)


def attend(q, k, v):
    s = jnp.einsum('bqhd,bkhd->bhqk', q, k).astype(jnp.float32) * MLA_SCALE
    p = jax.nn.softmax(s, axis=-1).astype(v.dtype)
    return jnp.einsum('bhqk,bkhd->bqhd', p, v)


def blocked_attention(q, k, v):
    bsz, t, h, dq = q.shape
    qb = jnp.moveaxis(q.reshape(bsz, t // Q_BLOCK, Q_BLOCK, h, dq), 1, 0)
    ob = lax.map(lambda qi: attend(qi, k, v), qb)
    return jnp.moveaxis(ob, 0, 1).reshape(bsz, t, h, -1)


def mla_q(cq, q_norm_w, w_uq, rope):
    bsz, t = cq.shape[:2]
    q = (rmsnorm(cq, q_norm_w) @ w_uq).reshape(bsz, t, MLA_HEADS, MLA_NOPE + MLA_ROPE)
    q_nope, q_pe = jnp.split(q, [MLA_NOPE], axis=-1)
    if rope is not None:
        cos, sin = rope
        q_pe = apply_rope(q_pe, cos[:, None], sin[:, None])
    return jnp.concatenate([q_nope, q_pe], axis=-1)


def mla_kv(ckv, k_rope, kv_norm_w, w_ukv, rope):
    bsz, t = ckv.shape[:2]
    kv = (rmsnorm(ckv, kv_norm_w) @ w_ukv).reshape(bsz, t, MLA_HEADS, MLA_NOPE + MLA_DV)
    k_nope, v = jnp.split(kv, [MLA_NOPE], axis=-1)
    if rope is not None:
        cos, sin = rope
        k_rope = apply_rope(k_rope, cos, sin)
    k_pe = jnp.broadcast_to(k_rope[:, :, None, :], (bsz, t, MLA_HEADS, MLA_ROPE))
    return jnp.concatenate([k_nope, k_pe], axis=-1), v


def mla_branch(parts_c, parts_l, q_norm_w, w_uq, kv_norm_w, w_ukv, rope, need_ctx_out):
    cq_c, ckv_c, kr_c, z_c = parts_c
    cq_l, ckv_l, kr_l, z_l = parts_l
    k_c, v_c = mla_kv(ckv_c, kr_c, kv_norm_w, w_ukv, None)
    k_l, v_l = mla_kv(ckv_l, kr_l, kv_norm_w, w_ukv, rope)
    q_l = mla_q(cq_l, q_norm_w, w_uq, rope)
    o_l = blocked_attention(q_l, jnp.concatenate([k_c, k_l], axis=1), jnp.concatenate([v_c, v_l], axis=1))
    out_l = o_l.reshape(z_l.shape) * jax.nn.silu(z_l)
    if not need_ctx_out:
        return None, out_l
    o_c = attend(mla_q(cq_c, q_norm_w, w_uq, None), k_c, v_c)
    return o_c.reshape(z_c.shape) * jax.nn.silu(z_c), out_l


def gla_branch(parts_c, parts_l, w_gk, b_gk, norm_w):
    def prep(q, k, v, g_low):
        bsz, t = q.shape[:2]
        q = q.reshape(bsz, t, GLA_HEADS, GLA_DK) * GLA_DK ** -0.5
        k = k.reshape(bsz, t, GLA_HEADS, GLA_DK)
        v = v.reshape(bsz, t, GLA_HEADS, GLA_DV)
        g_low = g_low.reshape(bsz, t, 2, GLA_GATE_RANK)
        gk = jnp.einsum('btgr,grk->btgk', g_low, w_gk) + b_gk
        la = (jax.nn.log_sigmoid(gk.astype(jnp.float32)) / GLA_GATE_NORM).reshape(bsz, t, 2, GLA_HEADS, GLA_DK)
        return (q, k, v), tuple((la[:, :, d],) for d in range(2))

    q_c, k_c, v_c, z_c, g_c = parts_c
    q_l, k_l, v_l, z_l, g_l = parts_l
    shared_c, dirs_c = prep(q_c, k_c, v_c, g_c)
    shared_l, dirs_l = prep(q_l, k_l, v_l, g_l)
    s0 = jnp.zeros((q_l.shape[0], GLA_HEADS, GLA_DK, GLA_DV), jnp.float32)
    o_c, o_l = bidirectional_scan(gla_chunk_scan, shared_c, shared_l, dirs_c, dirs_l, s0)

    def gate(o, z):
        return rmsnorm(o, norm_w).reshape(z.shape) * jax.nn.silu(z)
    return gate(o_c, z_c), gate(o_l, z_l)


def hybrid_layer(x_l, x_c, mod_l, mod_c, w_in, conv_w, a_log, dt_bias, gdn_norm_w, q_norm_w, w_uq,
                 kv_norm_w, w_ukv, w_gk, b_gk, gla_norm_w, w_out, rope, update_ctx):
    shift_l, scale_l, gate_l = jnp.split(mod_l[:, None, :], 3, axis=-1)
    shift_c, scale_c, gate_c = jnp.split(mod_c, 3, axis=-1)
    h_l = rmsnorm(x_l) * (1 + scale_l) + shift_l
    h_c = rmsnorm(x_c) * (1 + scale_c) + shift_c
    p_l = split_cols(h_l @ w_in, IN_SPLITS)
    p_c = split_cols(h_c @ w_in, IN_SPLITS)
    gdn_c, gdn_l = gdn_branch(tuple(p_c[0:4]), tuple(p_l[0:4]), conv_w, a_log, dt_bias, gdn_norm_w)
    mla_c, mla_l = mla_branch(tuple(p_c[4:8]), tuple(p_l[4:8]), q_norm_w, w_uq, kv_norm_w, w_ukv, rope, update_ctx)
    gla_c, gla_l = gla_branch(tuple(p_c[8:13]), tuple(p_l[8:13]), w_gk, b_gk, gla_norm_w)
    x_l = x_l + gate_l * (jnp.concatenate([gdn_l, mla_l, gla_l], axis=-1) @ w_out)
    if update_ctx:
        x_c = x_c + gate_c * (jnp.concatenate([gdn_c, mla_c, gla_c], axis=-1) @ w_out)
    return x_l, x_c


def setup_inputs(seed: int = 0) -> dict:
    key = jax.random.key(seed)
    ks = jax.random.split(key, 20)
    f32 = jnp.float32
    L = DEPTH

    def nrm(k, shape, s):
        return jax.random.normal(k, shape, f32) * s

    def gain(k, shape):
        return 1.0 + 0.02 * jax.random.normal(k, shape, f32)

    dt = jnp.exp(jax.random.uniform(ks[8], (L, 2, GDN_HEADS), f32, math.log(1e-3), math.log(1e-1)))
    return {
        'x': nrm(ks[0], (BATCH, SEQ, D_MODEL), 1.0),
        'c': nrm(ks[1], (BATCH, D_MODEL), 1.0),
        'ctx': nrm(ks[2], (BATCH, CTX_LEN, D_MODEL), 1.0),
        'c_ctx': nrm(ks[3], (D_MODEL,), 1.0),
        'w_ada': nrm(ks[4], (L, D_MODEL, 3 * D_MODEL), 0.5 * D_MODEL ** -0.5),
        'b_ada': nrm(ks[5], (L, 3 * D_MODEL), 0.02),
        'w_in': nrm(ks[6], (L, D_MODEL, N_IN), D_MODEL ** -0.5),
        'gdn_conv_w': nrm(ks[7], (L, GDN_CONV, GDN_QKV), GDN_CONV ** -0.5),
        'gdn_a_log': jnp.log(jax.random.uniform(ks[9], (L, 2, GDN_HEADS), f32, 1.0, 16.0)),
        'gdn_dt_bias': dt + jnp.log(-jnp.expm1(-dt)),
        'gdn_norm_w': gain(ks[10], (L, GDN_DV)),
        'mla_q_norm_w': gain(ks[11], (L, MLA_Q_RANK)),
        'mla_w_uq': nrm(ks[12], (L, MLA_Q_RANK, MLA_HEADS * (MLA_NOPE + MLA_ROPE)), MLA_Q_RANK ** -0.5),
        'mla_kv_norm_w': gain(ks[13], (L, MLA_KV_RANK)),
        'mla_w_ukv': nrm(ks[14], (L, MLA_KV_RANK, MLA_HEADS * (MLA_NOPE + MLA_DV)), MLA_KV_RANK ** -0.5),
        'gla_w_gk': nrm(ks[15], (L, 2, GLA_GATE_RANK, GLA_QK), GLA_GATE_RANK ** -0.5),
        'gla_b_gk': nrm(ks[16], (L, 2, GLA_QK), 0.1),
        'gla_norm_w': gain(ks[17], (L, GLA_DV)),
        'w_out': nrm(ks[18], (L, MIX_W, D_MODEL), MIX_W ** -0.5),
        'final_norm_w': gain(ks[19], (D_MODEL,)),
    }


def reference(x, c, ctx, c_ctx, w_ada, b_ada, w_in, gdn_conv_w, gdn_a_log, gdn_dt_bias, gdn_norm_w,
              mla_q_norm_w, mla_w_uq, mla_kv_norm_w, mla_w_ukv, gla_w_gk, gla_b_gk, gla_norm_w,
              w_out, final_norm_w):
    rope = axial_rope(x.shape[1])
    x_l, x_c = x, ctx
    for layer in range(DEPTH):
        mod_l = jax.nn.silu(c) @ w_ada[layer] + b_ada[layer]
        mod_c = jax.nn.silu(c_ctx) @ w_ada[layer] + b_ada[layer]
        x_l, x_c = hybrid_layer(x_l, x_c, mod_l, mod_c, w_in[layer], gdn_conv_w[layer], gdn_a_log[layer],
                                gdn_dt_bias[layer], gdn_norm_w[layer], mla_q_norm_w[layer], mla_w_uq[layer],
                                mla_kv_norm_w[layer], mla_w_ukv[layer], gla_w_gk[layer], gla_b_gk[layer],
                                gla_norm_w[layer], w_out[layer], rope, layer < DEPTH - 1)
    return rmsnorm(x_l, final_norm_w)
```

```python
import contextlib
import numpy as np
import concourse.bass as bass
import concourse.mybir as mybir
from concourse.bass_utils import run_bass_kernel_spmd

F32 = mybir.dt.float32
AF = mybir.ActivationFunctionType
ALU = mybir.AluOpType
AX = mybir.AxisListType

D = 1024
TC = 256
N_IN = 3024
O_QKV, O_ZG, O_A, O_B, O_CQ, O_CKV, O_KR, O_ZM, O_GQ, O_GK, O_GV, O_GZ, O_GL = (
    0, 768, 1024, 1032, 1040, 1424, 1680, 1712, 2224, 2352, 2480, 2736, 2992)
EPS = 1e-6
MLA_SCALE = 96 ** -0.5
NEG = -30000.0

PE, ACT, DVE, POOL, SP = "pe", "act", "dve", "pool", "sp"
COMPUTE = (PE, ACT, DVE, POOL)
NDSEM = 12


class Buf:
    def __init__(self, ap, key):
        self.ap = ap
        self.key = key

    def __getitem__(self, idx):
        return self.ap[idx]


def _key(x):
    if isinstance(x, Buf):
        return x.key
    if isinstance(x, tuple) and len(x) == 2 and isinstance(x[0], Buf):
        return (x[0].key, x[1])
    return x


class Sched:
    def __init__(self, nc):
        self.nc = nc
        self.ops = {e: [] for e in (PE, ACT, DVE, POOL, SP)}
        self.cnt = {e: 0 for e in COMPUTE}
        self.dcnt = {}
        self.dnext = {SP: 0, POOL: 0, ACT: 0}
        self.seen = {e: {} for e in (PE, ACT, DVE, POOL, SP)}
        self.lastw = {}
        self.readers = {}
        self.floor = {}
        self.sems = {}

    def _need(self, eng, ev, waits, same_ok=False):
        if ev is None:
            return
        k, v = ev
        if same_ok and k == ("c", eng):
            return
        if self.seen[eng].get(k, 0) >= v:
            return
        self.seen[eng][k] = v
        waits[k] = max(waits.get(k, 0), v)

    def barrier(self):
        fl = {("c", e): v for e, v in self.cnt.items() if v}
        fl.update(self.dcnt)
        self.floor = fl

    def op(self, eng, fn, reads=(), writes=(), dma=False):
        reads = [_key(r) for r in reads]
        writes = [_key(w) for w in writes]
        waits = {}
        for k, v in self.floor.items():
            self._need(eng, (k, v), waits)
        for r in reads:
            self._need(eng, self.lastw.get(r), waits)
        for w in writes:
            self._need(eng, self.lastw.get(w), waits, same_ok=not dma)
            for ev in self.readers.get(w, ()):
                self._need(eng, ev, waits, same_ok=not dma)
        if dma:
            slot = self.dnext[eng] % NDSEM
            self.dnext[eng] += 1
            k = ("d", eng, slot)
            if self.dcnt.get(k, 0):
                self._need(eng, (k, self.dcnt[k]), waits)
            self.dcnt[k] = self.dcnt.get(k, 0) + 16
            ev = (k, self.dcnt[k])
            inc = (k, 16)
        else:
            self.cnt[eng] += 1
            k = ("c", eng)
            ev = (k, self.cnt[eng])
            inc = (k, 1)
        for w in writes:
            self.lastw[w] = ev
            self.readers[w] = []
        for r in reads:
            if r not in writes:
                self.readers.setdefault(r, []).append(ev)
        self.ops[eng].append((sorted(waits.items(), key=str), fn, inc))
        return ev

    def emit(self, final_events=()):
        nc = self.nc
        keys = set()
        for lst in self.ops.values():
            for _, _, inc in lst:
                keys.add(inc[0])
        with contextlib.ExitStack() as st:
            for k in sorted(keys, key=str):
                self.sems[k] = st.enter_context(nc.semaphore("s_" + "_".join(str(x) for x in k)))
            block = st.enter_context(nc.Block())

            def run(name):
                def body(eng):
                    for waits, fn, inc in self.ops[name]:
                        for k, v in waits:
                            eng.wait_ge(self.sems[k], v)
                        fn(eng).then_inc(self.sems[inc[0]], inc[1])
                    if name == SP:
                        for k, v in final_events:
                            eng.wait_ge(self.sems[k], v)
                return body

            block.sync(run(SP))
            block.tensor(run(PE))
            block.scalar(run(ACT))
            block.vector(run(DVE))
            block.gpsimd(run(POOL))


class Arena:
    def __init__(self, handle, size):
        self.h = handle
        self.size = size
        self.off = 0
        self.gen = 0

    def reset(self):
        self.off = 0
        self.gen += 1

    def alloc(self, name, fshape, parts=128):
        n = int(np.prod(fshape))
        n_al = (n + 7) // 8 * 8
        assert self.off + n_al <= self.size, f"arena overflow at {name}: {self.off}+{n_al}>{self.size}"
        ap = self.h[0:parts, self.off:self.off + n]
        self.off += n_al
        if len(fshape) == 2:
            ap = ap.rearrange("p (a b) -> p a b", a=fshape[0])
        elif len(fshape) == 3:
            ap = ap.rearrange("p (a b c) -> p a b c", a=fshape[0], b=fshape[1])
        return Buf(ap, f"{name}@{self.gen}")


def bcast(ap, axis, n):
    a = ap.unsqueeze(axis)
    shp = list(a.shape)
    shp[axis] = n
    return a.broadcast_to(shp)


def _const_layout():
    p = np.arange(128)[:, None]
    f = np.arange(128)[None, :]
    items = []

    def add(name, arr):
        a = np.zeros((128, arr.shape[1]), np.float32)
        a[:arr.shape[0]] = arr
        items.append((name, a))

    add("ident", (p == f).astype(np.float32))
    add("ones", np.ones((128, 128), np.float32))
    add("trif", (p <= f).astype(np.float32))
    add("trib", (p >= f).astype(np.float32))
    add("bd", ((p // 64) == (f // 64)).astype(np.float32))

    def m(valid):
        return np.tile(np.where(valid, 0.0, NEG).astype(np.float32), (1, 4))

    same = (p // 64) == (f // 64)
    add("mL0", m((f < p) & same)); add("mS0", m((p < f) & same)); add("mI0", m(p <= f)); add("mO0", m((p < f) & ~same))
    add("mL1", m((f > p) & same)); add("mS1", m((p > f) & same)); add("mI1", m(p >= f)); add("mO1", m((p > f) & ~same))
    add("m010", np.tile((p <= f).astype(np.float32), (1, 4)))
    add("m011", np.tile((p >= f).astype(np.float32), (1, 4)))
    add("hm", (p // 32 == np.arange(4)[None, :]).astype(np.float32))
    selb = np.zeros((8, 2, 4, 64), np.float32)
    for d in range(2):
        for h in range(4):
            selb[d * 4 + h, d, h, :] = 1.0
    add("selb", selb.reshape(8, 512))
    esel = np.zeros((32, 96), np.float32)
    esel[np.arange(32), 64 + np.arange(32)] = 1.0
    add("esel", esel)
    selrow = np.zeros((65, 64), np.float32)
    selrow[64, :] = 1.0
    add("selrow", selrow)
    add("reset", np.tile((f % 128 != 0).astype(np.float32), (128, 4)))
    add("bvals", np.tile(np.array([[EPS, 1.0, 64 * EPS, 0.0]], np.float32), (128, 1)))
    off = {}
    o = 0
    for name, a in items:
        off[name] = (o, a.shape[1])
        o += a.shape[1]
    return off, np.concatenate([a for _, a in items], axis=1)


CONST_OFF, CONST_ARR = _const_layout()
NCONST = CONST_ARR.shape[1]


def _rope_tables(TL):
    T = TC + TL
    rows = TL // 64
    row = np.broadcast_to(np.arange(rows, dtype=np.float32)[:, None], (rows, 64)).reshape(-1)
    col = np.broadcast_to(np.arange(64, dtype=np.float32)[None, :], (rows, 64)).reshape(-1)
    inv = (np.float32(10000.0) ** (-np.arange(8, dtype=np.float32) / np.float32(8))).astype(np.float32)
    ang = np.concatenate([row[:, None] * inv, col[:, None] * inv], axis=-1).astype(np.float32)
    cos = np.cos(ang).astype(np.float32).T
    sin = np.sin(ang).astype(np.float32).T
    rq = np.zeros((96, 2, T), np.float32)
    rq[:, 0, :] = 1.0
    rq[64:80, 0, TC:] = cos
    rq[80:96, 0, TC:] = cos
    rq[64:80, 1, TC:] = -sin
    rq[80:96, 1, TC:] = sin
    rk = np.ascontiguousarray(rq[64:96])
    return rq, rk


class Cfg:
    def __init__(self, TL=4096, NB=2, L=2, debug=False, phases="0ABCDE"):
        self.TL, self.NB, self.L, self.debug, self.phases = TL, NB, L, debug, phases


ARENA_F = 40448


def build(cfg):
    nc = bass.Bass("TRN2", target_bir_lowering=False)
    TL, NB, L = cfg.TL, cfg.NB, cfg.L
    T = TC + TL
    NCH = T // 128
    NCC = TC // 128
    groups = [(0, TC)] + [(TC + i * 512, 512) for i in range(TL // 512)]

    def din(name, shape):
        return nc.dram_tensor(name, list(shape), F32, kind="ExternalInput").ap()

    scr_kind = "ExternalOutput" if cfg.debug else "Internal"

    def dscr(name, shape):
        return Buf(nc.dram_tensor(name, list(shape), F32, kind=scr_kind).ap(), name)

    x_in = din("x", [NB, TL, D]); ctx_in = din("ctx", [NB, TC, D]); cT = din("cT", [128, 8, NB + 1])
    w_ada = din("w_ada", [L, D, 3 * D]); b_adaT = din("b_adaT", [L, 128, 24])
    w_in = din("w_in", [L, D, N_IN]); convw_in = din("convw", [L, 128, 6, 5])
    alog_in = din("alog", [L, 128, 8]); dtb_in = din("dtb", [L, 128, 8])
    gnw_in = din("gnw", [L, 128, 256]); qnw_in = din("qnw", [L, 128, 3]); w_uq = din("w_uq", [L, 384, 768])
    kvnw_in = din("kvnw", [L, 128, 2]); w_ukv = din("w_ukv", [L, 256, 1024])
    w_gk = din("w_gk", [L, 2, 16, 128]); bgk_in = din("b_gkT", [L, 128, 2]); lnw_in = din("lnw", [L, 128, 256])
    w_out = din("w_out", [L, D, D]); fnw_in = din("fnw", [128, D])
    consts_in = din("consts", [128, NCONST]); ropeQ_in = din("ropeQ", [96, 2, T]); ropeK_in = din("ropeK", [32, 2, T])
    out = Buf(nc.dram_tensor("out", [NB, TL, D], F32, kind="ExternalOutput").ap(), "out")

    xs = [dscr(f"xs{b}", [T, D]) for b in range(NB)]
    qkvT = dscr("qkvT", [768, T]); cqT = dscr("cqT", [384, T]); ckvT = dscr("ckvT", [256, T])
    krT = dscr("krT", [64, T]); zmT = dscr("zmT", [512, T]); gqT = dscr("gqT", [128, T]); gkT = dscr("gkT", [128, T])
    glT = dscr("glT", [32, T]); bT = dscr("bT", [8, T]); PT = dscr("PT", [T, 784]); yT = dscr("yT", [1024, T])
    o_d = [dscr(f"o_d{d}", [T, 256]) for d in range(2)]
    gqk = dscr("gqk", [2, 256, T]); ktm_d = dscr("ktm_d", [T, 256]); vtm_d = dscr("vtm_d", [T, 256])
    QT = dscr("QT", [8, 96, T]); KT = dscr("KT", [8, 96, T]); Vd = dscr("Vd", [T, 512])

    with contextlib.ExitStack() as st:
        def sb(name, shape):
            return Buf(st.enter_context(nc.sbuf_tensor(name, list(shape), F32)), name)

        cst = sb("cst", [128, NCONST])
        arena_h = st.enter_context(nc.sbuf_tensor("arena", [128, ARENA_F], F32))
        A = Arena(arena_h, ARENA_F)
        cs_t = sb("cs_t", [128, 8, NB + 1])
        modc = sb("modc", [128, 24, NB + 1])
        gate_bc = [sb(f"gate_bc{i}", [128, D]) for i in range(NB + 1)]
        badaT = sb("badaT", [128, 24])
        small = sb("small", [128, 64])
        P = [Buf(st.enter_context(nc.psum_tensor(f"P{i}", [128, 512], F32)), f"P{i}") for i in range(8)]
        S = Sched(nc)
        tog = [0]

        def C(name, rows=128):
            o, w = CONST_OFF[name]
            return cst[0:rows, o:o + w]

        def dma_in(out_ap, in_ap, rd=(), wr=(), q=SP):
            return S.op(q, lambda e: e.dma_start(out=out_ap, in_=in_ap), reads=rd, writes=wr, dma=True)

        def dma_out(out_ap, in_ap, rd=(), wr=()):
            return S.op(POOL, lambda e: e.dma_start(out=out_ap, in_=in_ap), reads=rd, writes=wr, dma=True)

        def mm(out_ap, lhsT, rhs, start=True, stop=True, rd=(), wr=()):
            S.op(PE, lambda e: e.matmul(out_ap, lhsT=lhsT, rhs=rhs, start=start, stop=stop), reads=rd, writes=wr)

        def tr(out_ap, in_ap, rd=(), wr=()):
            n = in_ap.shape[0]
            idn = C("ident")[0:n, 0:n]
            S.op(PE, lambda e: e.transpose(out_ap, in_ap, idn), reads=list(rd) + [cst], writes=wr)

        def act(out_ap, in_ap, func, bias=None, scale=None, accum=None, rd=(), wr=()):
            kw = {}
            if isinstance(bias, float):
                col = {EPS: 0, 1.0: 1, 64 * EPS: 2}[bias]
                bias = C("bvals")[0:in_ap.shape[0], col:col + 1]
                rd = list(rd) + [cst]
            if bias is not None:
                kw["bias"] = bias
            if scale is not None:
                kw["scale"] = scale
            if accum is not None:
                kw["accum_out"] = accum
            S.op(ACT, lambda e: e.activation(out=out_ap, in_=in_ap, func=func, **kw), reads=rd, writes=wr)

        def tt(eng, out_ap, in0, in1, op, rd=(), wr=()):
            S.op(eng, lambda e: e.tensor_tensor(out=out_ap, in0=in0, in1=in1, op=op), reads=rd, writes=wr)

        def ts(eng, out_ap, in0, s1, s2, op0, op1=None, rd=(), wr=()):
            if op1 is None:
                S.op(eng, lambda e: e.tensor_scalar(out=out_ap, in0=in0, scalar1=s1, scalar2=None, op0=op0),
                     reads=rd, writes=wr)
            else:
                S.op(eng, lambda e: e.tensor_scalar(out=out_ap, in0=in0, scalar1=s1, scalar2=s2, op0=op0, op1=op1),
                     reads=rd, writes=wr)

        def stt(out_ap, in0, scalar, in1, op0, op1, rd=(), wr=()):
            S.op(DVE, lambda e: e.scalar_tensor_tensor(out=out_ap, in0=in0, scalar=scalar, in1=in1, op0=op0, op1=op1),
                 reads=rd, writes=wr)

        def recip(out_ap, in_ap, rd=(), wr=()):
            S.op(DVE, lambda e: e.reciprocal(out=out_ap, in_=in_ap), reads=rd, writes=wr)

        def copy(out_ap, in_ap, rd=(), wr=(), eng=None):
            if eng is None:
                tog[0] ^= 1
                eng = ACT if tog[0] else DVE
            if eng == ACT:
                S.op(ACT, lambda e: e.copy(out=out_ap, in_=in_ap), reads=rd, writes=wr)
            else:
                S.op(eng, lambda e: e.tensor_copy(out=out_ap, in_=in_ap), reads=rd, writes=wr)

        def memset(eng, ap, val, wr=()):
            S.op(eng, lambda e: e.memset(ap, val), writes=wr)

        dma_in(cst[:, :], consts_in[:, :], wr=[cst])
        dma_in(cs_t[:, :, :], cT[:, :, :], wr=[cs_t])
        act(cs_t[:, :, :], cs_t[:, :, :], AF.Silu, rd=[cs_t], wr=[cs_t])

        def phase0(l):
            S.barrier(); A.reset()
            wada = A.alloc("wada", [8, 3 * D])
            dg = [A.alloc(f"dg{i}", [128]) for i in range(2)]
            for k in range(8):
                dma_in(wada[:, k, :], w_ada[l, k * 128:(k + 1) * 128, :], wr=[wada])
            dma_in(badaT[:, :], b_adaT[l, :, :], wr=[badaT])
            nv = NB + 1
            for blk in range(24):
                for k in range(8):
                    mm(P[0][:, blk * nv:(blk + 1) * nv], wada[:, k, blk * 128:(blk + 1) * 128], cs_t[:, k, :],
                       start=(k == 0), stop=(k == 7), rd=[wada, cs_t], wr=[P[0]])
            tt(DVE, modc[:, :, :], P[0][:, 0:24 * nv].rearrange("p (a b) -> p a b", b=nv),
               bcast(badaT[:, :], 2, nv), ALU.add, rd=[P[0], badaT], wr=[modc])
            ts(DVE, modc[:, 8:16, :], modc[:, 8:16, :], 1.0, None, ALU.add, rd=[modc], wr=[modc])
            for i in range(nv):
                for k in range(8):
                    d_ = dg[k % 2]
                    ts(DVE, d_[:, :], C("ident"), modc[:, 16 + k, i:i + 1], None, ALU.mult, rd=[modc, cst], wr=[d_])
                    pb = P[1 + k // 4]
                    mm(pb[:, (k % 4) * 128:(k % 4 + 1) * 128], C("ones"), d_[:, :], rd=[d_, cst], wr=[pb])
                copy(gate_bc[i][:, 0:512], P[1][:, :], rd=[P[1]], wr=[gate_bc[i]])
                copy(gate_bc[i][:, 512:1024], P[2][:, :], rd=[P[2]], wr=[gate_bc[i]])

        def x_src(l, b, tok):
            if l == 0:
                if tok < TC:
                    return ctx_in[b, tok:tok + 128, :], []
                return x_in[b, tok - TC:tok - TC + 128, :], []
            return xs[b][tok:tok + 128, :], [xs[b]]

        def phaseA(l, b):
            S.barrier(); A.reset()
            win = A.alloc("win", [8, N_IN]); wsw = A.alloc("wsw", [8, 32])
            hT = [A.alloc(f"hT{i}", [8, 512]) for i in range(2)]
            xt = [A.alloc(f"xt{i}", [D]) for i in range(2)]
            junk = A.alloc("junk", [D])
            stg = [A.alloc(f"stg{i}", [512]) for i in range(3)]
            stt_ = [A.alloc(f"stt{i}", [784]) for i in range(2)]
            ss = [A.alloc(f"ss{i}", [2]) for i in range(2)]
            for k in range(8):
                rows = slice(k * 128, (k + 1) * 128)
                dma_in(win[:, k, :], w_in[l, rows, :], wr=[win])
                dma_in(wsw[:, k, 0:16], w_in[l, rows, O_KR + 16:O_KR + 32], wr=[wsw])
                dma_in(wsw[:, k, 16:32], w_in[l, rows, O_KR:O_KR + 16], wr=[wsw])
            blocks = []
            for i in range(6):
                blocks.append((qkvT, i * 128, 128, win, O_QKV + i * 128))
            blocks.append((bT, 0, 8, win, O_B))
            for i in range(3):
                blocks.append((cqT, i * 128, 128, win, O_CQ + i * 128))
            for i in range(2):
                blocks.append((ckvT, i * 128, 128, win, O_CKV + i * 128))
            blocks.append((krT, 0, 32, win, O_KR))
            blocks.append((krT, 32, 32, wsw, 0))
            for i in range(4):
                blocks.append((zmT, i * 128, 128, win, O_ZM + i * 128))
            blocks.append((gqT, 0, 128, win, O_GQ))
            blocks.append((gkT, 0, 128, win, O_GK))
            blocks.append((glT, 0, 32, win, O_GL))
            ti = 0
            bi = 0
            for gi, (t0, n) in enumerate(groups):
                h = hT[gi % 2]
                col = NB if t0 < TC else b
                for i in range(n // 128):
                    tok = t0 + i * 128
                    xb = xt[ti % 2]; sb_ = ss[ti % 2]
                    src, srd = x_src(l, b, tok)
                    dma_in(xb[:, :], src, rd=srd, wr=[xb])
                    act(junk[:, :], xb[:, :], AF.Square, accum=sb_[:, 0:1], rd=[xb], wr=[junk, sb_])
                    act(sb_[:, 0:1], sb_[:, 0:1], AF.Sqrt, bias=EPS, scale=1.0 / D, rd=[sb_], wr=[sb_])
                    recip(sb_[:, 1:2], sb_[:, 0:1], rd=[sb_], wr=[sb_])
                    ts(DVE, xb[:, :], xb[:, :], sb_[:, 1:2], None, ALU.mult, rd=[xb, sb_], wr=[xb])
                    pa, pb = P[(ti % 2) * 2], P[(ti % 2) * 2 + 1]
                    for k in range(8):
                        pp = pa if k < 4 else pb
                        tr(pp[:, (k % 4) * 128:(k % 4 + 1) * 128], xb[:, k * 128:(k + 1) * 128], rd=[xb], wr=[pp])
                    for k in range(8):
                        pp = pa if k < 4 else pb
                        src_ps = pp[:, (k % 4) * 128:(k % 4 + 1) * 128]
                        dst = h[:, k, i * 128:(i + 1) * 128]
                        if k % 2 == 0:
                            act(dst, src_ps, AF.Identity, bias=modc[:, k, col:col + 1], scale=modc[:, 8 + k, col:col + 1],
                                rd=[pp, modc], wr=[h])
                        else:
                            ts(DVE, dst, src_ps, modc[:, 8 + k, col:col + 1], modc[:, k, col:col + 1], ALU.mult, ALU.add,
                               rd=[pp, modc], wr=[h])
                    ti += 1
                for (dst, r0, m, wt, c0) in blocks:
                    pp = P[4 + bi % 4]; sg = stg[bi % 3]
                    for k in range(8):
                        mm(pp[0:m, 0:n], wt[:, k, c0:c0 + m], h[:, k, 0:n], start=(k == 0), stop=(k == 7),
                           rd=[wt, h], wr=[pp])
                    copy(sg[0:m, 0:n], pp[0:m, 0:n], rd=[pp], wr=[sg])
                    dma_out(dst[r0:r0 + m, t0:t0 + n], sg[0:m, 0:n], rd=[sg], wr=[dst])
                    bi += 1
                for i in range(n // 128):
                    tok = t0 + i * 128
                    so = stt_[i % 2]
                    for (c0, w, o0) in ((O_ZG, 272, 0), (O_GV, 512, 272)):
                        pp = P[4 + bi % 4]
                        for k in range(8):
                            mm(pp[:, 0:w], h[:, k, i * 128:(i + 1) * 128], win[:, k, c0:c0 + w], start=(k == 0),
                               stop=(k == 7), rd=[win, h], wr=[pp])
                        copy(so[:, o0:o0 + w], pp[:, 0:w], rd=[pp], wr=[so])
                        bi += 1
                    dma_out(PT[tok:tok + 128, :], so[:, :], rd=[so], wr=[PT])

        def gating(l, nw_in, zcol, yrow):
            S.barrier(); A.reset()
            nw = A.alloc("nw", [256])
            dma_in(nw[:, :], nw_in[l, :, :], wr=[nw])
            o0 = [A.alloc(f"o0_{i}", [4, 256]) for i in range(2)]
            o1 = [A.alloc(f"o1_{i}", [4, 256]) for i in range(2)]
            zz = [A.alloc(f"zz{i}", [4, 256]) for i in range(2)]
            sq = A.alloc("sq", [4, 256])
            rs = [A.alloc(f"rs{i}", [32]) for i in range(2)]
            yst = [A.alloc(f"yst{i}", [2, 512]) for i in range(2)]
            gi = 0
            for (t0, n) in groups:
                if l == L - 1 and t0 < TC:
                    continue
                nc_ = n // 128
                a0, a1, z_, r_, ys = o0[gi % 2], o1[gi % 2], zz[gi % 2], rs[gi % 2], yst[gi % 2]
                dma_in(a0[:, 0:nc_, :], o_d[0][t0:t0 + n, :].rearrange("(c p) f -> p c f", p=128), rd=[o_d[0]], wr=[a0])
                dma_in(a1[:, 0:nc_, :], o_d[1][t0:t0 + n, :].rearrange("(c p) f -> p c f", p=128), rd=[o_d[1]], wr=[a1])
                dma_in(z_[:, 0:nc_, :], PT[t0:t0 + n, zcol:zcol + 256].rearrange("(c p) f -> p c f", p=128),
                       rd=[PT], wr=[z_])
                tt(DVE, a0[:, 0:nc_, :], a0[:, 0:nc_, :], a1[:, 0:nc_, :], ALU.add, rd=[a0, a1], wr=[a0])
                tt(POOL, sq[:, 0:nc_, :], a0[:, 0:nc_, :], a0[:, 0:nc_, :], ALU.mult, rd=[a0], wr=[sq])
                S.op(DVE, lambda e, o_=r_[:, 0:nc_ * 4], i_=sq[:, 0:nc_, :].rearrange("p c (h e) -> p (c h) e", h=4):
                     e.tensor_reduce(out=o_, in_=i_, axis=AX.X, op=ALU.add), reads=[sq], writes=[r_])
                act(r_[:, 0:nc_ * 4], r_[:, 0:nc_ * 4], AF.Sqrt, bias=EPS, scale=1.0 / 64, rd=[r_], wr=[r_])
                recip(r_[:, 16:16 + nc_ * 4], r_[:, 0:nc_ * 4], rd=[r_], wr=[r_])
                tt(DVE, a0[:, 0:nc_, :].rearrange("p c (h e) -> p (c h) e", h=4),
                   a0[:, 0:nc_, :].rearrange("p c (h e) -> p (c h) e", h=4),
                   bcast(r_[:, 16:16 + nc_ * 4], 2, 64), ALU.mult, rd=[a0, r_], wr=[a0])
                tt(POOL, a0[:, 0:nc_, :], a0[:, 0:nc_, :], bcast(nw[:, :], 1, nc_), ALU.mult, rd=[a0, nw], wr=[a0])
                act(z_[:, 0:nc_, :], z_[:, 0:nc_, :], AF.Silu, rd=[z_], wr=[z_])
                tt(DVE, a0[:, 0:nc_, :], a0[:, 0:nc_, :], z_[:, 0:nc_, :], ALU.mult, rd=[a0, z_], wr=[a0])
                for jj in range(2):
                    pp = P[(gi * 2 + jj) % 8]
                    for c in range(nc_):
                        tr(pp[:, c * 128:(c + 1) * 128], a0[:, c, jj * 128:(jj + 1) * 128], rd=[a0], wr=[pp])
                    copy(ys[:, jj, 0:n], pp[:, 0:n], rd=[pp], wr=[ys])
                dma_out(yT[yrow:yrow + 256, t0:t0 + n].rearrange("(j p) t -> p j t", p=128), ys[:, :, 0:n],
                        rd=[ys], wr=[yT])
                gi += 1

        def chunk_orders():
            fwd = list(range(NCH))
            bwd = list(range(NCC - 1, -1, -1)) + list(range(NCH - 1, NCC - 1, -1))
            return [fwd, bwd]

        def phaseB(l, b):
            S.barrier(); A.reset()
            cw = A.alloc("cw", [6, 5])
            dma_in(cw[:, :, :], convw_in[l, :, :, :], wr=[cw])
            xpad = [A.alloc(f"xpad{i}", [TL + 4]) for i in range(2)]
            acc = [A.alloc(f"acc{i}", [TL]) for i in range(2)]
            rn = [A.alloc(f"rn{i}", [512]) for i in range(2)]
            tst = [A.alloc(f"tst{i}", [4, 128]) for i in range(2)]
            it = 0
            ci = 0
            for blk in range(6):
                for (s0, n) in ((0, TC), (TC, TL)):
                    xp, ac = xpad[it % 2], acc[it % 2]
                    memset(POOL, xp[:, 0:2], 0.0, wr=[xp])
                    memset(POOL, xp[:, n + 2:n + 4], 0.0, wr=[xp])
                    dma_in(xp[:, 2:n + 2], qkvT[blk * 128:(blk + 1) * 128, s0:s0 + n], rd=[qkvT], wr=[xp])
                    ts(DVE, ac[:, 0:n], xp[:, 0:n], cw[:, blk, 0:1], None, ALU.mult, rd=[xp, cw], wr=[ac])
                    for j in range(1, 5):
                        stt(ac[:, 0:n], xp[:, j:j + n], cw[:, blk, j:j + 1], ac[:, 0:n], ALU.mult, ALU.add,
                            rd=[xp, cw, ac], wr=[ac])
                    act(ac[:, 0:n], ac[:, 0:n], AF.Silu, rd=[ac], wr=[ac])
                    if blk < 4:
                        sc = 64.0 if blk < 2 else 1.0
                        tt(POOL, xp[:, 0:n], ac[:, 0:n], ac[:, 0:n], ALU.mult, rd=[ac], wr=[xp])
                        for g0 in range(0, n, 512):
                            gn = min(512, n - g0)
                            pp = P[ci % 8]; r_ = rn[ci % 2]; ci += 1
                            mm(pp[:, 0:gn], C("bd"), xp[:, g0:g0 + gn], rd=[xp, cst], wr=[pp])
                            act(r_[:, 0:gn], pp[:, 0:gn], AF.Sqrt, bias=sc * EPS, scale=sc, rd=[pp], wr=[r_])
                            recip(r_[:, 0:gn], r_[:, 0:gn], rd=[r_], wr=[r_])
                            tt(DVE, ac[:, g0:g0 + gn], ac[:, g0:g0 + gn], r_[:, 0:gn], ALU.mult, rd=[ac, r_], wr=[ac])
                        dma_out(gqk[blk // 2, (blk % 2) * 128:(blk % 2 + 1) * 128, s0:s0 + n], ac[:, 0:n], rd=[ac], wr=[gqk])
                    if blk >= 2:
                        dst = ktm_d if blk < 4 else vtm_d
                        jj = blk % 2
                        for g0 in range(0, n, 512):
                            gn = min(512, n - g0)
                            pp = P[ci % 8]; ts_ = tst[ci % 2]; ci += 1
                            for c in range(gn // 128):
                                tr(pp[:, c * 128:(c + 1) * 128], ac[:, g0 + c * 128:g0 + (c + 1) * 128], rd=[ac], wr=[pp])
                            copy(ts_[:, 0:gn // 128, :], pp[:, 0:gn].rearrange("p (c f) -> p c f", f=128), rd=[pp], wr=[ts_])
                            dma_out(dst[s0 + g0:s0 + g0 + gn, jj * 128:(jj + 1) * 128].rearrange("(c p) f -> p c f", p=128),
                                    ts_[:, 0:gn // 128, :], rd=[ts_], wr=[dst])
                    it += 1
            S.barrier(); A.reset()
            ab = A.alloc("ab", [NCH, 16])
            dma_in(ab[:, :, :], PT[:, 256:272].rearrange("(c p) f -> p c f", p=128), rd=[PT], wr=[ab])
            prm = A.alloc("prm", [16])
            dma_in(prm[:, 0:8], alog_in[l, :, :], wr=[prm])
            dma_in(prm[:, 8:16], dtb_in[l, :, :], wr=[prm])
            act(prm[:, 0:8], prm[:, 0:8], AF.Exp, rd=[prm], wr=[prm])
            xg = A.alloc("xg", [NCH, 8]); t1 = A.alloc("t1", [NCH, 8]); t2 = A.alloc("t2", [NCH, 8])
            g_ = A.alloc("g", [2, NCH, 4]); beta = A.alloc("beta", [2, NCH, 4])
            gc = A.alloc("gc", [2, NCH, 4]); gtot = A.alloc("gtot", [2, NCH, 4])
            eg = A.alloc("eg", [2, NCH, 4]); edec = A.alloc("edec", [2, NCH, 4]); egtot = A.alloc("egtot", [2, NCH, 4])
            negegb = A.alloc("negegb", [2, NCH, 4])
            tt(DVE, xg[:, :, :], ab[:, :, 0:8], bcast(prm[:, 8:16], 1, NCH), ALU.add, rd=[ab, prm], wr=[xg])
            act(t1[:, :, :], xg[:, :, :], AF.Abs, rd=[xg], wr=[t1])
            act(t1[:, :, :], t1[:, :, :], AF.Exp, scale=-1.0, rd=[t1], wr=[t1])
            act(t1[:, :, :], t1[:, :, :], AF.Ln, bias=1.0, rd=[t1], wr=[t1])
            ts(DVE, t2[:, :, :], xg[:, :, :], 0.0, None, ALU.max, rd=[xg], wr=[t2])
            tt(DVE, t1[:, :, :], t1[:, :, :], t2[:, :, :], ALU.add, rd=[t1, t2], wr=[t1])
            stt(t1[:, :, :], t1[:, :, :], -1.0, bcast(prm[:, 0:8], 1, NCH), ALU.mult, ALU.mult, rd=[t1, prm], wr=[t1])
            act(ab[:, :, 8:16], ab[:, :, 8:16], AF.Sigmoid, rd=[ab], wr=[ab])
            for d in range(2):
                copy(g_[:, d, :, :], t1[:, :, d * 4:(d + 1) * 4], rd=[t1], wr=[g_], eng=DVE)
                copy(beta[:, d, :, :], ab[:, :, 8 + d * 4:8 + (d + 1) * 4], rd=[ab], wr=[beta], eng=DVE)
            for d in range(2):
                mm(P[0][:, d * NCH * 4:(d + 1) * NCH * 4], C("trif" if d == 0 else "trib"),
                   g_[:, d, :, :].rearrange("p c h -> p (c h)"), rd=[g_, cst], wr=[P[0]])
            mm(P[1][:, 0:2 * NCH * 4], C("ones"), g_[:, :, :, :].rearrange("p d c h -> p (d c h)"), rd=[g_, cst], wr=[P[1]])
            fl = "p d c h -> p (d c h)"
            copy(gc[:, :, :, :].rearrange(fl), P[0][:, 0:2 * NCH * 4], rd=[P[0]], wr=[gc], eng=DVE)
            copy(gtot[:, :, :, :].rearrange(fl), P[1][:, 0:2 * NCH * 4], rd=[P[1]], wr=[gtot], eng=DVE)
            act(eg[:, :, :, :].rearrange(fl), gc[:, :, :, :].rearrange(fl), AF.Exp, rd=[gc], wr=[eg])
            act(egtot[:, :, :, :].rearrange(fl), gtot[:, :, :, :].rearrange(fl), AF.Exp, rd=[gtot], wr=[egtot])
            tt(DVE, edec[:, :, :, :].rearrange(fl), gtot[:, :, :, :].rearrange(fl), gc[:, :, :, :].rearrange(fl),
               ALU.subtract, rd=[gtot, gc], wr=[edec])
            act(edec[:, :, :, :].rearrange(fl), edec[:, :, :, :].rearrange(fl), AF.Exp, rd=[edec], wr=[edec])
            stt(negegb[:, :, :, :].rearrange(fl), eg[:, :, :, :].rearrange(fl), -1.0, beta[:, :, :, :].rearrange(fl),
                ALU.mult, ALU.mult, rd=[eg, beta], wr=[negegb])
            betaT = A.alloc("betaT", [T], parts=8)
            dma_in(betaT[:, :], bT[:, :], rd=[bT], wr=[betaT])
            act(betaT[:, :], betaT[:, :], AF.Sigmoid, rd=[betaT], wr=[betaT])
            W = {}
            for d in range(2):
                w = {}
                for nm, shp, parts in (("qTc", [4, 128], 64), ("kTc", [4, 128], 64), ("ktmc", [256], 128), ("vtmc", [256], 128)):
                    w[nm] = [A.alloc(f"{nm}{d}{i}", shp, parts=parts) for i in range(2)]
                for nm, shp, parts in (("kbT", [4, 128], 64), ("dgx", [4, 128], 128), ("E", [4, 128], 128), ("V", [4, 512], 128),
                                       ("LMA", [4, 512], 128), ("ysb", [4, 64], 128), ("zsb", [4, 64], 128), ("Pa", [4, 128], 128), ("Pb", [4, 128], 128),
                                       ("La", [4, 128], 128), ("Lb", [4, 128], 128), ("Ma", [4, 128], 128), ("Mb", [4, 128], 128),
                                       ("vb", [4, 64], 128), ("rhs2", [4, 64], 128), ("vnew", [4, 64], 128),
                                       ("tmp", [4, 64], 128), ("kdec", [256], 128), ("Sst", [4, 64], 64), ("tmpS", [4, 64], 64)):
                    w[nm] = A.alloc(f"{nm}{d}", shp, parts=parts)
                w["osb"] = [A.alloc(f"osb{d}{i}", [4, 64]) for i in range(2)]
                W[d] = w
                memset(POOL, w["Sst"][:, :, :], 0.0, wr=[w["Sst"]])
            orders = chunk_orders()
            ident4 = bcast(C("ident"), 1, 4)
            o_sel = CONST_OFF["selb"][0]
            v4 = "p (h s) -> p h s"
            h4 = "p (h e) -> p h e"
            for step in range(NCH):
                for d in range(2):
                    c = orders[d][step]
                    w = W[d]
                    B0, B1, B2, B3 = P[d * 4], P[d * 4 + 1], P[d * 4 + 2], P[d * 4 + 3]
                    tok = c * 128
                    qTc, kTc, ktmc, vtmc = (w[nm][step % 2] for nm in ("qTc", "kTc", "ktmc", "vtmc"))
                    dma_in(qTc[:, :, :], gqk[0, :, tok:tok + 128].rearrange("(h p) t -> p h t", p=64), rd=[gqk], wr=[qTc])
                    dma_in(kTc[:, :, :], gqk[1, :, tok:tok + 128].rearrange("(h p) t -> p h t", p=64), rd=[gqk], wr=[kTc])
                    dma_in(ktmc[:, :], ktm_d[tok:tok + 128, :], rd=[ktm_d], wr=[ktmc])
                    dma_in(vtmc[:, :], vtm_d[tok:tok + 128, :], rd=[vtm_d], wr=[vtmc])
                    for h in range(4):
                        sel = cst[0:8, o_sel + (d * 4 + h) * 64:o_sel + (d * 4 + h + 1) * 64]
                        mm(B3[0:64, h * 128:(h + 1) * 128], sel, betaT[:, tok:tok + 128], rd=[betaT, cst], wr=[B3])
                    kbT = w["kbT"]
                    tt(DVE, kbT[:, :, :], kTc[:, :, :], B3[0:64, :].rearrange(v4, h=4), ALU.mult, rd=[kTc, B3], wr=[kbT])
                    for h in range(4):
                        hs = slice(h * 128, (h + 1) * 128)
                        mm(B0[:, hs], kbT[:, h, :], kTc[:, h, :], rd=[kbT, kTc], wr=[B0])
                        mm(B1[:, hs], kTc[:, h, :], kbT[:, h, :], rd=[kbT, kTc], wr=[B1])
                        mm(B2[:, hs], kTc[:, h, :], qTc[:, h, :], rd=[qTc, kTc], wr=[B2])
                    gcc = gc[:, d, c, :]
                    dgx, E, V, LMA = w["dgx"], w["E"], w["V"], w["LMA"]
                    tt(DVE, dgx[:, :, :], ident4, bcast(gcc, 2, 128), ALU.mult, rd=[gc, cst], wr=[dgx])
                    mm(B3[:, :], C("ones"), dgx[:, :, :].rearrange("p h s -> p (h s)"), rd=[dgx, cst], wr=[B3])
                    tt(DVE, E[:, :, :], B3[:, :].rearrange(v4, h=4), bcast(gcc, 2, 128), ALU.subtract, rd=[B3, gc], wr=[E])
                    Ef = E[:, :, :].rearrange("p h s -> p (h s)")
                    stt(V[:, 0, :], Ef, -1.0, C(f"mL{d}"), ALU.mult, ALU.min, rd=[E, cst], wr=[V])
                    tt(DVE, V[:, 1, :], Ef, C(f"mS{d}"), ALU.min, rd=[E, cst], wr=[V])
                    tt(DVE, V[:, 2, :], Ef, C(f"mI{d}"), ALU.min, rd=[E, cst], wr=[V])
                    tt(DVE, V[:, 3, :], Ef, C(f"mO{d}"), ALU.min, rd=[E, cst], wr=[V])
                    act(V[:, :, :], V[:, :, :], AF.Exp, rd=[V], wr=[V])
                    tt(DVE, LMA[:, 3, :], B1[:, :], V[:, 3, :], ALU.mult, rd=[B1, V], wr=[LMA])
                    tt(DVE, LMA[:, 0, :], B0[:, :], V[:, 0, :], ALU.mult, rd=[B0, V], wr=[LMA])
                    tt(DVE, LMA[:, 1, :], B1[:, :], V[:, 1, :], ALU.mult, rd=[B1, V], wr=[LMA])
                    tt(DVE, LMA[:, 2, :], B2[:, :], V[:, 2, :], ALU.mult, rd=[B2, V], wr=[LMA])
                    Lc = LMA[:, 0, :].rearrange(v4, h=4); Mc = LMA[:, 1, :].rearrange(v4, h=4)
                    Lk, Mk = LMA, LMA
                    ATm = LMA[:, 2, :].rearrange(v4, h=4)
                    Pc, Pk = w["Pa"][:, :, :], w["Pa"]
                    tt(DVE, Pc, ident4, Mc, ALU.subtract, rd=[LMA, cst], wr=[Pk])
                    Moff = LMA[:, 3, :].rearrange(v4, h=4)
                    for lvl in range(5):
                        last = lvl == 4
                        Ln_b = w["La"] if lvl % 2 == 0 else w["Lb"]
                        Mn_b = w["Ma"] if lvl % 2 == 0 else w["Mb"]
                        Pn_b = w["Pb"] if lvl % 2 == 0 else w["Pa"]
                        for h in range(4):
                            mm(B0[:, h * 128:(h + 1) * 128], Mc[:, h, :], Lc[:, h, :], rd=[Lk, Mk], wr=[B0])
                        if not last:
                            for h in range(4):
                                mm(B1[:, h * 128:(h + 1) * 128], Lc[:, h, :], Mc[:, h, :], rd=[Lk, Mk], wr=[B1])
                        copy(Ln_b[:, :, :], B0[:, :].rearrange(v4, h=4), rd=[B0], wr=[Ln_b], eng=ACT)
                        if not last:
                            copy(Mn_b[:, :, :], B1[:, :].rearrange(v4, h=4), rd=[B1], wr=[Mn_b], eng=DVE)
                        for h in range(4):
                            mm(B3[:, h * 128:(h + 1) * 128], Ln_b[:, h, :], Pc[:, h, :], rd=[Ln_b, Pk], wr=[B3])
                        tt(DVE, Pn_b[:, :, :], Pc, B3[:, :].rearrange(v4, h=4), ALU.add, rd=[Pk, B3], wr=[Pn_b])
                        Lc, Lk = Ln_b[:, :, :], Ln_b
                        if not last:
                            Mc, Mk = Mn_b[:, :, :], Mn_b
                        Pc, Pk = Pn_b[:, :, :], Pn_b
                    Sst, tmpS = w["Sst"], w["tmpS"]
                    for h in range(4):
                        mm(B2[:, h * 64:(h + 1) * 64], kTc[:, h, :], Sst[:, h, :], rd=[kTc, Sst], wr=[B2])
                        mm(B2[:, 256 + h * 64:256 + (h + 1) * 64], qTc[:, h, :], Sst[:, h, :], rd=[qTc, Sst], wr=[B2])
                    vb, rhs2, vnew, tmp, kdec = w["vb"], w["rhs2"], w["vnew"], w["tmp"], w["kdec"]
                    tt(POOL, vb[:, :, :], vtmc[:, :].rearrange(h4, h=4), bcast(beta[:, d, c, :], 2, 64), ALU.mult,
                       rd=[vtmc, beta], wr=[vb])
                    tt(POOL, kdec[:, :].rearrange(h4, h=4), ktmc[:, :].rearrange(h4, h=4), bcast(edec[:, d, c, :], 2, 64),
                       ALU.mult, rd=[ktmc, edec], wr=[kdec])
                    tt(DVE, rhs2[:, :, :], B2[:, 0:256].rearrange(h4, h=4), bcast(negegb[:, d, c, :], 2, 64), ALU.mult,
                       rd=[B2, negegb], wr=[rhs2])
                    tt(DVE, rhs2[:, :, :], rhs2[:, :, :], vb[:, :, :], ALU.add, rd=[rhs2, vb], wr=[rhs2])
                    ysb, zsb = w["ysb"], w["zsb"]
                    for h in range(4):
                        mm(B3[:, h * 64:(h + 1) * 64], Pc[:, h, :], rhs2[:, h, :], rd=[Pk, rhs2], wr=[B3])
                    copy(ysb[:, :, :], B3[:, 0:256].rearrange(h4, h=4), rd=[B3], wr=[ysb], eng=ACT)
                    for h in range(4):
                        mm(B3[:, 256 + h * 64:256 + (h + 1) * 64], Moff[:, h, :], ysb[:, h, :], rd=[LMA, ysb], wr=[B3])
                    tt(DVE, zsb[:, :, :], rhs2[:, :, :], B3[:, 256:512].rearrange(h4, h=4), ALU.subtract, rd=[rhs2, B3], wr=[zsb])
                    for h in range(4):
                        mm(B3[:, h * 64:(h + 1) * 64], Pc[:, h, :], zsb[:, h, :], rd=[Pk, zsb], wr=[B3])
                    copy(vnew[:, :, :], B3[:, 0:256].rearrange(h4, h=4), rd=[B3], wr=[vnew], eng=ACT)
                    for h in range(4):
                        mm(B3[:, 256 + h * 64:256 + (h + 1) * 64], ATm[:, h, :], vnew[:, h, :], rd=[LMA, vnew], wr=[B3])
                    osb = w["osb"][step % 2]
                    tt(DVE, tmp[:, :, :], B2[:, 256:512].rearrange(h4, h=4), bcast(eg[:, d, c, :], 2, 64), ALU.mult,
                       rd=[B2, eg], wr=[tmp])
                    tt(DVE, osb[:, :, :], tmp[:, :, :], B3[:, 256:512].rearrange(h4, h=4), ALU.add, rd=[tmp, B3], wr=[osb])
                    if not (l == L - 1 and c < NCC):
                        dma_out(o_d[d][tok:tok + 128, :], osb[:, :, :].rearrange("p h e -> p (h e)"), rd=[osb], wr=[o_d[d]])
                    for h in range(4):
                        mm(B0[0:64, h * 64:(h + 1) * 64], kdec[:, h * 64:(h + 1) * 64], vnew[:, h, :], rd=[kdec, vnew], wr=[B0])
                    tt(POOL, tmpS[:, :, :], Sst[:, :, :], bcast(egtot[0:64, d, c, :], 2, 64), ALU.mult, rd=[Sst, egtot], wr=[tmpS])
                    tt(DVE, Sst[:, :, :], tmpS[:, :, :], B0[0:64, 0:256].rearrange(h4, h=4), ALU.add, rd=[tmpS, B0], wr=[Sst])
            gating(l, gnw_in, 0, 0)

        def phaseC(l, b):
            S.barrier(); A.reset()
            wg = [A.alloc(f"wg{d}", [128], parts=32) for d in range(2)]
            nbg = A.alloc("nbg", [2])
            for d in range(2):
                memset(POOL, wg[d][:, :], 0.0, wr=[wg[d]])
                dma_in(wg[d][d * 16:(d + 1) * 16, :], w_gk[l, d, :, :], wr=[wg[d]])
            dma_in(nbg[:, :], bgk_in[l, :, :], wr=[nbg])
            ts(DVE, nbg[:, :], nbg[:, :], -1.0, None, ALU.mult, rd=[nbg], wr=[nbg])
            W = {}
            for d in range(2):
                w = {}
                for nm, shp, parts in (("gl", [512], 32), ("gq", [512], 128), ("gk", [512], 128), ("nl", [512], 128),
                                       ("cs", [512], 128), ("t", [512], 128), ("eq", [512], 128), ("ek", [512], 128),
                                       ("ekh", [512], 128), ("tot", [8], 128), ("qm", [4, 128], 128), ("sc", [4, 128], 128),
                                       ("khT", [128], 128), ("Sx", [256], 128)):
                    w[nm] = A.alloc(f"c{nm}{d}", shp, parts=parts)
                w["vt"] = [A.alloc(f"cvt{d}{i}", [256]) for i in range(2)]
                w["osb"] = [A.alloc(f"cosb{d}{i}", [256]) for i in range(2)]
                W[d] = w
                memset(POOL, w["Sx"][:, :], 0.0, wr=[w["Sx"]])
            gfwd = list(groups)
            gbwd = [groups[0]] + list(reversed(groups[1:]))
            gorder = [gfwd, gbwd]
            cnt = 0
            for gstep in range(len(groups)):
                for d in range(2):
                    w = W[d]
                    t0, n = gorder[d][gstep]
                    nc_ = n // 128
                    B0, B1, B2, B3 = P[d * 4], P[d * 4 + 1], P[d * 4 + 2], P[d * 4 + 3]
                    gl, gq, gk_, nl, cs, t_, eq, ek, ekh, tot = (w[k_] for k_ in
                                                                   ("gl", "gq", "gk", "nl", "cs", "t", "eq", "ek", "ekh", "tot"))
                    dma_in(gl[:, 0:n], glT[:, t0:t0 + n], rd=[glT], wr=[gl])
                    dma_in(gq[:, 0:n], gqT[:, t0:t0 + n], rd=[gqT], wr=[gq])
                    dma_in(gk_[:, 0:n], gkT[:, t0:t0 + n], rd=[gkT], wr=[gk_])
                    mm(B0[:, 0:n], wg[d][:, :], gl[:, 0:n], rd=[wg[d], gl], wr=[B0])
                    act(nl[:, 0:n], B0[:, 0:n], AF.Exp, bias=nbg[:, d:d + 1], scale=-1.0, rd=[B0, nbg], wr=[nl])
                    act(nl[:, 0:n], nl[:, 0:n], AF.Ln, bias=1.0, rd=[nl], wr=[nl])
                    S.op(DVE, lambda e, o_=cs[:, 0:n], a_=C("reset")[:, 0:n], b_=nl[:, 0:n]:
                         e.tensor_tensor_scan(out=o_, data0=a_, data1=b_, initial=0.0, op0=ALU.mult, op1=ALU.add),
                         reads=[nl, cst], writes=[cs])
                    c3 = "p (c s) -> p c s"
                    copy(tot[:, 0:nc_], cs[:, 0:n].rearrange(c3, s=128)[:, :, 127], rd=[cs], wr=[tot], eng=DVE)
                    if d == 1:
                        tt(DVE, t_[:, 0:n], nl[:, 0:n], cs[:, 0:n], ALU.subtract, rd=[nl, cs], wr=[t_])
                        tt(DVE, cs[:, 0:n].rearrange(c3, s=128), t_[:, 0:n].rearrange(c3, s=128),
                           bcast(tot[:, 0:nc_], 2, 128), ALU.add, rd=[t_, tot], wr=[cs])
                    act(eq[:, 0:n], cs[:, 0:n], AF.Exp, scale=-1.0 / 16, rd=[cs], wr=[eq])
                    act(ek[:, 0:n], cs[:, 0:n], AF.Exp, scale=1.0 / 16, rd=[cs], wr=[ek])
                    tt(DVE, t_[:, 0:n].rearrange(c3, s=128), cs[:, 0:n].rearrange(c3, s=128),
                       bcast(tot[:, 0:nc_], 2, 128), ALU.subtract, rd=[cs, tot], wr=[t_])
                    act(ekh[:, 0:n], t_[:, 0:n], AF.Exp, scale=1.0 / 16, rd=[t_], wr=[ekh])
                    act(tot[:, 4:4 + nc_], tot[:, 0:nc_], AF.Exp, scale=-1.0 / 16, rd=[tot], wr=[tot])
                    stt(eq[:, 0:n], gq[:, 0:n], 32 ** -0.5, eq[:, 0:n], ALU.mult, ALU.mult, rd=[gq, eq], wr=[eq])
                    tt(DVE, ek[:, 0:n], gk_[:, 0:n], ek[:, 0:n], ALU.mult, rd=[gk_, ek], wr=[ek])
                    tt(POOL, ekh[:, 0:n], gk_[:, 0:n], ekh[:, 0:n], ALU.mult, rd=[gk_, ekh], wr=[ekh])
                    clist = list(range(nc_)) if d == 0 else list(range(nc_ - 1, -1, -1))
                    for ci in clist:
                        tok = t0 + ci * 128
                        csl = slice(ci * 128, (ci + 1) * 128)
                        qm, sc, khT, Sx = w["qm"], w["sc"], w["khT"], w["Sx"]
                        vt = w["vt"][cnt % 2]; osb = w["osb"][cnt % 2]; cnt += 1
                        dma_in(vt[:, :], PT[tok:tok + 128, 272:528], rd=[PT], wr=[vt])
                        tt(DVE, qm[:, :, :], bcast(eq[:, csl], 1, 4), bcast(C("hm"), 2, 128), ALU.mult,
                           rd=[eq, cst], wr=[qm])
                        mm(B1[:, :], ek[:, csl], qm[:, :, :].rearrange("p h t -> p (h t)"), rd=[ek, qm], wr=[B1])
                        tt(DVE, sc[:, :, :].rearrange("p h t -> p (h t)"), B1[:, :], C(f"m01{d}"), ALU.mult,
                           rd=[B1, cst], wr=[sc])
                        tr(B2[:, 0:128], ekh[:, csl], rd=[ekh], wr=[B2])
                        copy(khT[:, :], B2[:, 0:128], rd=[B2], wr=[khT], eng=ACT)
                        for h in range(4):
                            mm(B3[:, h * 64:(h + 1) * 64], qm[:, h, :], Sx[:, h * 64:(h + 1) * 64], start=True, stop=False,
                               rd=[qm, Sx], wr=[B3])
                            mm(B3[:, h * 64:(h + 1) * 64], sc[:, h, :], vt[:, h * 64:(h + 1) * 64], start=False, stop=True,
                               rd=[sc, vt], wr=[B3])
                        copy(osb[:, :], B3[:, 0:256], rd=[B3], wr=[osb], eng=ACT)
                        if not (l == L - 1 and tok < TC):
                            dma_out(o_d[d][tok:tok + 128, :], osb[:, :], rd=[osb], wr=[o_d[d]])
                        mm(B2[:, 256:512], khT[:, :], vt[:, :], rd=[khT, vt], wr=[B2])
                        stt(Sx[:, :], Sx[:, :], tot[:, 4 + ci:5 + ci], B2[:, 256:512], ALU.mult, ALU.add,
                            rd=[Sx, tot, B2], wr=[Sx])
            gating(l, lnw_in, 528, 768)

        def phaseD(l, b):
            S.barrier(); A.reset()
            wq = A.alloc("wq", [3, 768]); wqs = A.alloc("wqs", [3, 768])
            wkn = A.alloc("wkn", [2, 8, 96]); wv = A.alloc("wv", [2, 8, 64])
            nq = A.alloc("nq", [3]); nkv = A.alloc("nkv", [2])
            rQ = A.alloc("rQ", [2, T], parts=96); rK = A.alloc("rK", [2, T], parts=32)
            dma_in(rQ[:, :, :], ropeQ_in[:, :, :], wr=[rQ])
            dma_in(rK[:, :, :], ropeK_in[:, :, :], wr=[rK])
            dma_in(nq[:, :], qnw_in[l, :, :], wr=[nq])
            dma_in(nkv[:, :], kvnw_in[l, :, :], wr=[nkv])
            memset(POOL, wqs[:, :, :], 0.0, wr=[wqs])
            memset(POOL, wkn[:, :, :, :], 0.0, wr=[wkn])
            for j in range(3):
                rows = slice(j * 128, (j + 1) * 128)
                dma_in(wq[:, j, :], w_uq[l, rows, :], wr=[wq])
                src = w_uq[l, rows, :].rearrange("k (h c) -> k h c", c=96)
                dst = wqs[:, j, :].rearrange("k (h c) -> k h c", c=96)
                dma_in(dst[:, :, 64:80], src[:, :, 80:96], wr=[wqs])
                dma_in(dst[:, :, 80:96], src[:, :, 64:80], wr=[wqs])
            for j in range(2):
                rows = slice(j * 128, (j + 1) * 128)
                src = w_ukv[l, rows, :].rearrange("k (h c) -> k h c", c=128)
                dma_in(wkn[:, j, :, 0:64], src[:, :, 0:64], wr=[wkn])
                dma_in(wv[:, j, :, :], src[:, :, 64:128], wr=[wv])
            for j in range(3):
                ts(DVE, wq[:, j, :], wq[:, j, :], nq[:, j:j + 1], None, ALU.mult, rd=[wq, nq], wr=[wq])
                ts(DVE, wqs[:, j, :], wqs[:, j, :], nq[:, j:j + 1], None, ALU.mult, rd=[wqs, nq], wr=[wqs])
            for j in range(2):
                ts(DVE, wkn[:, j, :, :].rearrange("p h c -> p (h c)"), wkn[:, j, :, :].rearrange("p h c -> p (h c)"),
                   nkv[:, j:j + 1], None, ALU.mult, rd=[wkn, nkv], wr=[wkn])
                ts(DVE, wv[:, j, :, :].rearrange("p h c -> p (h c)"), wv[:, j, :, :].rearrange("p h c -> p (h c)"),
                   nkv[:, j:j + 1], None, ALU.mult, rd=[wv, nkv], wr=[wv])
            cq = [A.alloc(f"cq{i}", [3, 512]) for i in range(2)]
            ckv = [A.alloc(f"ckv{i}", [2, 512]) for i in range(2)]
            kr = [A.alloc(f"kr{i}", [2, 512], parts=32) for i in range(2)]
            sq = A.alloc("sqd", [3, 512]); rr = A.alloc("rr", [512]); t1 = A.alloc("dt1", [512], parts=96)
            t2 = A.alloc("dt2", [512], parts=96)
            qst = [A.alloc(f"qst{i}", [512], parts=96) for i in range(2)]
            kst = [A.alloc(f"kst{i}", [512], parts=96) for i in range(2)]
            vst = [A.alloc(f"vst{i}", [512]) for i in range(2)]
            mx = A.alloc("mx", [16]); mt = A.alloc("mt", [2])
            memset(POOL, mx[:, :], 0.0, wr=[mx])
            pi = 0

            def nextP():
                nonlocal pi
                pi += 1
                return P[pi % 8]

            def norm_stat(src_buf, src_ap, rows, n, col):
                tt(POOL, t2[0:rows, 0:n], src_ap, src_ap, ALU.mult, rd=[src_buf], wr=[t2])
                pp = nextP()
                mm(pp[:, 0:n], C("ones")[0:rows, :], t2[0:rows, 0:n], rd=[t2, cst], wr=[pp])
                S.op(DVE, lambda e, o_=mt[:, 0:1], i_=pp[:, 0:n]: e.tensor_reduce(out=o_, in_=i_, axis=AX.X, op=ALU.max),
                     reads=[pp], writes=[mt])
                tt(DVE, mx[:, col:col + 1], mx[:, col:col + 1], mt[:, 0:1], ALU.max, rd=[mx, mt], wr=[mx])

            for gi, (t0, n) in enumerate(groups):
                cq_, ckv_, kr_ = cq[gi % 2], ckv[gi % 2], kr[gi % 2]
                dma_in(cq_[:, :, 0:n], cqT[:, t0:t0 + n].rearrange("(j p) t -> p j t", p=128), rd=[cqT], wr=[cq_])
                dma_in(ckv_[:, :, 0:n], ckvT[:, t0:t0 + n].rearrange("(j p) t -> p j t", p=128), rd=[ckvT], wr=[ckv_])
                dma_in(kr_[:, :, 0:n], krT[:, t0:t0 + n].rearrange("(j p) t -> p j t", p=32), rd=[krT], wr=[kr_])
                for (buf, nj, dim) in ((cq_, 3, 384.0), (ckv_, 2, 256.0)):
                    act(sq[:, 0:nj, 0:n], buf[:, 0:nj, 0:n], AF.Square, rd=[buf], wr=[sq])
                    pp = nextP()
                    for j in range(nj):
                        mm(pp[:, 0:n], C("ones"), sq[:, j, 0:n], start=(j == 0), stop=(j == nj - 1), rd=[sq, cst], wr=[pp])
                    act(rr[:, 0:n], pp[:, 0:n], AF.Sqrt, bias=EPS, scale=1.0 / dim, rd=[pp], wr=[rr])
                    recip(rr[:, 0:n], rr[:, 0:n], rd=[rr], wr=[rr])
                    tt(DVE, buf[:, 0:nj, 0:n], buf[:, 0:nj, 0:n], bcast(rr[:, 0:n], 1, nj), ALU.mult, rd=[buf, rr], wr=[buf])
                tt(DVE, kr_[:, 0, 0:n], kr_[:, 0, 0:n], rK[:, 0, t0:t0 + n], ALU.mult, rd=[kr_, rK], wr=[kr_])
                tt(POOL, kr_[:, 1, 0:n], kr_[:, 1, 0:n], rK[:, 1, t0:t0 + n], ALU.mult, rd=[kr_, rK], wr=[kr_])
                tt(DVE, kr_[:, 0, 0:n], kr_[:, 0, 0:n], kr_[:, 1, 0:n], ALU.add, rd=[kr_], wr=[kr_])
                need_q = (t0 >= TC) or (l < L - 1)
                for h in range(8):
                    pp = nextP(); ks = kst[h % 2]
                    for j in range(2):
                        mm(pp[0:96, 0:n], wkn[:, j, h, :], ckv_[:, j, 0:n], start=(j == 0), stop=False, rd=[wkn, ckv_], wr=[pp])
                    mm(pp[0:96, 0:n], C("esel", 32), kr_[:, 0, 0:n], start=False, stop=True, rd=[kr_, cst], wr=[pp])
                    copy(ks[:, 0:n], pp[0:96, 0:n], rd=[pp], wr=[ks])
                    dma_out(KT[h, :, t0:t0 + n], ks[:, 0:n], rd=[ks], wr=[KT])
                    norm_stat(ks, ks[:, 0:n], 96, n, 8 + h)
                    if not need_q:
                        continue
                    p1 = nextP(); p2 = nextP(); qs = qst[h % 2]
                    for j in range(3):
                        mm(p1[0:96, 0:n], wq[:, j, h * 96:(h + 1) * 96], cq_[:, j, 0:n], start=(j == 0), stop=(j == 2),
                           rd=[wq, cq_], wr=[p1])
                    for j in range(3):
                        mm(p2[0:96, 0:n], wqs[:, j, h * 96:(h + 1) * 96], cq_[:, j, 0:n], start=(j == 0), stop=(j == 2),
                           rd=[wqs, cq_], wr=[p2])
                    tt(DVE, t1[:, 0:n], p1[0:96, 0:n], rQ[:, 0, t0:t0 + n], ALU.mult, rd=[p1, rQ], wr=[t1])
                    tt(DVE, qs[:, 0:n], p2[0:96, 0:n], rQ[:, 1, t0:t0 + n], ALU.mult, rd=[p2, rQ], wr=[qs])
                    tt(POOL, qs[:, 0:n], qs[:, 0:n], t1[:, 0:n], ALU.add, rd=[qs, t1], wr=[qs])
                    dma_out(QT[h, :, t0:t0 + n], qs[:, 0:n], rd=[qs], wr=[QT])
                    norm_stat(qs, qs[:, 0:n], 96, n, h)
                for i in range(n // 128):
                    pp = nextP(); vs = vst[i % 2]
                    for j in range(2):
                        mm(pp[:, :], ckv_[:, j, i * 128:(i + 1) * 128], wv[:, j, :, :].rearrange("p h c -> p (h c)"),
                           start=(j == 0), stop=(j == 1), rd=[ckv_, wv], wr=[pp])
                    copy(vs[:, :], pp[:, :], rd=[pp], wr=[vs])
                    dma_out(Vd[t0 + i * 128:t0 + (i + 1) * 128, :], vs[:, :], rd=[vs], wr=[Vd])
            tt(DVE, small[:, 0:8], mx[:, 0:8], mx[:, 8:16], ALU.mult, rd=[mx], wr=[small])
            act(small[:, 0:8], small[:, 0:8], AF.Sqrt, rd=[small], wr=[small])
            ts(DVE, small[:, 0:8], small[:, 0:8], -MLA_SCALE, None, ALU.mult, rd=[small], wr=[small])
            S.barrier(); A.reset()
            kt = [A.alloc(f"kt{i}", [T], parts=96) for i in range(2)]
            vh = [A.alloc(f"vh{i}", [NCH, 65]) for i in range(2)]
            for i in range(2):
                memset(POOL, vh[i][:, :, 64:65], 1.0, wr=[vh[i]])
            qg = [A.alloc(f"qg{i}", [512], parts=96) for i in range(2)]
            zg = [A.alloc(f"zg{i}", [512], parts=64) for i in range(2)]
            pT = [A.alloc(f"pT{i}", [512]) for i in range(3)]
            ot = [A.alloc(f"ot{i}", [512], parts=65) for i in range(2)]
            rd_ = [A.alloc(f"rden{i}", [512], parts=64) for i in range(2)]
            ys = [A.alloc(f"ysd{i}", [512], parts=64) for i in range(2)]
            qi = 0
            si = 0
            for h in range(8):
                kt_, vh_ = kt[h % 2], vh[h % 2]
                dma_in(kt_[:, :], KT[h, :, :], rd=[KT], wr=[kt_])
                dma_in(vh_[:, :, 0:64], Vd[:, h * 64:(h + 1) * 64].rearrange("(c p) e -> p c e", p=128), rd=[Vd], wr=[vh_])
                for (t0, n) in groups:
                    if t0 < TC:
                        if l == L - 1:
                            continue
                        kchunks = list(range(NCC))
                    else:
                        kchunks = list(range(NCH))
                    q_, z_, o_, r_, y_ = qg[qi % 2], zg[qi % 2], ot[qi % 2], rd_[qi % 2], ys[qi % 2]
                    pacc = P[6 + qi % 2]
                    qi += 1
                    dma_in(q_[:, 0:n], QT[h, :, t0:t0 + n], rd=[QT], wr=[q_])
                    dma_in(z_[:, 0:n], zmT[h * 64:(h + 1) * 64, t0:t0 + n], rd=[zmT], wr=[z_])
                    act(z_[:, 0:n], z_[:, 0:n], AF.Silu, rd=[z_], wr=[z_])
                    for ki, c in enumerate(kchunks):
                        ps_ = P[si % 4]; p_ = pT[si % 3]; si += 1
                        mm(ps_[:, 0:n], kt_[:, c * 128:(c + 1) * 128], q_[:, 0:n], rd=[kt_, q_], wr=[ps_])
                        act(p_[:, 0:n], ps_[:, 0:n], AF.Exp, bias=small[:, h:h + 1], scale=MLA_SCALE, rd=[ps_, small], wr=[p_])
                        mm(pacc[0:65, 0:n], vh_[:, c, :], p_[:, 0:n], start=(ki == 0), stop=(ki == len(kchunks) - 1),
                           rd=[vh_, p_], wr=[pacc])
                    copy(o_[:, 0:n], pacc[0:65, 0:n], rd=[pacc], wr=[o_], eng=DVE)
                    pd = P[4 + qi % 2]
                    mm(pd[0:64, 0:n], C("selrow", 65), o_[:, 0:n], rd=[o_, cst], wr=[pd])
                    recip(r_[:, 0:n], pd[0:64, 0:n], rd=[pd], wr=[r_])
                    tt(DVE, y_[:, 0:n], o_[0:64, 0:n], r_[:, 0:n], ALU.mult, rd=[o_, r_], wr=[y_])
                    tt(POOL, y_[:, 0:n], y_[:, 0:n], z_[:, 0:n], ALU.mult, rd=[y_, z_], wr=[y_])
                    dma_out(yT[256 + h * 64:256 + (h + 1) * 64, t0:t0 + n], y_[:, 0:n], rd=[y_], wr=[yT])

        out_events = []

        def phaseE(l, b):
            S.barrier(); A.reset()
            wo = A.alloc("wo", [8, D])
            for k in range(8):
                dma_in(wo[:, k, :], w_out[l, k * 128:(k + 1) * 128, :], wr=[wo])
            fn = A.alloc("fn", [D])
            if l == L - 1:
                dma_in(fn[:, :], fnw_in[:, :], wr=[fn])
            yt = [A.alloc(f"yt{i}", [8, 128]) for i in range(2)]
            xt = [A.alloc(f"ext{i}", [D]) for i in range(2)]
            tm = [A.alloc(f"etm{i}", [D]) for i in range(2)]
            junk = A.alloc("ejunk", [D])
            ss = [A.alloc(f"ess{i}", [2]) for i in range(2)]
            ti = 0
            for c in range(NCH):
                tok = c * 128
                if tok < TC and l == L - 1:
                    continue
                gb = gate_bc[NB if tok < TC else b]
                y_, x_, t_, s_ = yt[ti % 2], xt[ti % 2], tm[ti % 2], ss[ti % 2]
                pa, pb = P[(ti % 4) * 2], P[(ti % 4) * 2 + 1]
                ti += 1
                dma_in(y_[:, :, :], yT[:, tok:tok + 128].rearrange("(k p) t -> p k t", p=128), rd=[yT], wr=[y_])
                src, srd = x_src(l, b, tok)
                dma_in(x_[:, :], src, rd=srd, wr=[x_])
                for half, pp in ((0, pa), (1, pb)):
                    for k in range(8):
                        mm(pp[:, :], y_[:, k, :], wo[:, k, half * 512:(half + 1) * 512], start=(k == 0), stop=(k == 7),
                           rd=[y_, wo], wr=[pp])
                    hs = slice(half * 512, (half + 1) * 512)
                    tt(DVE, t_[:, hs], pp[:, :], gb[:, hs], ALU.mult, rd=[pp, gb], wr=[t_])
                tt(POOL, t_[:, :], t_[:, :], x_[:, :], ALU.add, rd=[t_, x_], wr=[t_])
                if l < L - 1:
                    dma_out(xs[b][tok:tok + 128, :], t_[:, :], rd=[t_], wr=[xs[b]])
                else:
                    act(junk[:, :], t_[:, :], AF.Square, accum=s_[:, 0:1], rd=[t_], wr=[junk, s_])
                    act(s_[:, 0:1], s_[:, 0:1], AF.Sqrt, bias=EPS, scale=1.0 / D, rd=[s_], wr=[s_])
                    recip(s_[:, 1:2], s_[:, 0:1], rd=[s_], wr=[s_])
                    stt(t_[:, :], t_[:, :], s_[:, 1:2], fn[:, :], ALU.mult, ALU.mult, rd=[t_, s_, fn], wr=[t_])
                    ev = dma_out(out[b, tok - TC:tok - TC + 128, :], t_[:, :], rd=[t_], wr=[("out", b, c)])
                    out_events.append(ev)

        for l in range(L):
            if "0" in cfg.phases:
                phase0(l)
            for b in range(NB):
                if "A" in cfg.phases:
                    phaseA(l, b)
                if "B" in cfg.phases:
                    phaseB(l, b)
                if "C" in cfg.phases:
                    phaseC(l, b)
                if "D" in cfg.phases:
                    phaseD(l, b)
                if "E" in cfg.phases:
                    phaseE(l, b)
        S.barrier()
        fin = list(S.floor.items())
        S.emit(final_events=fin)
        build.stats = {k: len(v) for k, v in S.ops.items()}
    return nc


def _col(v, nchunk):
    return np.ascontiguousarray(np.asarray(v, np.float32).reshape(nchunk, 128).T)


def make_in_maps(inp, n_cores, NB, TL, L):
    f = lambda a: np.ascontiguousarray(np.asarray(a, dtype=np.float32))
    rq, rk = _rope_tables(TL)
    shared = {
        "w_ada": f(inp["w_ada"]), "w_in": f(inp["w_in"]), "w_uq": f(inp["mla_w_uq"]), "w_ukv": f(inp["mla_w_ukv"]),
        "w_gk": f(inp["gla_w_gk"]), "w_out": f(inp["w_out"]),
        "b_adaT": np.stack([_col(inp["b_ada"][l], 24) for l in range(L)]),
        "convw": np.ascontiguousarray(np.asarray(inp["gdn_conv_w"], np.float32).reshape(L, 5, 6, 128).transpose(0, 3, 2, 1)),
        "alog": np.ascontiguousarray(np.broadcast_to(np.asarray(inp["gdn_a_log"], np.float32).reshape(L, 1, 8), (L, 128, 8))),
        "dtb": np.ascontiguousarray(np.broadcast_to(np.asarray(inp["gdn_dt_bias"], np.float32).reshape(L, 1, 8), (L, 128, 8))),
        "gnw": np.ascontiguousarray(np.broadcast_to(np.tile(np.asarray(inp["gdn_norm_w"], np.float32), (1, 4))[:, None, :], (L, 128, 256))),
        "lnw": np.ascontiguousarray(np.broadcast_to(np.tile(np.asarray(inp["gla_norm_w"], np.float32), (1, 4))[:, None, :], (L, 128, 256))),
        "qnw": np.stack([_col(inp["mla_q_norm_w"][l], 3) for l in range(L)]),
        "kvnw": np.stack([_col(inp["mla_kv_norm_w"][l], 2) for l in range(L)]),
        "b_gkT": np.stack([np.ascontiguousarray(np.asarray(inp["gla_b_gk"][l], np.float32).T) for l in range(L)]),
        "fnw": np.ascontiguousarray(np.broadcast_to(np.asarray(inp["final_norm_w"], np.float32)[None, :], (128, D))),
        "consts": CONST_ARR, "ropeQ": rq, "ropeK": rk,
    }
    maps = []
    x = np.asarray(inp["x"], np.float32); ctx = np.asarray(inp["ctx"], np.float32)
    c = np.asarray(inp["c"], np.float32); c_ctx = np.asarray(inp["c_ctx"], np.float32)
    for i in range(n_cores):
        bs = slice(i * NB, (i + 1) * NB)
        vecs = np.concatenate([c[bs], c_ctx[None, :]], axis=0)
        cT = np.ascontiguousarray(vecs.reshape(NB + 1, 8, 128).transpose(2, 1, 0))
        m = dict(shared)
        m["x"] = np.ascontiguousarray(x[bs]); m["ctx"] = np.ascontiguousarray(ctx[bs]); m["cT"] = cT
        maps.append(m)
    return maps


_NC_CACHE = {}


def kernel(**inputs):
    B, TL, _ = inputs["x"].shape
    L = inputs["w_in"].shape[0]
    n_cores = 8
    NB = B // n_cores
    key = (TL, NB, L)
    if key not in _NC_CACHE:
        _NC_CACHE[key] = build(Cfg(TL=TL, NB=NB, L=L))
    nc = _NC_CACHE[key]
    maps = make_in_maps(inputs, n_cores, NB, TL, L)
    res = run_bass_kernel_spmd(nc, maps, core_ids=list(range(n_cores)))
    return np.concatenate([np.asarray(r["out"], np.float32) for r in res.results], axis=0)
```

```python
import contextlib
import numpy as np
import concourse.bass as bass
import concourse.mybir as mybir
from concourse.bass_utils import run_bass_kernel_spmd

F32 = mybir.dt.float32
AF = mybir.ActivationFunctionType
ALU = mybir.AluOpType
AX = mybir.AxisListType

D = 1024
TC = 256
N_IN = 3024
O_QKV, O_ZG, O_A, O_B, O_CQ, O_CKV, O_KR, O_ZM, O_GQ, O_GK, O_GV, O_GZ, O_GL = (
    0, 768, 1024, 1032, 1040, 1424, 1680, 1712, 2224, 2352, 2480, 2736, 2992)
EPS = 1e-6
MLA_SCALE = 96 ** -0.5
NEG = -30000.0

PE, ACT, DVE, POOL, SP = "pe", "act", "dve", "pool", "sp"
COMPUTE = (PE, ACT, DVE, POOL)
NDSEM = 12


class Buf:
    def __init__(self, ap, key):
        self.ap = ap
        self.key = key

    def __getitem__(self, idx):
        return self.ap[idx]


def _key(x):
    if isinstance(x, Buf):
        return x.key
    if isinstance(x, tuple) and len(x) == 2 and isinstance(x[0], Buf):
        return (x[0].key, x[1])
    return x


class Sched:
    def __init__(self, nc):
        self.nc = nc
        self.ops = {e: [] for e in (PE, ACT, DVE, POOL, SP)}
        self.cnt = {e: 0 for e in COMPUTE}
        self.dcnt = {}
        self.dnext = {SP: 0, POOL: 0, ACT: 0}
        self.seen = {e: {} for e in (PE, ACT, DVE, POOL, SP)}
        self.lastw = {}
        self.readers = {}
        self.floor = {}
        self.sems = {}

    def _need(self, eng, ev, waits, same_ok=False):
        if ev is None:
            return
        k, v = ev
        if same_ok and k == ("c", eng):
            return
        if self.seen[eng].get(k, 0) >= v:
            return
        self.seen[eng][k] = v
        waits[k] = max(waits.get(k, 0), v)

    def barrier(self):
        fl = {("c", e): v for e, v in self.cnt.items() if v}
        fl.update(self.dcnt)
        self.floor = fl

    def op(self, eng, fn, reads=(), writes=(), dma=False):
        reads = [_key(r) for r in reads]
        writes = [_key(w) for w in writes]
        waits = {}
        for k, v in self.floor.items():
            self._need(eng, (k, v), waits)
        for r in reads:
            self._need(eng, self.lastw.get(r), waits)
        for w in writes:
            self._need(eng, self.lastw.get(w), waits, same_ok=not dma)
            for ev in self.readers.get(w, ()):
                self._need(eng, ev, waits, same_ok=not dma)
        if dma:
            slot = self.dnext[eng] % NDSEM
            self.dnext[eng] += 1
            k = ("d", eng, slot)
            if self.dcnt.get(k, 0):
                self._need(eng, (k, self.dcnt[k]), waits)
            self.dcnt[k] = self.dcnt.get(k, 0) + 16
            ev = (k, self.dcnt[k])
            inc = (k, 16)
        else:
            self.cnt[eng] += 1
            k = ("c", eng)
            ev = (k, self.cnt[eng])
            inc = (k, 1)
        for w in writes:
            self.lastw[w] = ev
            self.readers[w] = []
        for r in reads:
            if r not in writes:
                self.readers.setdefault(r, []).append(ev)
        self.ops[eng].append((sorted(waits.items(), key=str), fn, inc))
        return ev

    def emit(self, final_events=()):
        nc = self.nc
        keys = set()
        for lst in self.ops.values():
            for _, _, inc in lst:
                keys.add(inc[0])
        with contextlib.ExitStack() as st:
            for k in sorted(keys, key=str):
                self.sems[k] = st.enter_context(nc.semaphore("s_" + "_".join(str(x) for x in k)))
            block = st.enter_context(nc.Block())

            def run(name):
                def body(eng):
                    for waits, fn, inc in self.ops[name]:
                        for k, v in waits:
                            eng.wait_ge(self.sems[k], v)
                        fn(eng).then_inc(self.sems[inc[0]], inc[1])
                    if name == SP:
                        for k, v in final_events:
                            eng.wait_ge(self.sems[k], v)
                return body

            block.sync(run(SP))
            block.tensor(run(PE))
            block.scalar(run(ACT))
            block.vector(run(DVE))
            block.gpsimd(run(POOL))


class Arena:
    def __init__(self, handle, size):
        self.h = handle
        self.size = size
        self.off = 0
        self.gen = 0

    def reset(self):
        self.off = 0
        self.gen += 1

    def alloc(self, name, fshape, parts=128):
        n = int(np.prod(fshape))
        n_al = (n + 7) // 8 * 8
        assert self.off + n_al <= self.size, f"arena overflow at {name}: {self.off}+{n_al}>{self.size}"
        ap = self.h[0:parts, self.off:self.off + n]
        self.off += n_al
        if len(fshape) == 2:
            ap = ap.rearrange("p (a b) -> p a b", a=fshape[0])
        elif len(fshape) == 3:
            ap = ap.rearrange("p (a b c) -> p a b c", a=fshape[0], b=fshape[1])
        return Buf(ap, f"{name}@{self.gen}")


def bcast(ap, axis, n):
    a = ap.unsqueeze(axis)
    shp = list(a.shape)
    shp[axis] = n
    return a.broadcast_to(shp)


def _const_layout():
    p = np.arange(128)[:, None]
    f = np.arange(128)[None, :]
    items = []

    def add(name, arr):
        a = np.zeros((128, arr.shape[1]), np.float32)
        a[:arr.shape[0]] = arr
        items.append((name, a))

    add("ident", (p == f).astype(np.float32))
    add("ones", np.ones((128, 128), np.float32))
    add("trif", (p <= f).astype(np.float32))
    add("trib", (p >= f).astype(np.float32))
    add("bd", ((p // 64) == (f // 64)).astype(np.float32))

    def m(valid):
        return np.tile(np.where(valid, 0.0, NEG).astype(np.float32), (1, 4))

    same = (p // 64) == (f // 64)
    add("mL0", m((f < p) & same)); add("mS0", m((p < f) & same)); add("mI0", m(p <= f)); add("mO0", m((p < f) & ~same))
    add("mL1", m((f > p) & same)); add("mS1", m((p > f) & same)); add("mI1", m(p >= f)); add("mO1", m((p > f) & ~same))
    add("m010", np.tile((p <= f).astype(np.float32), (1, 4)))
    add("m011", np.tile((p >= f).astype(np.float32), (1, 4)))
    add("hm", (p // 32 == np.arange(4)[None, :]).astype(np.float32))
    selb = np.zeros((8, 2, 4, 64), np.float32)
    for d in range(2):
        for h in range(4):
            selb[d * 4 + h, d, h, :] = 1.0
    add("selb", selb.reshape(8, 512))
    esel = np.zeros((32, 96), np.float32)
    esel[np.arange(32), 64 + np.arange(32)] = 1.0
    add("esel", esel)
    selrow = np.zeros((65, 64), np.float32)
    selrow[64, :] = 1.0
    add("selrow", selrow)
    add("reset", np.tile((f % 128 != 0).astype(np.float32), (128, 4)))
    add("bvals", np.tile(np.array([[EPS, 1.0, 64 * EPS, 0.0]], np.float32), (128, 1)))
    off = {}
    o = 0
    for name, a in items:
        off[name] = (o, a.shape[1])
        o += a.shape[1]
    return off, np.concatenate([a for _, a in items], axis=1)


CONST_OFF, CONST_ARR = _const_layout()
NCONST = CONST_ARR.shape[1]


def _rope_tables(TL):
    T = TC + TL
    rows = TL // 64
    row = np.broadcast_to(np.arange(rows, dtype=np.float32)[:, None], (rows, 64)).reshape(-1)
    col = np.broadcast_to(np.arange(64, dtype=np.float32)[None, :], (rows, 64)).reshape(-1)
    inv = (np.float32(10000.0) ** (-np.arange(8, dtype=np.float32) / np.float32(8))).astype(np.float32)
    ang = np.concatenate([row[:, None] * inv, col[:, None] * inv], axis=-1).astype(np.float32)
    cos = np.cos(ang).astype(np.float32).T
    sin = np.sin(ang).astype(np.float32).T
    rq = np.zeros((96, 2, T), np.float32)
    rq[:, 0, :] = 1.0
    rq[64:80, 0, TC:] = cos
    rq[80:96, 0, TC:] = cos
    rq[64:80, 1, TC:] = -sin
    rq[80:96, 1, TC:] = sin
    rk = np.ascontiguousarray(rq[64:96])
    return rq, rk


class Cfg:
    def __init__(self, TL=4096, NB=2, L=2, debug=False, phases="0ABCDE"):
        self.TL, self.NB, self.L, self.debug, self.phases = TL, NB, L, debug, phases


ARENA_F = 40448


def build(cfg):
    nc = bass.Bass("TRN2", target_bir_lowering=False)
    TL, NB, L = cfg.TL, cfg.NB, cfg.L
    T = TC + TL
    NCH = T // 128
    NCC = TC // 128
    groups = [(0, TC)] + [(TC + i * 512, 512) for i in range(TL // 512)]

    def din(name, shape):
        return nc.dram_tensor(name, list(shape), F32, kind="ExternalInput").ap()

    scr_kind = "ExternalOutput" if cfg.debug else "Internal"

    def dscr(name, shape):
        return Buf(nc.dram_tensor(name, list(shape), F32, kind=scr_kind).ap(), name)

    x_in = din("x", [NB, TL, D]); ctx_in = din("ctx", [NB, TC, D]); cT = din("cT", [128, 8, NB + 1])
    w_ada = din("w_ada", [L, D, 3 * D]); b_adaT = din("b_adaT", [L, 128, 24])
    w_in = din("w_in", [L, D, N_IN]); convw_in = din("convw", [L, 128, 6, 5])
    alog_in = din("alog", [L, 128, 8]); dtb_in = din("dtb", [L, 128, 8])
    gnw_in = din("gnw", [L, 128, 256]); qnw_in = din("qnw", [L, 128, 3]); w_uq = din("w_uq", [L, 384, 768])
    kvnw_in = din("kvnw", [L, 128, 2]); w_ukv = din("w_ukv", [L, 256, 1024])
    w_gk = din("w_gk", [L, 2, 16, 128]); bgk_in = din("b_gkT", [L, 128, 2]); lnw_in = din("lnw", [L, 128, 256])
    w_out = din("w_out", [L, D, D]); fnw_in = din("fnw", [128, D])
    consts_in = din("consts", [128, NCONST]); ropeQ_in = din("ropeQ", [96, 2, T]); ropeK_in = din("ropeK", [32, 2, T])
    out = Buf(nc.dram_tensor("out", [NB, TL, D], F32, kind="ExternalOutput").ap(), "out")

    xs = [dscr(f"xs{b}", [T, D]) for b in range(NB)]
    qkvT = dscr("qkvT", [768, T]); cqT = dscr("cqT", [384, T]); ckvT = dscr("ckvT", [256, T])
    krT = dscr("krT", [64, T]); zmT = dscr("zmT", [512, T]); gqT = dscr("gqT", [128, T]); gkT = dscr("gkT", [128, T])
    glT = dscr("glT", [32, T]); bT = dscr("bT", [8, T]); PT = dscr("PT", [T, 784]); yT = dscr("yT", [1024, T])
    o_d = [dscr(f"o_d{d}", [T, 256]) for d in range(2)]
    gqk = dscr("gqk", [2, 256, T]); ktm_d = dscr("ktm_d", [T, 256]); vtm_d = dscr("vtm_d", [T, 256])
    QT = dscr("QT", [8, 96, T]); KT = dscr("KT", [8, 96, T]); Vd = dscr("Vd", [T, 512])

    with contextlib.ExitStack() as st:
        def sb(name, shape):
            return Buf(st.enter_context(nc.sbuf_tensor(name, list(shape), F32)), name)

        cst = sb("cst", [128, NCONST])
        arena_h = st.enter_context(nc.sbuf_tensor("arena", [128, ARENA_F], F32))
        A = Arena(arena_h, ARENA_F)
        cs_t = sb("cs_t", [128, 8, NB + 1])
        modc = sb("modc", [128, 24, NB + 1])
        gate_bc = [sb(f"gate_bc{i}", [128, D]) for i in range(NB + 1)]
        badaT = sb("badaT", [128, 24])
        small = sb("small", [128, 64])
        P = [Buf(st.enter_context(nc.psum_tensor(f"P{i}", [128, 512], F32)), f"P{i}") for i in range(8)]
        S = Sched(nc)
        tog = [0]

        def C(name, rows=128):
            o, w = CONST_OFF[name]
            return cst[0:rows, o:o + w]

        def dma_in(out_ap, in_ap, rd=(), wr=(), q=SP):
            return S.op(q, lambda e: e.dma_start(out=out_ap, in_=in_ap), reads=rd, writes=wr, dma=True)

        def dma_out(out_ap, in_ap, rd=(), wr=()):
            return S.op(POOL, lambda e: e.dma_start(out=out_ap, in_=in_ap), reads=rd, writes=wr, dma=True)

        def mm(out_ap, lhsT, rhs, start=True, stop=True, rd=(), wr=()):
            S.op(PE, lambda e: e.matmul(out_ap, lhsT=lhsT, rhs=rhs, start=start, stop=stop), reads=rd, writes=wr)

        def tr(out_ap, in_ap, rd=(), wr=()):
            n = in_ap.shape[0]
            idn = C("ident")[0:n, 0:n]
            S.op(PE, lambda e: e.transpose(out_ap, in_ap, idn), reads=list(rd) + [cst], writes=wr)

        def act(out_ap, in_ap, func, bias=None, scale=None, accum=None, rd=(), wr=()):
            kw = {}
            if isinstance(bias, float):
                col = {EPS: 0, 1.0: 1, 64 * EPS: 2}[bias]
                bias = C("bvals")[0:in_ap.shape[0], col:col + 1]
                rd = list(rd) + [cst]
            if bias is not None:
                kw["bias"] = bias
            if scale is not None:
                kw["scale"] = scale
            if accum is not None:
                kw["accum_out"] = accum
            S.op(ACT, lambda e: e.activation(out=out_ap, in_=in_ap, func=func, **kw), reads=rd, writes=wr)

        def tt(eng, out_ap, in0, in1, op, rd=(), wr=()):
            S.op(eng, lambda e: e.tensor_tensor(out=out_ap, in0=in0, in1=in1, op=op), reads=rd, writes=wr)

        def ts(eng, out_ap, in0, s1, s2, op0, op1=None, rd=(), wr=()):
            if op1 is None:
                S.op(eng, lambda e: e.tensor_scalar(out=out_ap, in0=in0, scalar1=s1, scalar2=None, op0=op0),
                     reads=rd, writes=wr)
            else:
                S.op(eng, lambda e: e.tensor_scalar(out=out_ap, in0=in0, scalar1=s1, scalar2=s2, op0=op0, op1=op1),
                     reads=rd, writes=wr)

        def stt(out_ap, in0, scalar, in1, op0, op1, rd=(), wr=()):
            S.op(DVE, lambda e: e.scalar_tensor_tensor(out=out_ap, in0=in0, scalar=scalar, in1=in1, op0=op0, op1=op1),
                 reads=rd, writes=wr)

        def recip(out_ap, in_ap, rd=(), wr=()):
            S.op(DVE, lambda e: e.reciprocal(out=out_ap, in_=in_ap), reads=rd, writes=wr)

        def copy(out_ap, in_ap, rd=(), wr=(), eng=None):
            if eng is None:
                tog[0] ^= 1
                eng = ACT if tog[0] else DVE
            if eng == ACT:
                S.op(ACT, lambda e: e.copy(out=out_ap, in_=in_ap), reads=rd, writes=wr)
            else:
                S.op(eng, lambda e: e.tensor_copy(out=out_ap, in_=in_ap), reads=rd, writes=wr)

        def memset(eng, ap, val, wr=()):
            S.op(eng, lambda e: e.memset(ap, val), writes=wr)

        dma_in(cst[:, :], consts_in[:, :], wr=[cst])
        dma_in(cs_t[:, :, :], cT[:, :, :], wr=[cs_t])
        act(cs_t[:, :, :], cs_t[:, :, :], AF.Silu, rd=[cs_t], wr=[cs_t])

        def phase0(l):
            S.barrier(); A.reset()
            wada = A.alloc("wada", [8, 3 * D])
            dg = [A.alloc(f"dg{i}", [128]) for i in range(2)]
            for k in range(8):
                dma_in(wada[:, k, :], w_ada[l, k * 128:(k + 1) * 128, :], wr=[wada])
            dma_in(badaT[:, :], b_adaT[l, :, :], wr=[badaT])
            nv = NB + 1
            for blk in range(24):
                for k in range(8):
                    mm(P[0][:, blk * nv:(blk + 1) * nv], wada[:, k, blk * 128:(blk + 1) * 128], cs_t[:, k, :],
                       start=(k == 0), stop=(k == 7), rd=[wada, cs_t], wr=[P[0]])
            tt(DVE, modc[:, :, :], P[0][:, 0:24 * nv].rearrange("p (a b) -> p a b", b=nv),
               bcast(badaT[:, :], 2, nv), ALU.add, rd=[P[0], badaT], wr=[modc])
            ts(DVE, modc[:, 8:16, :], modc[:, 8:16, :], 1.0, None, ALU.add, rd=[modc], wr=[modc])
            for i in range(nv):
                for k in range(8):
                    d_ = dg[k % 2]
                    ts(DVE, d_[:, :], C("ident"), modc[:, 16 + k, i:i + 1], None, ALU.mult, rd=[modc, cst], wr=[d_])
                    pb = P[1 + k // 4]
                    mm(pb[:, (k % 4) * 128:(k % 4 + 1) * 128], C("ones"), d_[:, :], rd=[d_, cst], wr=[pb])
                copy(gate_bc[i][:, 0:512], P[1][:, :], rd=[P[1]], wr=[gate_bc[i]])
                copy(gate_bc[i][:, 512:1024], P[2][:, :], rd=[P[2]], wr=[gate_bc[i]])

        def x_src(l, b, tok):
            if l == 0:
                if tok < TC:
                    return ctx_in[b, tok:tok + 128, :], []
                return x_in[b, tok - TC:tok - TC + 128, :], []
            return xs[b][tok:tok + 128, :], [xs[b]]

        def phaseA(l, b):
            S.barrier(); A.reset()
            win = A.alloc("win", [8, N_IN]); wsw = A.alloc("wsw", [8, 32])
            hT = [A.alloc(f"hT{i}", [8, 512]) for i in range(2)]
            xt = [A.alloc(f"xt{i}", [D]) for i in range(2)]
            junk = A.alloc("junk", [D])
            stg = [A.alloc(f"stg{i}", [512]) for i in range(3)]
            stt_ = [A.alloc(f"stt{i}", [784]) for i in range(2)]
            ss = [A.alloc(f"ss{i}", [2]) for i in range(2)]
            for k in range(8):
                rows = slice(k * 128, (k + 1) * 128)
                dma_in(win[:, k, :], w_in[l, rows, :], wr=[win])
                dma_in(wsw[:, k, 0:16], w_in[l, rows, O_KR + 16:O_KR + 32], wr=[wsw])
                dma_in(wsw[:, k, 16:32], w_in[l, rows, O_KR:O_KR + 16], wr=[wsw])
            blocks = []
            for i in range(6):
                blocks.append((qkvT, i * 128, 128, win, O_QKV + i * 128))
            blocks.append((bT, 0, 8, win, O_B))
            for i in range(3):
                blocks.append((cqT, i * 128, 128, win, O_CQ + i * 128))
            for i in range(2):
                blocks.append((ckvT, i * 128, 128, win, O_CKV + i * 128))
            blocks.append((krT, 0, 32, win, O_KR))
            blocks.append((krT, 32, 32, wsw, 0))
            for i in range(4):
                blocks.append((zmT, i * 128, 128, win, O_ZM + i * 128))
            blocks.append((gqT, 0, 128, win, O_GQ))
            blocks.append((gkT, 0, 128, win, O_GK))
            blocks.append((glT, 0, 32, win, O_GL))
            ti = 0
            bi = 0
            for gi, (t0, n) in enumerate(groups):
                h = hT[gi % 2]
                col = NB if t0 < TC else b
                for i in range(n // 128):
                    tok = t0 + i * 128
                    xb = xt[ti % 2]; sb_ = ss[ti % 2]
                    src, srd = x_src(l, b, tok)
                    dma_in(xb[:, :], src, rd=srd, wr=[xb])
                    act(junk[:, :], xb[:, :], AF.Square, accum=sb_[:, 0:1], rd=[xb], wr=[junk, sb_])
                    act(sb_[:, 0:1], sb_[:, 0:1], AF.Sqrt, bias=EPS, scale=1.0 / D, rd=[sb_], wr=[sb_])
                    recip(sb_[:, 1:2], sb_[:, 0:1], rd=[sb_], wr=[sb_])
                    ts(DVE, xb[:, :], xb[:, :], sb_[:, 1:2], None, ALU.mult, rd=[xb, sb_], wr=[xb])
                    pa, pb = P[(ti % 2) * 2], P[(ti % 2) * 2 + 1]
                    for k in range(8):
                        pp = pa if k < 4 else pb
                        tr(pp[:, (k % 4) * 128:(k % 4 + 1) * 128], xb[:, k * 128:(k + 1) * 128], rd=[xb], wr=[pp])
                    for k in range(8):
                        pp = pa if k < 4 else pb
                        src_ps = pp[:, (k % 4) * 128:(k % 4 + 1) * 128]
                        dst = h[:, k, i * 128:(i + 1) * 128]
                        if k % 2 == 0:
                            act(dst, src_ps, AF.Identity, bias=modc[:, k, col:col + 1], scale=modc[:, 8 + k, col:col + 1],
                                rd=[pp, modc], wr=[h])
                        else:
                            ts(DVE, dst, src_ps, modc[:, 8 + k, col:col + 1], modc[:, k, col:col + 1], ALU.mult, ALU.add,
                               rd=[pp, modc], wr=[h])
                    ti += 1
                for (dst, r0, m, wt, c0) in blocks:
                    pp = P[4 + bi % 4]; sg = stg[bi % 3]
                    for k in range(8):
                        mm(pp[0:m, 0:n], wt[:, k, c0:c0 + m], h[:, k, 0:n], start=(k == 0), stop=(k == 7),
                           rd=[wt, h], wr=[pp])
                    copy(sg[0:m, 0:n], pp[0:m, 0:n], rd=[pp], wr=[sg])
                    dma_out(dst[r0:r0 + m, t0:t0 + n], sg[0:m, 0:n], rd=[sg], wr=[dst])
                    bi += 1
                for i in range(n // 128):
                    tok = t0 + i * 128
                    so = stt_[i % 2]
                    for (c0, w, o0) in ((O_ZG, 272, 0), (O_GV, 512, 272)):
                        pp = P[4 + bi % 4]
                        for k in range(8):
                            mm(pp[:, 0:w], h[:, k, i * 128:(i + 1) * 128], win[:, k, c0:c0 + w], start=(k == 0),
                               stop=(k == 7), rd=[win, h], wr=[pp])
                        copy(so[:, o0:o0 + w], pp[:, 0:w], rd=[pp], wr=[so])
                        bi += 1
                    dma_out(PT[tok:tok + 128, :], so[:, :], rd=[so], wr=[PT])

        def gating(l, nw_in, zcol, yrow):
            S.barrier(); A.reset()
            nw = A.alloc("nw", [256])
            dma_in(nw[:, :], nw_in[l, :, :], wr=[nw])
            o0 = [A.alloc(f"o0_{i}", [4, 256]) for i in range(2)]
            o1 = [A.alloc(f"o1_{i}", [4, 256]) for i in range(2)]
            zz = [A.alloc(f"zz{i}", [4, 256]) for i in range(2)]
            sq = A.alloc("sq", [4, 256])
            rs = [A.alloc(f"rs{i}", [32]) for i in range(2)]
            yst = [A.alloc(f"yst{i}", [2, 512]) for i in range(2)]
            gi = 0
            for (t0, n) in groups:
                if l == L - 1 and t0 < TC:
                    continue
                nc_ = n // 128
                a0, a1, z_, r_, ys = o0[gi % 2], o1[gi % 2], zz[gi % 2], rs[gi % 2], yst[gi % 2]
                dma_in(a0[:, 0:nc_, :], o_d[0][t0:t0 + n, :].rearrange("(c p) f -> p c f", p=128), rd=[o_d[0]], wr=[a0])
                dma_in(a1[:, 0:nc_, :], o_d[1][t0:t0 + n, :].rearrange("(c p) f -> p c f", p=128), rd=[o_d[1]], wr=[a1])
                dma_in(z_[:, 0:nc_, :], PT[t0:t0 + n, zcol:zcol + 256].rearrange("(c p) f -> p c f", p=128),
                       rd=[PT], wr=[z_])
                tt(DVE, a0[:, 0:nc_, :], a0[:, 0:nc_, :], a1[:, 0:nc_, :], ALU.add, rd=[a0, a1], wr=[a0])
                tt(POOL, sq[:, 0:nc_, :], a0[:, 0:nc_, :], a0[:, 0:nc_, :], ALU.mult, rd=[a0], wr=[sq])
                S.op(DVE, lambda e, o_=r_[:, 0:nc_ * 4], i_=sq[:, 0:nc_, :].rearrange("p c (h e) -> p (c h) e", h=4):
                     e.tensor_reduce(out=o_, in_=i_, axis=AX.X, op=ALU.add), reads=[sq], writes=[r_])
                act(r_[:, 0:nc_ * 4], r_[:, 0:nc_ * 4], AF.Sqrt, bias=EPS, scale=1.0 / 64, rd=[r_], wr=[r_])
                recip(r_[:, 16:16 + nc_ * 4], r_[:, 0:nc_ * 4], rd=[r_], wr=[r_])
                tt(DVE, a0[:, 0:nc_, :].rearrange("p c (h e) -> p (c h) e", h=4),
                   a0[:, 0:nc_, :].rearrange("p c (h e) -> p (c h) e", h=4),
                   bcast(r_[:, 16:16 + nc_ * 4], 2, 64), ALU.mult, rd=[a0, r_], wr=[a0])
                tt(POOL, a0[:, 0:nc_, :], a0[:, 0:nc_, :], bcast(nw[:, :], 1, nc_), ALU.mult, rd=[a0, nw], wr=[a0])
                act(z_[:, 0:nc_, :], z_[:, 0:nc_, :], AF.Silu, rd=[z_], wr=[z_])
                tt(DVE, a0[:, 0:nc_, :], a0[:, 0:nc_, :], z_[:, 0:nc_, :], ALU.mult, rd=[a0, z_], wr=[a0])
                for jj in range(2):
                    pp = P[(gi * 2 + jj) % 8]
                    for c in range(nc_):
                        tr(pp[:, c * 128:(c + 1) * 128], a0[:, c, jj * 128:(jj + 1) * 128], rd=[a0], wr=[pp])
                    copy(ys[:, jj, 0:n], pp[:, 0:n], rd=[pp], wr=[ys])
                dma_out(yT[yrow:yrow + 256, t0:t0 + n].rearrange("(j p) t -> p j t", p=128), ys[:, :, 0:n],
                        rd=[ys], wr=[yT])
                gi += 1

        def chunk_orders():
            fwd = list(range(NCH))
            bwd = list(range(NCC - 1, -1, -1)) + list(range(NCH - 1, NCC - 1, -1))
            return [fwd, bwd]

        def phaseB(l, b):
            S.barrier(); A.reset()
            cw = A.alloc("cw", [6, 5])
            dma_in(cw[:, :, :], convw_in[l, :, :, :], wr=[cw])
            xpad = [A.alloc(f"xpad{i}", [TL + 4]) for i in range(2)]
            acc = [A.alloc(f"acc{i}", [TL]) for i in range(2)]
            rn = [A.alloc(f"rn{i}", [512]) for i in range(2)]
            tst = [A.alloc(f"tst{i}", [4, 128]) for i in range(2)]
            it = 0
            ci = 0
            for blk in range(6):
                for (s0, n) in ((0, TC), (TC, TL)):
                    xp, ac = xpad[it % 2], acc[it % 2]
                    memset(POOL, xp[:, 0:2], 0.0, wr=[xp])
                    memset(POOL, xp[:, n + 2:n + 4], 0.0, wr=[xp])
                    dma_in(xp[:, 2:n + 2], qkvT[blk * 128:(blk + 1) * 128, s0:s0 + n], rd=[qkvT], wr=[xp])
                    ts(DVE, ac[:, 0:n], xp[:, 0:n], cw[:, blk, 0:1], None, ALU.mult, rd=[xp, cw], wr=[ac])
                    for j in range(1, 5):
                        stt(ac[:, 0:n], xp[:, j:j + n], cw[:, blk, j:j + 1], ac[:, 0:n], ALU.mult, ALU.add,
                            rd=[xp, cw, ac], wr=[ac])
                    act(ac[:, 0:n], ac[:, 0:n], AF.Silu, rd=[ac], wr=[ac])
                    if blk < 4:
                        sc = 64.0 if blk < 2 else 1.0
                        tt(POOL, xp[:, 0:n], ac[:, 0:n], ac[:, 0:n], ALU.mult, rd=[ac], wr=[xp])
                        for g0 in range(0, n, 512):
                            gn = min(512, n - g0)
                            pp = P[ci % 8]; r_ = rn[ci % 2]; ci += 1
                            mm(pp[:, 0:gn], C("bd"), xp[:, g0:g0 + gn], rd=[xp, cst], wr=[pp])
                            act(r_[:, 0:gn], pp[:, 0:gn], AF.Sqrt, bias=sc * EPS, scale=sc, rd=[pp], wr=[r_])
                            recip(r_[:, 0:gn], r_[:, 0:gn], rd=[r_], wr=[r_])
                            tt(DVE, ac[:, g0:g0 + gn], ac[:, g0:g0 + gn], r_[:, 0:gn], ALU.mult, rd=[ac, r_], wr=[ac])
                        dma_out(gqk[blk // 2, (blk % 2) * 128:(blk % 2 + 1) * 128, s0:s0 + n], ac[:, 0:n], rd=[ac], wr=[gqk])
                    if blk >= 2:
                        dst = ktm_d if blk < 4 else vtm_d
                        jj = blk % 2
                        for g0 in range(0, n, 512):
                            gn = min(512, n - g0)
                            pp = P[ci % 8]; ts_ = tst[ci % 2]; ci += 1
                            for c in range(gn // 128):
                                tr(pp[:, c * 128:(c + 1) * 128], ac[:, g0 + c * 128:g0 + (c + 1) * 128], rd=[ac], wr=[pp])
                            copy(ts_[:, 0:gn // 128, :], pp[:, 0:gn].rearrange("p (c f) -> p c f", f=128), rd=[pp], wr=[ts_])
                            dma_out(dst[s0 + g0:s0 + g0 + gn, jj * 128:(jj + 1) * 128].rearrange("(c p) f -> p c f", p=128),
                                    ts_[:, 0:gn // 128, :], rd=[ts_], wr=[dst])
                    it += 1
            S.barrier(); A.reset()
            ab = A.alloc("ab", [NCH, 16])
            dma_in(ab[:, :, :], PT[:, 256:272].rearrange("(c p) f -> p c f", p=128), rd=[PT], wr=[ab])
            prm = A.alloc("prm", [16])
            dma_in(prm[:, 0:8], alog_in[l, :, :], wr=[prm])
            dma_in(prm[:, 8:16], dtb_in[l, :, :], wr=[prm])
            act(prm[:, 0:8], prm[:, 0:8], AF.Exp, rd=[prm], wr=[prm])
            xg = A.alloc("xg", [NCH, 8]); t1 = A.alloc("t1", [NCH, 8]); t2 = A.alloc("t2", [NCH, 8])
            g_ = A.alloc("g", [2, NCH, 4]); beta = A.alloc("beta", [2, NCH, 4])
            gc = A.alloc("gc", [2, NCH, 4]); gtot = A.alloc("gtot", [2, NCH, 4])
            eg = A.alloc("eg", [2, NCH, 4]); edec = A.alloc("edec", [2, NCH, 4]); egtot = A.alloc("egtot", [2, NCH, 4])
            negegb = A.alloc("negegb", [2, NCH, 4])
            tt(DVE, xg[:, :, :], ab[:, :, 0:8], bcast(prm[:, 8:16], 1, NCH), ALU.add, rd=[ab, prm], wr=[xg])
            act(t1[:, :, :], xg[:, :, :], AF.Abs, rd=[xg], wr=[t1])
            act(t1[:, :, :], t1[:, :, :], AF.Exp, scale=-1.0, rd=[t1], wr=[t1])
            act(t1[:, :, :], t1[:, :, :], AF.Ln, bias=1.0, rd=[t1], wr=[t1])
            ts(DVE, t2[:, :, :], xg[:, :, :], 0.0, None, ALU.max, rd=[xg], wr=[t2])
            tt(DVE, t1[:, :, :], t1[:, :, :], t2[:, :, :], ALU.add, rd=[t1, t2], wr=[t1])
            stt(t1[:, :, :], t1[:, :, :], -1.0, bcast(prm[:, 0:8], 1, NCH), ALU.mult, ALU.mult, rd=[t1, prm], wr=[t1])
            act(ab[:, :, 8:16], ab[:, :, 8:16], AF.Sigmoid, rd=[ab], wr=[ab])
            for d in range(2):
                copy(g_[:, d, :, :], t1[:, :, d * 4:(d + 1) * 4], rd=[t1], wr=[g_], eng=DVE)
                copy(beta[:, d, :, :], ab[:, :, 8 + d * 4:8 + (d + 1) * 4], rd=[ab], wr=[beta], eng=DVE)
            for d in range(2):
                mm(P[0][:, d * NCH * 4:(d + 1) * NCH * 4], C("trif" if d == 0 else "trib"),
                   g_[:, d, :, :].rearrange("p c h -> p (c h)"), rd=[g_, cst], wr=[P[0]])
            mm(P[1][:, 0:2 * NCH * 4], C("ones"), g_[:, :, :, :].rearrange("p d c h -> p (d c h)"), rd=[g_, cst], wr=[P[1]])
            fl = "p d c h -> p (d c h)"
            copy(gc[:, :, :, :].rearrange(fl), P[0][:, 0:2 * NCH * 4], rd=[P[0]], wr=[gc], eng=DVE)
            copy(gtot[:, :, :, :].rearrange(fl), P[1][:, 0:2 * NCH * 4], rd=[P[1]], wr=[gtot], eng=DVE)
            act(eg[:, :, :, :].rearrange(fl), gc[:, :, :, :].rearrange(fl), AF.Exp, rd=[gc], wr=[eg])
            act(egtot[:, :, :, :].rearrange(fl), gtot[:, :, :, :].rearrange(fl), AF.Exp, rd=[gtot], wr=[egtot])
            tt(DVE, edec[:, :, :, :].rearrange(fl), gtot[:, :, :, :].rearrange(fl), gc[:, :, :, :].rearrange(fl),
               ALU.subtract, rd=[gtot, gc], wr=[edec])
            act(edec[:, :, :, :].rearrange(fl), edec[:, :, :, :].rearrange(fl), AF.Exp, rd=[edec], wr=[edec])
            stt(negegb[:, :, :, :].rearrange(fl), eg[:, :, :, :].rearrange(fl), -1.0, beta[:, :, :, :].rearrange(fl),
                ALU.mult, ALU.mult, rd=[eg, beta], wr=[negegb])
            betaT = A.alloc("betaT", [T], parts=8)
            dma_in(betaT[:, :], bT[:, :], rd=[bT], wr=[betaT])
            act(betaT[:, :], betaT[:, :], AF.Sigmoid, rd=[betaT], wr=[betaT])
            W = {}
            for d in range(2):
                w = {}
                for nm, shp, parts in (("qTc", [4, 128], 64), ("kTc", [4, 128], 64), ("ktmc", [256], 128), ("vtmc", [256], 128)):
                    w[nm] = [A.alloc(f"{nm}{d}{i}", shp, parts=parts) for i in range(2)]
                for nm, shp, parts in (("kbT", [4, 128], 64), ("dgx", [4, 128], 128), ("E", [4, 128], 128), ("V", [4, 512], 128),
                                       ("LMA", [4, 512], 128), ("ysb", [4, 64], 128), ("zsb", [4, 64], 128), ("Pa", [4, 128], 128), ("Pb", [4, 128], 128),
                                       ("La", [4, 128], 128), ("Lb", [4, 128], 128), ("Ma", [4, 128], 128), ("Mb", [4, 128], 128),
                                       ("vb", [4, 64], 128), ("rhs2", [4, 64], 128), ("vnew", [4, 64], 128),
                                       ("tmp", [4, 64], 128), ("kdec", [256], 128), ("Sst", [4, 64], 64), ("tmpS", [4, 64], 64)):
                    w[nm] = A.alloc(f"{nm}{d}", shp, parts=parts)
                w["osb"] = [A.alloc(f"osb{d}{i}", [4, 64]) for i in range(2)]
                W[d] = w
                memset(POOL, w["Sst"][:, :, :], 0.0, wr=[w["Sst"]])
            orders = chunk_orders()
            ident4 = bcast(C("ident"), 1, 4)
            o_sel = CONST_OFF["selb"][0]
            v4 = "p (h s) -> p h s"
            h4 = "p (h e) -> p h e"
            def chunk_body(step, d):
                c = orders[d][step]
                w = W[d]
                B0, B1, B2, B3 = P[d * 4], P[d * 4 + 1], P[d * 4 + 2], P[d * 4 + 3]
                tok = c * 128
                qTc, kTc, ktmc, vtmc = (w[nm][step % 2] for nm in ("qTc", "kTc", "ktmc", "vtmc"))
                dma_in(qTc[:, :, :], gqk[0, :, tok:tok + 128].rearrange("(h p) t -> p h t", p=64), rd=[gqk], wr=[qTc])
                dma_in(kTc[:, :, :], gqk[1, :, tok:tok + 128].rearrange("(h p) t -> p h t", p=64), rd=[gqk], wr=[kTc])
                dma_in(ktmc[:, :], ktm_d[tok:tok + 128, :], rd=[ktm_d], wr=[ktmc])
                dma_in(vtmc[:, :], vtm_d[tok:tok + 128, :], rd=[vtm_d], wr=[vtmc])
                for h in range(4):
                    sel = cst[0:8, o_sel + (d * 4 + h) * 64:o_sel + (d * 4 + h + 1) * 64]
                    mm(B3[0:64, h * 128:(h + 1) * 128], sel, betaT[:, tok:tok + 128], rd=[betaT, cst], wr=[B3])
                kbT = w["kbT"]
                tt(DVE, kbT[:, :, :], kTc[:, :, :], B3[0:64, :].rearrange(v4, h=4), ALU.mult, rd=[kTc, B3], wr=[kbT])
                yield
                for h in range(4):
                    hs = slice(h * 128, (h + 1) * 128)
                    mm(B0[:, hs], kbT[:, h, :], kTc[:, h, :], rd=[kbT, kTc], wr=[B0])
                    mm(B1[:, hs], kTc[:, h, :], kbT[:, h, :], rd=[kbT, kTc], wr=[B1])
                    mm(B2[:, hs], kTc[:, h, :], qTc[:, h, :], rd=[qTc, kTc], wr=[B2])
                gcc = gc[:, d, c, :]
                dgx, E, V, LMA = w["dgx"], w["E"], w["V"], w["LMA"]
                tt(DVE, dgx[:, :, :], ident4, bcast(gcc, 2, 128), ALU.mult, rd=[gc, cst], wr=[dgx])
                yield
                mm(B3[:, :], C("ones"), dgx[:, :, :].rearrange("p h s -> p (h s)"), rd=[dgx, cst], wr=[B3])
                tt(DVE, E[:, :, :], B3[:, :].rearrange(v4, h=4), bcast(gcc, 2, 128), ALU.subtract, rd=[B3, gc], wr=[E])
                Ef = E[:, :, :].rearrange("p h s -> p (h s)")
                stt(V[:, 0, :], Ef, -1.0, C(f"mL{d}"), ALU.mult, ALU.min, rd=[E, cst], wr=[V])
                tt(DVE, V[:, 1, :], Ef, C(f"mS{d}"), ALU.min, rd=[E, cst], wr=[V])
                tt(DVE, V[:, 2, :], Ef, C(f"mI{d}"), ALU.min, rd=[E, cst], wr=[V])
                tt(DVE, V[:, 3, :], Ef, C(f"mO{d}"), ALU.min, rd=[E, cst], wr=[V])
                act(V[:, :, :], V[:, :, :], AF.Exp, rd=[V], wr=[V])
                yield
                tt(DVE, LMA[:, 3, :], B1[:, :], V[:, 3, :], ALU.mult, rd=[B1, V], wr=[LMA])
                tt(DVE, LMA[:, 0, :], B0[:, :], V[:, 0, :], ALU.mult, rd=[B0, V], wr=[LMA])
                tt(DVE, LMA[:, 1, :], B1[:, :], V[:, 1, :], ALU.mult, rd=[B1, V], wr=[LMA])
                tt(DVE, LMA[:, 2, :], B2[:, :], V[:, 2, :], ALU.mult, rd=[B2, V], wr=[LMA])
                Lc = LMA[:, 0, :].rearrange(v4, h=4); Mc = LMA[:, 1, :].rearrange(v4, h=4)
                Lk, Mk = LMA, LMA
                ATm = LMA[:, 2, :].rearrange(v4, h=4)
                Pc, Pk = w["Pa"][:, :, :], w["Pa"]
                tt(DVE, Pc, ident4, Mc, ALU.subtract, rd=[LMA, cst], wr=[Pk])
                yield
                Moff = LMA[:, 3, :].rearrange(v4, h=4)
                for lvl in range(5):
                    last = lvl == 4
                    Ln_b = w["La"] if lvl % 2 == 0 else w["Lb"]
                    Mn_b = w["Ma"] if lvl % 2 == 0 else w["Mb"]
                    Pn_b = w["Pb"] if lvl % 2 == 0 else w["Pa"]
                    for h in range(4):
                        mm(B0[:, h * 128:(h + 1) * 128], Mc[:, h, :], Lc[:, h, :], rd=[Lk, Mk], wr=[B0])
                    if not last:
                        for h in range(4):
                            mm(B1[:, h * 128:(h + 1) * 128], Lc[:, h, :], Mc[:, h, :], rd=[Lk, Mk], wr=[B1])
                    yield
                    copy(Ln_b[:, :, :], B0[:, :].rearrange(v4, h=4), rd=[B0], wr=[Ln_b], eng=ACT)
                    if not last:
                        copy(Mn_b[:, :, :], B1[:, :].rearrange(v4, h=4), rd=[B1], wr=[Mn_b], eng=DVE)
                    for h in range(4):
                        mm(B3[:, h * 128:(h + 1) * 128], Ln_b[:, h, :], Pc[:, h, :], rd=[Ln_b, Pk], wr=[B3])
                    yield
                    tt(DVE, Pn_b[:, :, :], Pc, B3[:, :].rearrange(v4, h=4), ALU.add, rd=[Pk, B3], wr=[Pn_b])
                    Lc, Lk = Ln_b[:, :, :], Ln_b
                    if not last:
                        Mc, Mk = Mn_b[:, :, :], Mn_b
                    Pc, Pk = Pn_b[:, :, :], Pn_b
                Sst, tmpS = w["Sst"], w["tmpS"]
                for h in range(4):
                    mm(B2[:, h * 64:(h + 1) * 64], kTc[:, h, :], Sst[:, h, :], rd=[kTc, Sst], wr=[B2])
                    mm(B2[:, 256 + h * 64:256 + (h + 1) * 64], qTc[:, h, :], Sst[:, h, :], rd=[qTc, Sst], wr=[B2])
                vb, rhs2, vnew, tmp, kdec = w["vb"], w["rhs2"], w["vnew"], w["tmp"], w["kdec"]
                tt(POOL, vb[:, :, :], vtmc[:, :].rearrange(h4, h=4), bcast(beta[:, d, c, :], 2, 64), ALU.mult,
                   rd=[vtmc, beta], wr=[vb])
                tt(POOL, kdec[:, :].rearrange(h4, h=4), ktmc[:, :].rearrange(h4, h=4), bcast(edec[:, d, c, :], 2, 64),
                   ALU.mult, rd=[ktmc, edec], wr=[kdec])
                tt(DVE, rhs2[:, :, :], B2[:, 0:256].rearrange(h4, h=4), bcast(negegb[:, d, c, :], 2, 64), ALU.mult,
                   rd=[B2, negegb], wr=[rhs2])
                tt(DVE, rhs2[:, :, :], rhs2[:, :, :], vb[:, :, :], ALU.add, rd=[rhs2, vb], wr=[rhs2])
                yield
                ysb, zsb = w["ysb"], w["zsb"]
                for h in range(4):
                    mm(B3[:, h * 64:(h + 1) * 64], Pc[:, h, :], rhs2[:, h, :], rd=[Pk, rhs2], wr=[B3])
                copy(ysb[:, :, :], B3[:, 0:256].rearrange(h4, h=4), rd=[B3], wr=[ysb], eng=ACT)
                yield
                for h in range(4):
                    mm(B3[:, 256 + h * 64:256 + (h + 1) * 64], Moff[:, h, :], ysb[:, h, :], rd=[LMA, ysb], wr=[B3])
                tt(DVE, zsb[:, :, :], rhs2[:, :, :], B3[:, 256:512].rearrange(h4, h=4), ALU.subtract, rd=[rhs2, B3], wr=[zsb])
                yield
                for h in range(4):
                    mm(B3[:, h * 64:(h + 1) * 64], Pc[:, h, :], zsb[:, h, :], rd=[Pk, zsb], wr=[B3])
                copy(vnew[:, :, :], B3[:, 0:256].rearrange(h4, h=4), rd=[B3], wr=[vnew], eng=ACT)
                yield
                for h in range(4):
                    mm(B3[:, 256 + h * 64:256 + (h + 1) * 64], ATm[:, h, :], vnew[:, h, :], rd=[LMA, vnew], wr=[B3])
                osb = w["osb"][step % 2]
                tt(DVE, tmp[:, :, :], B2[:, 256:512].rearrange(h4, h=4), bcast(eg[:, d, c, :], 2, 64), ALU.mult,
                   rd=[B2, eg], wr=[tmp])
                tt(DVE, osb[:, :, :], tmp[:, :, :], B3[:, 256:512].rearrange(h4, h=4), ALU.add, rd=[tmp, B3], wr=[osb])
                yield
                if not (l == L - 1 and c < NCC):
                    dma_out(o_d[d][tok:tok + 128, :], osb[:, :, :].rearrange("p h e -> p (h e)"), rd=[osb], wr=[o_d[d]])
                for h in range(4):
                    mm(B0[0:64, h * 64:(h + 1) * 64], kdec[:, h * 64:(h + 1) * 64], vnew[:, h, :], rd=[kdec, vnew], wr=[B0])
                tt(POOL, tmpS[:, :, :], Sst[:, :, :], bcast(egtot[0:64, d, c, :], 2, 64), ALU.mult, rd=[Sst, egtot], wr=[tmpS])
                tt(DVE, Sst[:, :, :], tmpS[:, :, :], B0[0:64, 0:256].rearrange(h4, h=4), ALU.add, rd=[tmpS, B0], wr=[Sst])
            for step in range(NCH):
                gens = [chunk_body(step, 0), chunk_body(step, 1)]
                while gens:
                    for g_ in list(gens):
                        try:
                            next(g_)
                        except StopIteration:
                            gens.remove(g_)
            gating(l, gnw_in, 0, 0)

        def phaseC(l, b):
            S.barrier(); A.reset()
            wg = [A.alloc(f"wg{d}", [128], parts=32) for d in range(2)]
            nbg = A.alloc("nbg", [2])
            for d in range(2):
                memset(POOL, wg[d][:, :], 0.0, wr=[wg[d]])
                dma_in(wg[d][d * 16:(d + 1) * 16, :], w_gk[l, d, :, :], wr=[wg[d]])
            dma_in(nbg[:, :], bgk_in[l, :, :], wr=[nbg])
            ts(DVE, nbg[:, :], nbg[:, :], -1.0, None, ALU.mult, rd=[nbg], wr=[nbg])
            W = {}
            for d in range(2):
                w = {}
                for nm, shp, parts in (("gl", [512], 32), ("gq", [512], 128), ("gk", [512], 128), ("nl", [512], 128),
                                       ("cs", [512], 128), ("t", [512], 128), ("eq", [512], 128), ("ek", [512], 128),
                                       ("ekh", [512], 128), ("tot", [8], 128), ("qm", [4, 128], 128), ("sc", [4, 128], 128),
                                       ("khT", [128], 128), ("Sx", [256], 128)):
                    w[nm] = A.alloc(f"c{nm}{d}", shp, parts=parts)
                w["vt"] = [A.alloc(f"cvt{d}{i}", [256]) for i in range(2)]
                w["osb"] = [A.alloc(f"cosb{d}{i}", [256]) for i in range(2)]
                W[d] = w
                memset(POOL, w["Sx"][:, :], 0.0, wr=[w["Sx"]])
            gfwd = list(groups)
            gbwd = [groups[0]] + list(reversed(groups[1:]))
            gorder = [gfwd, gbwd]
            cnt = 0
            for gstep in range(len(groups)):
                for d in range(2):
                    w = W[d]
                    t0, n = gorder[d][gstep]
                    nc_ = n // 128
                    B0, B1, B2, B3 = P[d * 4], P[d * 4 + 1], P[d * 4 + 2], P[d * 4 + 3]
                    gl, gq, gk_, nl, cs, t_, eq, ek, ekh, tot = (w[k_] for k_ in
                                                                   ("gl", "gq", "gk", "nl", "cs", "t", "eq", "ek", "ekh", "tot"))
                    dma_in(gl[:, 0:n], glT[:, t0:t0 + n], rd=[glT], wr=[gl])
                    dma_in(gq[:, 0:n], gqT[:, t0:t0 + n], rd=[gqT], wr=[gq])
                    dma_in(gk_[:, 0:n], gkT[:, t0:t0 + n], rd=[gkT], wr=[gk_])
                    mm(B0[:, 0:n], wg[d][:, :], gl[:, 0:n], rd=[wg[d], gl], wr=[B0])
                    act(nl[:, 0:n], B0[:, 0:n], AF.Exp, bias=nbg[:, d:d + 1], scale=-1.0, rd=[B0, nbg], wr=[nl])
                    act(nl[:, 0:n], nl[:, 0:n], AF.Ln, bias=1.0, rd=[nl], wr=[nl])
                    S.op(DVE, lambda e, o_=cs[:, 0:n], a_=C("reset")[:, 0:n], b_=nl[:, 0:n]:
                         e.tensor_tensor_scan(out=o_, data0=a_, data1=b_, initial=0.0, op0=ALU.mult, op1=ALU.add),
                         reads=[nl, cst], writes=[cs])
                    c3 = "p (c s) -> p c s"
                    copy(tot[:, 0:nc_], cs[:, 0:n].rearrange(c3, s=128)[:, :, 127], rd=[cs], wr=[tot], eng=DVE)
                    if d == 1:
                        tt(DVE, t_[:, 0:n], nl[:, 0:n], cs[:, 0:n], ALU.subtract, rd=[nl, cs], wr=[t_])
                        tt(DVE, cs[:, 0:n].rearrange(c3, s=128), t_[:, 0:n].rearrange(c3, s=128),
                           bcast(tot[:, 0:nc_], 2, 128), ALU.add, rd=[t_, tot], wr=[cs])
                    act(eq[:, 0:n], cs[:, 0:n], AF.Exp, scale=-1.0 / 16, rd=[cs], wr=[eq])
                    act(ek[:, 0:n], cs[:, 0:n], AF.Exp, scale=1.0 / 16, rd=[cs], wr=[ek])
                    tt(DVE, t_[:, 0:n].rearrange(c3, s=128), cs[:, 0:n].rearrange(c3, s=128),
                       bcast(tot[:, 0:nc_], 2, 128), ALU.subtract, rd=[cs, tot], wr=[t_])
                    act(ekh[:, 0:n], t_[:, 0:n], AF.Exp, scale=1.0 / 16, rd=[t_], wr=[ekh])
                    act(tot[:, 4:4 + nc_], tot[:, 0:nc_], AF.Exp, scale=-1.0 / 16, rd=[tot], wr=[tot])
                    stt(eq[:, 0:n], gq[:, 0:n], 32 ** -0.5, eq[:, 0:n], ALU.mult, ALU.mult, rd=[gq, eq], wr=[eq])
                    tt(DVE, ek[:, 0:n], gk_[:, 0:n], ek[:, 0:n], ALU.mult, rd=[gk_, ek], wr=[ek])
                    tt(POOL, ekh[:, 0:n], gk_[:, 0:n], ekh[:, 0:n], ALU.mult, rd=[gk_, ekh], wr=[ekh])
                    clist = list(range(nc_)) if d == 0 else list(range(nc_ - 1, -1, -1))
                    for ci in clist:
                        tok = t0 + ci * 128
                        csl = slice(ci * 128, (ci + 1) * 128)
                        qm, sc, khT, Sx = w["qm"], w["sc"], w["khT"], w["Sx"]
                        vt = w["vt"][cnt % 2]; osb = w["osb"][cnt % 2]; cnt += 1
                        dma_in(vt[:, :], PT[tok:tok + 128, 272:528], rd=[PT], wr=[vt])
                        tt(DVE, qm[:, :, :], bcast(eq[:, csl], 1, 4), bcast(C("hm"), 2, 128), ALU.mult,
                           rd=[eq, cst], wr=[qm])
                        mm(B1[:, :], ek[:, csl], qm[:, :, :].rearrange("p h t -> p (h t)"), rd=[ek, qm], wr=[B1])
                        tt(DVE, sc[:, :, :].rearrange("p h t -> p (h t)"), B1[:, :], C(f"m01{d}"), ALU.mult,
                           rd=[B1, cst], wr=[sc])
                        tr(B2[:, 0:128], ekh[:, csl], rd=[ekh], wr=[B2])
                        copy(khT[:, :], B2[:, 0:128], rd=[B2], wr=[khT], eng=ACT)
                        for h in range(4):
                            mm(B3[:, h * 64:(h + 1) * 64], qm[:, h, :], Sx[:, h * 64:(h + 1) * 64], start=True, stop=False,
                               rd=[qm, Sx], wr=[B3])
                            mm(B3[:, h * 64:(h + 1) * 64], sc[:, h, :], vt[:, h * 64:(h + 1) * 64], start=False, stop=True,
                               rd=[sc, vt], wr=[B3])
                        copy(osb[:, :], B3[:, 0:256], rd=[B3], wr=[osb], eng=ACT)
                        if not (l == L - 1 and tok < TC):
                            dma_out(o_d[d][tok:tok + 128, :], osb[:, :], rd=[osb], wr=[o_d[d]])
                        mm(B2[:, 256:512], khT[:, :], vt[:, :], rd=[khT, vt], wr=[B2])
                        stt(Sx[:, :], Sx[:, :], tot[:, 4 + ci:5 + ci], B2[:, 256:512], ALU.mult, ALU.add,
                            rd=[Sx, tot, B2], wr=[Sx])
            gating(l, lnw_in, 528, 768)

        def phaseD(l, b):
            S.barrier(); A.reset()
            wq = A.alloc("wq", [3, 768]); wqs = A.alloc("wqs", [3, 768])
            wkn = A.alloc("wkn", [2, 8, 96]); wv = A.alloc("wv", [2, 8, 64])
            nq = A.alloc("nq", [3]); nkv = A.alloc("nkv", [2])
            rQ = A.alloc("rQ", [2, T], parts=96); rK = A.alloc("rK", [2, T], parts=32)
            dma_in(rQ[:, :, :], ropeQ_in[:, :, :], wr=[rQ])
            dma_in(rK[:, :, :], ropeK_in[:, :, :], wr=[rK])
            dma_in(nq[:, :], qnw_in[l, :, :], wr=[nq])
            dma_in(nkv[:, :], kvnw_in[l, :, :], wr=[nkv])
            memset(POOL, wqs[:, :, :], 0.0, wr=[wqs])
            memset(POOL, wkn[:, :, :, :], 0.0, wr=[wkn])
            for j in range(3):
                rows = slice(j * 128, (j + 1) * 128)
                dma_in(wq[:, j, :], w_uq[l, rows, :], wr=[wq])
                src = w_uq[l, rows, :].rearrange("k (h c) -> k h c", c=96)
                dst = wqs[:, j, :].rearrange("k (h c) -> k h c", c=96)
                dma_in(dst[:, :, 64:80], src[:, :, 80:96], wr=[wqs])
                dma_in(dst[:, :, 80:96], src[:, :, 64:80], wr=[wqs])
            for j in range(2):
                rows = slice(j * 128, (j + 1) * 128)
                src = w_ukv[l, rows, :].rearrange("k (h c) -> k h c", c=128)
                dma_in(wkn[:, j, :, 0:64], src[:, :, 0:64], wr=[wkn])
                dma_in(wv[:, j, :, :], src[:, :, 64:128], wr=[wv])
            for j in range(3):
                ts(DVE, wq[:, j, :], wq[:, j, :], nq[:, j:j + 1], None, ALU.mult, rd=[wq, nq], wr=[wq])
                ts(DVE, wqs[:, j, :], wqs[:, j, :], nq[:, j:j + 1], None, ALU.mult, rd=[wqs, nq], wr=[wqs])
            for j in range(2):
                ts(DVE, wkn[:, j, :, :].rearrange("p h c -> p (h c)"), wkn[:, j, :, :].rearrange("p h c -> p (h c)"),
                   nkv[:, j:j + 1], None, ALU.mult, rd=[wkn, nkv], wr=[wkn])
                ts(DVE, wv[:, j, :, :].rearrange("p h c -> p (h c)"), wv[:, j, :, :].rearrange("p h c -> p (h c)"),
                   nkv[:, j:j + 1], None, ALU.mult, rd=[wv, nkv], wr=[wv])
            cq = [A.alloc(f"cq{i}", [3, 512]) for i in range(2)]
            ckv = [A.alloc(f"ckv{i}", [2, 512]) for i in range(2)]
            kr = [A.alloc(f"kr{i}", [2, 512], parts=32) for i in range(2)]
            sq = A.alloc("sqd", [3, 512]); rr = A.alloc("rr", [512]); t1 = A.alloc("dt1", [512], parts=96)
            t2 = A.alloc("dt2", [512], parts=96)
            qst = [A.alloc(f"qst{i}", [512], parts=96) for i in range(2)]
            kst = [A.alloc(f"kst{i}", [512], parts=96) for i in range(2)]
            vst = [A.alloc(f"vst{i}", [512]) for i in range(2)]
            mx = A.alloc("mx", [16]); mt = A.alloc("mt", [2])
            memset(POOL, mx[:, :], 0.0, wr=[mx])
            pi = 0

            def nextP():
                nonlocal pi
                pi += 1
                return P[pi % 8]

            def norm_stat(src_buf, src_ap, rows, n, col):
                tt(POOL, t2[0:rows, 0:n], src_ap, src_ap, ALU.mult, rd=[src_buf], wr=[t2])
                pp = nextP()
                mm(pp[:, 0:n], C("ones")[0:rows, :], t2[0:rows, 0:n], rd=[t2, cst], wr=[pp])
                S.op(DVE, lambda e, o_=mt[:, 0:1], i_=pp[:, 0:n]: e.tensor_reduce(out=o_, in_=i_, axis=AX.X, op=ALU.max),
                     reads=[pp], writes=[mt])
                tt(DVE, mx[:, col:col + 1], mx[:, col:col + 1], mt[:, 0:1], ALU.max, rd=[mx, mt], wr=[mx])

            for gi, (t0, n) in enumerate(groups):
                cq_, ckv_, kr_ = cq[gi % 2], ckv[gi % 2], kr[gi % 2]
                dma_in(cq_[:, :, 0:n], cqT[:, t0:t0 + n].rearrange("(j p) t -> p j t", p=128), rd=[cqT], wr=[cq_])
                dma_in(ckv_[:, :, 0:n], ckvT[:, t0:t0 + n].rearrange("(j p) t -> p j t", p=128), rd=[ckvT], wr=[ckv_])
                dma_in(kr_[:, :, 0:n], krT[:, t0:t0 + n].rearrange("(j p) t -> p j t", p=32), rd=[krT], wr=[kr_])
                for (buf, nj, dim) in ((cq_, 3, 384.0), (ckv_, 2, 256.0)):
                    act(sq[:, 0:nj, 0:n], buf[:, 0:nj, 0:n], AF.Square, rd=[buf], wr=[sq])
                    pp = nextP()
                    for j in range(nj):
                        mm(pp[:, 0:n], C("ones"), sq[:, j, 0:n], start=(j == 0), stop=(j == nj - 1), rd=[sq, cst], wr=[pp])
                    act(rr[:, 0:n], pp[:, 0:n], AF.Sqrt, bias=EPS, scale=1.0 / dim, rd=[pp], wr=[rr])
                    recip(rr[:, 0:n], rr[:, 0:n], rd=[rr], wr=[rr])
                    tt(DVE, buf[:, 0:nj, 0:n], buf[:, 0:nj, 0:n], bcast(rr[:, 0:n], 1, nj), ALU.mult, rd=[buf, rr], wr=[buf])
                tt(DVE, kr_[:, 0, 0:n], kr_[:, 0, 0:n], rK[:, 0, t0:t0 + n], ALU.mult, rd=[kr_, rK], wr=[kr_])
                tt(POOL, kr_[:, 1, 0:n], kr_[:, 1, 0:n], rK[:, 1, t0:t0 + n], ALU.mult, rd=[kr_, rK], wr=[kr_])
                tt(DVE, kr_[:, 0, 0:n], kr_[:, 0, 0:n], kr_[:, 1, 0:n], ALU.add, rd=[kr_], wr=[kr_])
                need_q = (t0 >= TC) or (l < L - 1)
                for h in range(8):
                    pp = nextP(); ks = kst[h % 2]
                    for j in range(2):
                        mm(pp[0:96, 0:n], wkn[:, j, h, :], ckv_[:, j, 0:n], start=(j == 0), stop=False, rd=[wkn, ckv_], wr=[pp])
                    mm(pp[0:96, 0:n], C("esel", 32), kr_[:, 0, 0:n], start=False, stop=True, rd=[kr_, cst], wr=[pp])
                    copy(ks[:, 0:n], pp[0:96, 0:n], rd=[pp], wr=[ks])
                    dma_out(KT[h, :, t0:t0 + n], ks[:, 0:n], rd=[ks], wr=[KT])
                    norm_stat(ks, ks[:, 0:n], 96, n, 8 + h)
                    if not need_q:
                        continue
                    p1 = nextP(); p2 = nextP(); qs = qst[h % 2]
                    for j in range(3):
                        mm(p1[0:96, 0:n], wq[:, j, h * 96:(h + 1) * 96], cq_[:, j, 0:n], start=(j == 0), stop=(j == 2),
                           rd=[wq, cq_], wr=[p1])
                    for j in range(3):
                        mm(p2[0:96, 0:n], wqs[:, j, h * 96:(h + 1) * 96], cq_[:, j, 0:n], start=(j == 0), stop=(j == 2),
                           rd=[wqs, cq_], wr=[p2])
                    tt(DVE, t1[:, 0:n], p1[0:96, 0:n], rQ[:, 0, t0:t0 + n], ALU.mult, rd=[p1, rQ], wr=[t1])
                    tt(DVE, qs[:, 0:n], p2[0:96, 0:n], rQ[:, 1, t0:t0 + n], ALU.mult, rd=[p2, rQ], wr=[qs])
                    tt(POOL, qs[:, 0:n], qs[:, 0:n], t1[:, 0:n], ALU.add, rd=[qs, t1], wr=[qs])
                    dma_out(QT[h, :, t0:t0 + n], qs[:, 0:n], rd=[qs], wr=[QT])
                    norm_stat(qs, qs[:, 0:n], 96, n, h)
                for i in range(n // 128):
                    pp = nextP(); vs = vst[i % 2]
                    for j in range(2):
                        mm(pp[:, :], ckv_[:, j, i * 128:(i + 1) * 128], wv[:, j, :, :].rearrange("p h c -> p (h c)"),
                           start=(j == 0), stop=(j == 1), rd=[ckv_, wv], wr=[pp])
                    copy(vs[:, :], pp[:, :], rd=[pp], wr=[vs])
                    dma_out(Vd[t0 + i * 128:t0 + (i + 1) * 128, :], vs[:, :], rd=[vs], wr=[Vd])
            tt(DVE, small[:, 0:8], mx[:, 0:8], mx[:, 8:16], ALU.mult, rd=[mx], wr=[small])
            act(small[:, 0:8], small[:, 0:8], AF.Sqrt, rd=[small], wr=[small])
            ts(DVE, small[:, 0:8], small[:, 0:8], -MLA_SCALE, None, ALU.mult, rd=[small], wr=[small])
            S.barrier(); A.reset()
            kt = [A.alloc(f"kt{i}", [T], parts=96) for i in range(2)]
            vh = [A.alloc(f"vh{i}", [NCH, 65]) for i in range(2)]
            for i in range(2):
                memset(POOL, vh[i][:, :, 64:65], 1.0, wr=[vh[i]])
            qg = [A.alloc(f"qg{i}", [512], parts=96) for i in range(2)]
            zg = [A.alloc(f"zg{i}", [512], parts=64) for i in range(2)]
            pT = [A.alloc(f"pT{i}", [512]) for i in range(4)]
            ot = [A.alloc(f"ot{i}", [512], parts=65) for i in range(2)]
            rd_ = [A.alloc(f"rden{i}", [512], parts=64) for i in range(2)]
            ys = [A.alloc(f"ysd{i}", [512], parts=64) for i in range(2)]
            qi = 0
            si = 0
            for h in range(8):
                kt_, vh_ = kt[h % 2], vh[h % 2]
                dma_in(kt_[:, :], KT[h, :, :], rd=[KT], wr=[kt_])
                dma_in(vh_[:, :, 0:64], Vd[:, h * 64:(h + 1) * 64].rearrange("(c p) e -> p c e", p=128), rd=[Vd], wr=[vh_])
                for (t0, n) in groups:
                    if t0 < TC:
                        if l == L - 1:
                            continue
                        kchunks = list(range(NCC))
                    else:
                        kchunks = list(range(NCH))
                    q_, z_, o_, r_, y_ = qg[qi % 2], zg[qi % 2], ot[qi % 2], rd_[qi % 2], ys[qi % 2]
                    pacc = P[6 + qi % 2]
                    qi += 1
                    dma_in(q_[:, 0:n], QT[h, :, t0:t0 + n], rd=[QT], wr=[q_])
                    dma_in(z_[:, 0:n], zmT[h * 64:(h + 1) * 64, t0:t0 + n], rd=[zmT], wr=[z_])
                    act(z_[:, 0:n], z_[:, 0:n], AF.Silu, rd=[z_], wr=[z_])
                    pend = []
                    nk = len(kchunks)

                    def pv(ki_, c_, pp_):
                        mm(pacc[0:65, 0:n], vh_[:, c_, :], pp_[:, 0:n], start=(ki_ == 0), stop=(ki_ == nk - 1),
                           rd=[vh_, pp_], wr=[pacc])

                    for ki, c in enumerate(kchunks):
                        ps_ = P[si % 4]; p_ = pT[si % 4]; si += 1
                        mm(ps_[:, 0:n], kt_[:, c * 128:(c + 1) * 128], q_[:, 0:n], rd=[kt_, q_], wr=[ps_])
                        act(p_[:, 0:n], ps_[:, 0:n], AF.Exp, bias=small[:, h:h + 1], scale=MLA_SCALE, rd=[ps_, small], wr=[p_])
                        pend.append((ki, c, p_))
                        if len(pend) > 2:
                            pv(*pend.pop(0))
                    for it_ in pend:
                        pv(*it_)
                    copy(o_[:, 0:n], pacc[0:65, 0:n], rd=[pacc], wr=[o_], eng=DVE)
                    pd = P[4 + qi % 2]
                    mm(pd[0:64, 0:n], C("selrow", 65), o_[:, 0:n], rd=[o_, cst], wr=[pd])
                    recip(r_[:, 0:n], pd[0:64, 0:n], rd=[pd], wr=[r_])
                    tt(DVE, y_[:, 0:n], o_[0:64, 0:n], r_[:, 0:n], ALU.mult, rd=[o_, r_], wr=[y_])
                    tt(POOL, y_[:, 0:n], y_[:, 0:n], z_[:, 0:n], ALU.mult, rd=[y_, z_], wr=[y_])
                    dma_out(yT[256 + h * 64:256 + (h + 1) * 64, t0:t0 + n], y_[:, 0:n], rd=[y_], wr=[yT])

        out_events = []

        def phaseE(l, b):
            S.barrier(); A.reset()
            wo = A.alloc("wo", [8, D])
            for k in range(8):
                dma_in(wo[:, k, :], w_out[l, k * 128:(k + 1) * 128, :], wr=[wo])
            fn = A.alloc("fn", [D])
            if l == L - 1:
                dma_in(fn[:, :], fnw_in[:, :], wr=[fn])
            yt = [A.alloc(f"yt{i}", [8, 128]) for i in range(2)]
            xt = [A.alloc(f"ext{i}", [D]) for i in range(2)]
            tm = [A.alloc(f"etm{i}", [D]) for i in range(2)]
            junk = A.alloc("ejunk", [D])
            ss = [A.alloc(f"ess{i}", [2]) for i in range(2)]
            ti = 0
            for c in range(NCH):
                tok = c * 128
                if tok < TC and l == L - 1:
                    continue
                gb = gate_bc[NB if tok < TC else b]
                y_, x_, t_, s_ = yt[ti % 2], xt[ti % 2], tm[ti % 2], ss[ti % 2]
                pa, pb = P[(ti % 4) * 2], P[(ti % 4) * 2 + 1]
                ti += 1
                dma_in(y_[:, :, :], yT[:, tok:tok + 128].rearrange("(k p) t -> p k t", p=128), rd=[yT], wr=[y_])
                src, srd = x_src(l, b, tok)
                dma_in(x_[:, :], src, rd=srd, wr=[x_])
                for half, pp in ((0, pa), (1, pb)):
                    for k in range(8):
                        mm(pp[:, :], y_[:, k, :], wo[:, k, half * 512:(half + 1) * 512], start=(k == 0), stop=(k == 7),
                           rd=[y_, wo], wr=[pp])
                    hs = slice(half * 512, (half + 1) * 512)
                    tt(DVE, t_[:, hs], pp[:, :], gb[:, hs], ALU.mult, rd=[pp, gb], wr=[t_])
                tt(POOL, t_[:, :], t_[:, :], x_[:, :], ALU.add, rd=[t_, x_], wr=[t_])
                if l < L - 1:
                    dma_out(xs[b][tok:tok + 128, :], t_[:, :], rd=[t_], wr=[xs[b]])
                else:
                    act(junk[:, :], t_[:, :], AF.Square, accum=s_[:, 0:1], rd=[t_], wr=[junk, s_])
                    act(s_[:, 0:1], s_[:, 0:1], AF.Sqrt, bias=EPS, scale=1.0 / D, rd=[s_], wr=[s_])
                    recip(s_[:, 1:2], s_[:, 0:1], rd=[s_], wr=[s_])
                    stt(t_[:, :], t_[:, :], s_[:, 1:2], fn[:, :], ALU.mult, ALU.mult, rd=[t_, s_, fn], wr=[t_])
                    ev = dma_out(out[b, tok - TC:tok - TC + 128, :], t_[:, :], rd=[t_], wr=[("out", b, c)])
                    out_events.append(ev)

        for l in range(L):
            if "0" in cfg.phases:
                phase0(l)
            for b in range(NB):
                if "A" in cfg.phases:
                    phaseA(l, b)
                if "B" in cfg.phases:
                    phaseB(l, b)
                if "C" in cfg.phases:
                    phaseC(l, b)
                if "D" in cfg.phases:
                    phaseD(l, b)
                if "E" in cfg.phases:
                    phaseE(l, b)
        S.barrier()
        fin = list(S.floor.items())
        S.emit(final_events=fin)
        build.stats = {k: len(v) for k, v in S.ops.items()}
    return nc


def _col(v, nchunk):
    return np.ascontiguousarray(np.asarray(v, np.float32).reshape(nchunk, 128).T)


def make_in_maps(inp, n_cores, NB, TL, L):
    f = lambda a: np.ascontiguousarray(np.asarray(a, dtype=np.float32))
    rq, rk = _rope_tables(TL)
    shared = {
        "w_ada": f(inp["w_ada"]), "w_in": f(inp["w_in"]), "w_uq": f(inp["mla_w_uq"]), "w_ukv": f(inp["mla_w_ukv"]),
        "w_gk": f(inp["gla_w_gk"]), "w_out": f(inp["w_out"]),
        "b_adaT": np.stack([_col(inp["b_ada"][l], 24) for l in range(L)]),
        "convw": np.ascontiguousarray(np.asarray(inp["gdn_conv_w"], np.float32).reshape(L, 5, 6, 128).transpose(0, 3, 2, 1)),
        "alog": np.ascontiguousarray(np.broadcast_to(np.asarray(inp["gdn_a_log"], np.float32).reshape(L, 1, 8), (L, 128, 8))),
        "dtb": np.ascontiguousarray(np.broadcast_to(np.asarray(inp["gdn_dt_bias"], np.float32).reshape(L, 1, 8), (L, 128, 8))),
        "gnw": np.ascontiguousarray(np.broadcast_to(np.tile(np.asarray(inp["gdn_norm_w"], np.float32), (1, 4))[:, None, :], (L, 128, 256))),
        "lnw": np.ascontiguousarray(np.broadcast_to(np.tile(np.asarray(inp["gla_norm_w"], np.float32), (1, 4))[:, None, :], (L, 128, 256))),
        "qnw": np.stack([_col(inp["mla_q_norm_w"][l], 3) for l in range(L)]),
        "kvnw": np.stack([_col(inp["mla_kv_norm_w"][l], 2) for l in range(L)]),
        "b_gkT": np.stack([np.ascontiguousarray(np.asarray(inp["gla_b_gk"][l], np.float32).T) for l in range(L)]),
        "fnw": np.ascontiguousarray(np.broadcast_to(np.asarray(inp["final_norm_w"], np.float32)[None, :], (128, D))),
        "consts": CONST_ARR, "ropeQ": rq, "ropeK": rk,
    }
    maps = []
    x = np.asarray(inp["x"], np.float32); ctx = np.asarray(inp["ctx"], np.float32)
    c = np.asarray(inp["c"], np.float32); c_ctx = np.asarray(inp["c_ctx"], np.float32)
    for i in range(n_cores):
        bs = slice(i * NB, (i + 1) * NB)
        vecs = np.concatenate([c[bs], c_ctx[None, :]], axis=0)
        cT = np.ascontiguousarray(vecs.reshape(NB + 1, 8, 128).transpose(2, 1, 0))
        m = dict(shared)
        m["x"] = np.ascontiguousarray(x[bs]); m["ctx"] = np.ascontiguousarray(ctx[bs]); m["cT"] = cT
        maps.append(m)
    return maps


_NC_CACHE = {}


def kernel(**inputs):
    B, TL, _ = inputs["x"].shape
    L = inputs["w_in"].shape[0]
    n_cores = 8
    NB = B // n_cores
    key = (TL, NB, L)
    if key not in _NC_CACHE:
        _NC_CACHE[key] = build(Cfg(TL=TL, NB=NB, L=L))
    nc = _NC_CACHE[key]
    maps = make_in_maps(inputs, n_cores, NB, TL, L)
    res = run_bass_kernel_spmd(nc, maps, core_ids=list(range(n_cores)))
    return np.concatenate([np.asarray(r["out"], np.float32) for r in res.results], axis=0)
```

```python
import contextlib
import numpy as np
import concourse.bass as bass
import concourse.mybir as mybir
from concourse.bass_utils import run_bass_kernel_spmd

F32 = mybir.dt.float32
AF = mybir.ActivationFunctionType
ALU = mybir.AluOpType
AX = mybir.AxisListType

D = 1024
TC = 256
N_IN = 3024
O_QKV, O_ZG, O_A, O_B, O_CQ, O_CKV, O_KR, O_ZM, O_GQ, O_GK, O_GV, O_GZ, O_GL = (
    0, 768, 1024, 1032, 1040, 1424, 1680, 1712, 2224, 2352, 2480, 2736, 2992)
EPS = 1e-6
MLA_SCALE = 96 ** -0.5
NEG = -30000.0

PE, ACT, DVE, POOL, SP = "pe", "act", "dve", "pool", "sp"
COMPUTE = (PE, ACT, DVE, POOL)
NDSEM = 12


class Buf:
    def __init__(self, ap, key):
        self.ap = ap
        self.key = key

    def __getitem__(self, idx):
        return self.ap[idx]


def _key(x):
    if isinstance(x, Buf):
        return x.key
    if isinstance(x, tuple) and len(x) == 2 and isinstance(x[0], Buf):
        return (x[0].key, x[1])
    return x


class Sched:
    def __init__(self, nc):
        self.nc = nc
        self.ops = {e: [] for e in (PE, ACT, DVE, POOL, SP)}
        self.cnt = {e: 0 for e in COMPUTE}
        self.dcnt = {}
        self.dnext = {SP: 0, POOL: 0, ACT: 0}
        self.seen = {e: {} for e in (PE, ACT, DVE, POOL, SP)}
        self.lastw = {}
        self.readers = {}
        self.floor = {}
        self.sems = {}

    def _need(self, eng, ev, waits, same_ok=False):
        if ev is None:
            return
        k, v = ev
        if same_ok and k == ("c", eng):
            return
        if self.seen[eng].get(k, 0) >= v:
            return
        self.seen[eng][k] = v
        waits[k] = max(waits.get(k, 0), v)

    def barrier(self):
        fl = {("c", e): v for e, v in self.cnt.items() if v}
        fl.update(self.dcnt)
        self.floor = fl

    def op(self, eng, fn, reads=(), writes=(), dma=False):
        reads = [_key(r) for r in reads]
        writes = [_key(w) for w in writes]
        waits = {}
        for k, v in self.floor.items():
            self._need(eng, (k, v), waits)
        for r in reads:
            self._need(eng, self.lastw.get(r), waits)
        pe_ok = (eng == PE)
        for w in writes:
            self._need(eng, self.lastw.get(w), waits, same_ok=pe_ok)
            for ev in self.readers.get(w, ()):
                self._need(eng, ev, waits, same_ok=pe_ok)
        if dma:
            slot = self.dnext[eng] % NDSEM
            self.dnext[eng] += 1
            k = ("d", eng, slot)
            if self.dcnt.get(k, 0):
                self._need(eng, (k, self.dcnt[k]), waits)
            self.dcnt[k] = self.dcnt.get(k, 0) + 16
            ev = (k, self.dcnt[k])
            inc = (k, 16)
        else:
            self.cnt[eng] += 1
            k = ("c", eng)
            ev = (k, self.cnt[eng])
            inc = (k, 1)
        for w in writes:
            self.lastw[w] = ev
            self.readers[w] = []
        for r in reads:
            if r not in writes:
                self.readers.setdefault(r, []).append(ev)
        self.ops[eng].append((sorted(waits.items(), key=str), fn, inc))
        return ev

    def emit(self, final_events=()):
        nc = self.nc
        keys = set()
        for lst in self.ops.values():
            for _, _, inc in lst:
                keys.add(inc[0])
        with contextlib.ExitStack() as st:
            for k in sorted(keys, key=str):
                self.sems[k] = st.enter_context(nc.semaphore("s_" + "_".join(str(x) for x in k)))
            block = st.enter_context(nc.Block())

            def run(name):
                def body(eng):
                    for waits, fn, inc in self.ops[name]:
                        for k, v in waits:
                            eng.wait_ge(self.sems[k], v)
                        fn(eng).then_inc(self.sems[inc[0]], inc[1])
                    if name == SP:
                        for k, v in final_events:
                            eng.wait_ge(self.sems[k], v)
                return body

            block.sync(run(SP))
            block.tensor(run(PE))
            block.scalar(run(ACT))
            block.vector(run(DVE))
            block.gpsimd(run(POOL))


class Arena:
    def __init__(self, handle, size):
        self.h = handle
        self.size = size
        self.off = 0
        self.gen = 0

    def reset(self):
        self.off = 0
        self.gen += 1

    def alloc(self, name, fshape, parts=128):
        n = int(np.prod(fshape))
        n_al = (n + 7) // 8 * 8
        assert self.off + n_al <= self.size, f"arena overflow at {name}: {self.off}+{n_al}>{self.size}"
        ap = self.h[0:parts, self.off:self.off + n]
        self.off += n_al
        if len(fshape) == 2:
            ap = ap.rearrange("p (a b) -> p a b", a=fshape[0])
        elif len(fshape) == 3:
            ap = ap.rearrange("p (a b c) -> p a b c", a=fshape[0], b=fshape[1])
        return Buf(ap, f"{name}@{self.gen}")


def bcast(ap, axis, n):
    a = ap.unsqueeze(axis)
    shp = list(a.shape)
    shp[axis] = n
    return a.broadcast_to(shp)


def _const_layout():
    p = np.arange(128)[:, None]
    f = np.arange(128)[None, :]
    items = []

    def add(name, arr):
        a = np.zeros((128, arr.shape[1]), np.float32)
        a[:arr.shape[0]] = arr
        items.append((name, a))

    add("ident", (p == f).astype(np.float32))
    add("ones", np.ones((128, 128), np.float32))
    add("trif", (p <= f).astype(np.float32))
    add("trib", (p >= f).astype(np.float32))
    add("bd", ((p // 64) == (f // 64)).astype(np.float32))

    def m(valid):
        return np.tile(np.where(valid, 0.0, NEG).astype(np.float32), (1, 4))

    same = (p // 64) == (f // 64)
    add("mL0", m((f < p) & same)); add("mS0", m((p < f) & same)); add("mI0", m(p <= f)); add("mO0", m((p < f) & ~same))
    add("mL1", m((f > p) & same)); add("mS1", m((p > f) & same)); add("mI1", m(p >= f)); add("mO1", m((p > f) & ~same))
    add("m010", np.tile((p <= f).astype(np.float32), (1, 4)))
    add("m011", np.tile((p >= f).astype(np.float32), (1, 4)))
    add("hm", (p // 32 == np.arange(4)[None, :]).astype(np.float32))
    selb = np.zeros((8, 2, 4, 64), np.float32)
    for d in range(2):
        for h in range(4):
            selb[d * 4 + h, d, h, :] = 1.0
    add("selb", selb.reshape(8, 512))
    esel = np.zeros((32, 96), np.float32)
    esel[np.arange(32), 64 + np.arange(32)] = 1.0
    add("esel", esel)
    selrow = np.zeros((65, 64), np.float32)
    selrow[64, :] = 1.0
    add("selrow", selrow)
    add("reset", np.tile((f % 128 != 0).astype(np.float32), (128, 4)))
    add("bvals", np.tile(np.array([[EPS, 1.0, 64 * EPS, 0.0]], np.float32), (128, 1)))
    off = {}
    o = 0
    for name, a in items:
        off[name] = (o, a.shape[1])
        o += a.shape[1]
    return off, np.concatenate([a for _, a in items], axis=1)


CONST_OFF, CONST_ARR = _const_layout()
NCONST = CONST_ARR.shape[1]


def _rope_tables(TL):
    T = TC + TL
    rows = TL // 64
    row = np.broadcast_to(np.arange(rows, dtype=np.float32)[:, None], (rows, 64)).reshape(-1)
    col = np.broadcast_to(np.arange(64, dtype=np.float32)[None, :], (rows, 64)).reshape(-1)
    inv = (np.float32(10000.0) ** (-np.arange(8, dtype=np.float32) / np.float32(8))).astype(np.float32)
    ang = np.concatenate([row[:, None] * inv, col[:, None] * inv], axis=-1).astype(np.float32)
    cos = np.cos(ang).astype(np.float32).T
    sin = np.sin(ang).astype(np.float32).T
    rq = np.zeros((96, 2, T), np.float32)
    rq[:, 0, :] = 1.0
    rq[64:80, 0, TC:] = cos
    rq[80:96, 0, TC:] = cos
    rq[64:80, 1, TC:] = -sin
    rq[80:96, 1, TC:] = sin
    rk = np.ascontiguousarray(rq[64:96])
    return rq, rk


class Cfg:
    def __init__(self, TL=4096, NB=2, L=2, debug=False, phases="0ABCDE"):
        self.TL, self.NB, self.L, self.debug, self.phases = TL, NB, L, debug, phases


ARENA_F = 40448


def build(cfg):
    nc = bass.Bass("TRN2", target_bir_lowering=False)
    TL, NB, L = cfg.TL, cfg.NB, cfg.L
    T = TC + TL
    NCH = T // 128
    NCC = TC // 128
    groups = [(0, TC)] + [(TC + i * 512, 512) for i in range(TL // 512)]

    def din(name, shape):
        return nc.dram_tensor(name, list(shape), F32, kind="ExternalInput").ap()

    scr_kind = "ExternalOutput" if cfg.debug else "Internal"

    def dscr(name, shape):
        return Buf(nc.dram_tensor(name, list(shape), F32, kind=scr_kind).ap(), name)

    x_in = din("x", [NB, TL, D]); ctx_in = din("ctx", [NB, TC, D]); cT = din("cT", [128, 8, NB + 1])
    w_ada = din("w_ada", [L, D, 3 * D]); b_adaT = din("b_adaT", [L, 128, 24])
    w_in = din("w_in", [L, D, N_IN]); convw_in = din("convw", [L, 128, 6, 5])
    alog_in = din("alog", [L, 128, 8]); dtb_in = din("dtb", [L, 128, 8])
    gnw_in = din("gnw", [L, 128, 256]); qnw_in = din("qnw", [L, 128, 3]); w_uq = din("w_uq", [L, 384, 768])
    kvnw_in = din("kvnw", [L, 128, 2]); w_ukv = din("w_ukv", [L, 256, 1024])
    w_gk = din("w_gk", [L, 2, 16, 128]); bgk_in = din("b_gkT", [L, 128, 2]); lnw_in = din("lnw", [L, 128, 256])
    w_out = din("w_out", [L, D, D]); fnw_in = din("fnw", [128, D])
    consts_in = din("consts", [128, NCONST]); ropeQ_in = din("ropeQ", [96, 2, T]); ropeK_in = din("ropeK", [32, 2, T])
    out = Buf(nc.dram_tensor("out", [NB, TL, D], F32, kind="ExternalOutput").ap(), "out")

    xs = [dscr(f"xs{b}", [T, D]) for b in range(NB)]
    qkvT = dscr("qkvT", [768, T]); cqT = dscr("cqT", [384, T]); ckvT = dscr("ckvT", [256, T])
    krT = dscr("krT", [64, T]); zmT = dscr("zmT", [512, T]); gqT = dscr("gqT", [128, T]); gkT = dscr("gkT", [128, T])
    glT = dscr("glT", [32, T]); bT = dscr("bT", [8, T]); PT = dscr("PT", [T, 784]); yT = dscr("yT", [1024, T])
    o_d = [dscr(f"o_d{d}", [T, 256]) for d in range(2)]
    gqk = dscr("gqk", [2, 256, T]); ktm_d = dscr("ktm_d", [T, 256]); vtm_d = dscr("vtm_d", [T, 256])
    QT = dscr("QT", [8, 96, T]); KT = dscr("KT", [8, 96, T]); Vd = dscr("Vd", [T, 512])

    with contextlib.ExitStack() as st:
        def sb(name, shape):
            return Buf(st.enter_context(nc.sbuf_tensor(name, list(shape), F32)), name)

        cst = sb("cst", [128, NCONST])
        arena_h = st.enter_context(nc.sbuf_tensor("arena", [128, ARENA_F], F32))
        A = Arena(arena_h, ARENA_F)
        cs_t = sb("cs_t", [128, 8, NB + 1])
        modc = sb("modc", [128, 24, NB + 1])
        gate_bc = [sb(f"gate_bc{i}", [128, D]) for i in range(NB + 1)]
        badaT = sb("badaT", [128, 24])
        small = sb("small", [128, 64])
        P = [Buf(st.enter_context(nc.psum_tensor(f"P{i}", [128, 512], F32)), f"P{i}") for i in range(8)]
        S = Sched(nc)
        tog = [0]

        def C(name, rows=128):
            o, w = CONST_OFF[name]
            return cst[0:rows, o:o + w]

        def dma_in(out_ap, in_ap, rd=(), wr=(), q=SP):
            return S.op(q, lambda e: e.dma_start(out=out_ap, in_=in_ap), reads=rd, writes=wr, dma=True)

        def dma_out(out_ap, in_ap, rd=(), wr=()):
            return S.op(POOL, lambda e: e.dma_start(out=out_ap, in_=in_ap), reads=rd, writes=wr, dma=True)

        def mm(out_ap, lhsT, rhs, start=True, stop=True, rd=(), wr=()):
            S.op(PE, lambda e: e.matmul(out_ap, lhsT=lhsT, rhs=rhs, start=start, stop=stop), reads=rd, writes=wr)

        def tr(out_ap, in_ap, rd=(), wr=()):
            n = in_ap.shape[0]
            idn = C("ident")[0:n, 0:n]
            S.op(PE, lambda e: e.transpose(out_ap, in_ap, idn), reads=list(rd) + [cst], writes=wr)

        def act(out_ap, in_ap, func, bias=None, scale=None, accum=None, rd=(), wr=()):
            kw = {}
            if isinstance(bias, float):
                col = {EPS: 0, 1.0: 1, 64 * EPS: 2}[bias]
                bias = C("bvals")[0:in_ap.shape[0], col:col + 1]
                rd = list(rd) + [cst]
            if bias is not None:
                kw["bias"] = bias
            if scale is not None:
                kw["scale"] = scale
            if accum is not None:
                kw["accum_out"] = accum
            S.op(ACT, lambda e: e.activation(out=out_ap, in_=in_ap, func=func, **kw), reads=rd, writes=wr)

        def tt(eng, out_ap, in0, in1, op, rd=(), wr=()):
            S.op(eng, lambda e: e.tensor_tensor(out=out_ap, in0=in0, in1=in1, op=op), reads=rd, writes=wr)

        def ts(eng, out_ap, in0, s1, s2, op0, op1=None, rd=(), wr=()):
            if op1 is None:
                S.op(eng, lambda e: e.tensor_scalar(out=out_ap, in0=in0, scalar1=s1, scalar2=None, op0=op0),
                     reads=rd, writes=wr)
            else:
                S.op(eng, lambda e: e.tensor_scalar(out=out_ap, in0=in0, scalar1=s1, scalar2=s2, op0=op0, op1=op1),
                     reads=rd, writes=wr)

        def stt(out_ap, in0, scalar, in1, op0, op1, rd=(), wr=()):
            S.op(DVE, lambda e: e.scalar_tensor_tensor(out=out_ap, in0=in0, scalar=scalar, in1=in1, op0=op0, op1=op1),
                 reads=rd, writes=wr)

        def recip(out_ap, in_ap, rd=(), wr=()):
            S.op(DVE, lambda e: e.reciprocal(out=out_ap, in_=in_ap), reads=rd, writes=wr)

        def copy(out_ap, in_ap, rd=(), wr=(), eng=None):
            if eng is None:
                tog[0] ^= 1
                eng = ACT if tog[0] else DVE
            if eng == ACT:
                S.op(ACT, lambda e: e.copy(out=out_ap, in_=in_ap), reads=rd, writes=wr)
            else:
                S.op(eng, lambda e: e.tensor_copy(out=out_ap, in_=in_ap), reads=rd, writes=wr)

        def memset(eng, ap, val, wr=()):
            S.op(eng, lambda e: e.memset(ap, val), writes=wr)

        dma_in(cst[:, :], consts_in[:, :], wr=[cst])
        dma_in(cs_t[:, :, :], cT[:, :, :], wr=[cs_t])
        act(cs_t[:, :, :], cs_t[:, :, :], AF.Silu, rd=[cs_t], wr=[cs_t])

        def phase0(l):
            S.barrier(); A.reset()
            wada = A.alloc("wada", [8, 3 * D])
            dg = [A.alloc(f"dg{i}", [128]) for i in range(2)]
            for k in range(8):
                dma_in(wada[:, k, :], w_ada[l, k * 128:(k + 1) * 128, :], wr=[wada])
            dma_in(badaT[:, :], b_adaT[l, :, :], wr=[badaT])
            nv = NB + 1
            for blk in range(24):
                for k in range(8):
                    mm(P[0][:, blk * nv:(blk + 1) * nv], wada[:, k, blk * 128:(blk + 1) * 128], cs_t[:, k, :],
                       start=(k == 0), stop=(k == 7), rd=[wada, cs_t], wr=[P[0]])
            tt(DVE, modc[:, :, :], P[0][:, 0:24 * nv].rearrange("p (a b) -> p a b", b=nv),
               bcast(badaT[:, :], 2, nv), ALU.add, rd=[P[0], badaT], wr=[modc])
            ts(DVE, modc[:, 8:16, :], modc[:, 8:16, :], 1.0, None, ALU.add, rd=[modc], wr=[modc])
            for i in range(nv):
                for k in range(8):
                    d_ = dg[k % 2]
                    ts(DVE, d_[:, :], C("ident"), modc[:, 16 + k, i:i + 1], None, ALU.mult, rd=[modc, cst], wr=[d_])
                    pb = P[1 + k // 4]
                    mm(pb[:, (k % 4) * 128:(k % 4 + 1) * 128], C("ones"), d_[:, :], rd=[d_, cst], wr=[pb])
                copy(gate_bc[i][:, 0:512], P[1][:, :], rd=[P[1]], wr=[gate_bc[i]])
                copy(gate_bc[i][:, 512:1024], P[2][:, :], rd=[P[2]], wr=[gate_bc[i]])

        def x_src(l, b, tok):
            if l == 0:
                if tok < TC:
                    return ctx_in[b, tok:tok + 128, :], []
                return x_in[b, tok - TC:tok - TC + 128, :], []
            return xs[b][tok:tok + 128, :], [xs[b]]

        def phaseA(l, b):
            S.barrier(); A.reset()
            win = A.alloc("win", [8, N_IN]); wsw = A.alloc("wsw", [8, 32])
            hT = [A.alloc(f"hT{i}", [8, 512]) for i in range(2)]
            xt = [A.alloc(f"xt{i}", [D]) for i in range(2)]
            junk = A.alloc("junk", [D])
            stg = [A.alloc(f"stg{i}", [512]) for i in range(3)]
            stt_ = [A.alloc(f"stt{i}", [784]) for i in range(2)]
            ss = [A.alloc(f"ss{i}", [2]) for i in range(2)]
            for k in range(8):
                rows = slice(k * 128, (k + 1) * 128)
                dma_in(win[:, k, :], w_in[l, rows, :], wr=[win])
                dma_in(wsw[:, k, 0:16], w_in[l, rows, O_KR + 16:O_KR + 32], wr=[wsw])
                dma_in(wsw[:, k, 16:32], w_in[l, rows, O_KR:O_KR + 16], wr=[wsw])
            blocks = []
            for i in range(6):
                blocks.append((qkvT, i * 128, 128, win, O_QKV + i * 128))
            blocks.append((bT, 0, 8, win, O_B))
            for i in range(3):
                blocks.append((cqT, i * 128, 128, win, O_CQ + i * 128))
            for i in range(2):
                blocks.append((ckvT, i * 128, 128, win, O_CKV + i * 128))
            blocks.append((krT, 0, 32, win, O_KR))
            blocks.append((krT, 32, 32, wsw, 0))
            for i in range(4):
                blocks.append((zmT, i * 128, 128, win, O_ZM + i * 128))
            blocks.append((gqT, 0, 128, win, O_GQ))
            blocks.append((gkT, 0, 128, win, O_GK))
            blocks.append((glT, 0, 32, win, O_GL))
            ti = 0
            bi = 0
            for gi, (t0, n) in enumerate(groups):
                h = hT[gi % 2]
                col = NB if t0 < TC else b
                for i in range(n // 128):
                    tok = t0 + i * 128
                    xb = xt[ti % 2]; sb_ = ss[ti % 2]
                    src, srd = x_src(l, b, tok)
                    dma_in(xb[:, :], src, rd=srd, wr=[xb])
                    act(junk[:, :], xb[:, :], AF.Square, accum=sb_[:, 0:1], rd=[xb], wr=[junk, sb_])
                    act(sb_[:, 0:1], sb_[:, 0:1], AF.Sqrt, bias=EPS, scale=1.0 / D, rd=[sb_], wr=[sb_])
                    recip(sb_[:, 1:2], sb_[:, 0:1], rd=[sb_], wr=[sb_])
                    ts(DVE, xb[:, :], xb[:, :], sb_[:, 1:2], None, ALU.mult, rd=[xb, sb_], wr=[xb])
                    pa, pb = P[(ti % 2) * 2], P[(ti % 2) * 2 + 1]
                    for k in range(8):
                        pp = pa if k < 4 else pb
                        tr(pp[:, (k % 4) * 128:(k % 4 + 1) * 128], xb[:, k * 128:(k + 1) * 128], rd=[xb], wr=[pp])
                    for k in range(8):
                        pp = pa if k < 4 else pb
                        src_ps = pp[:, (k % 4) * 128:(k % 4 + 1) * 128]
                        dst = h[:, k, i * 128:(i + 1) * 128]
                        if k % 2 == 0:
                            act(dst, src_ps, AF.Identity, bias=modc[:, k, col:col + 1], scale=modc[:, 8 + k, col:col + 1],
                                rd=[pp, modc], wr=[h])
                        else:
                            ts(DVE, dst, src_ps, modc[:, 8 + k, col:col + 1], modc[:, k, col:col + 1], ALU.mult, ALU.add,
                               rd=[pp, modc], wr=[h])
                    ti += 1
                for (dst, r0, m, wt, c0) in blocks:
                    pp = P[4 + bi % 4]; sg = stg[bi % 3]
                    for k in range(8):
                        mm(pp[0:m, 0:n], wt[:, k, c0:c0 + m], h[:, k, 0:n], start=(k == 0), stop=(k == 7),
                           rd=[wt, h], wr=[pp])
                    copy(sg[0:m, 0:n], pp[0:m, 0:n], rd=[pp], wr=[sg])
                    dma_out(dst[r0:r0 + m, t0:t0 + n], sg[0:m, 0:n], rd=[sg], wr=[dst])
                    bi += 1
                for i in range(n // 128):
                    tok = t0 + i * 128
                    so = stt_[i % 2]
                    for (c0, w, o0) in ((O_ZG, 272, 0), (O_GV, 512, 272)):
                        pp = P[4 + bi % 4]
                        for k in range(8):
                            mm(pp[:, 0:w], h[:, k, i * 128:(i + 1) * 128], win[:, k, c0:c0 + w], start=(k == 0),
                               stop=(k == 7), rd=[win, h], wr=[pp])
                        copy(so[:, o0:o0 + w], pp[:, 0:w], rd=[pp], wr=[so])
                        bi += 1
                    dma_out(PT[tok:tok + 128, :], so[:, :], rd=[so], wr=[PT])

        def gating(l, nw_in, zcol, yrow):
            S.barrier(); A.reset()
            nw = A.alloc("nw", [256])
            dma_in(nw[:, :], nw_in[l, :, :], wr=[nw])
            o0 = [A.alloc(f"o0_{i}", [4, 256]) for i in range(2)]
            o1 = [A.alloc(f"o1_{i}", [4, 256]) for i in range(2)]
            zz = [A.alloc(f"zz{i}", [4, 256]) for i in range(2)]
            sq = A.alloc("sq", [4, 256])
            rs = [A.alloc(f"rs{i}", [32]) for i in range(2)]
            yst = [A.alloc(f"yst{i}", [2, 512]) for i in range(2)]
            gi = 0
            for (t0, n) in groups:
                if l == L - 1 and t0 < TC:
                    continue
                nc_ = n // 128
                a0, a1, z_, r_, ys = o0[gi % 2], o1[gi % 2], zz[gi % 2], rs[gi % 2], yst[gi % 2]
                dma_in(a0[:, 0:nc_, :], o_d[0][t0:t0 + n, :].rearrange("(c p) f -> p c f", p=128), rd=[o_d[0]], wr=[a0])
                dma_in(a1[:, 0:nc_, :], o_d[1][t0:t0 + n, :].rearrange("(c p) f -> p c f", p=128), rd=[o_d[1]], wr=[a1])
                dma_in(z_[:, 0:nc_, :], PT[t0:t0 + n, zcol:zcol + 256].rearrange("(c p) f -> p c f", p=128),
                       rd=[PT], wr=[z_])
                tt(DVE, a0[:, 0:nc_, :], a0[:, 0:nc_, :], a1[:, 0:nc_, :], ALU.add, rd=[a0, a1], wr=[a0])
                tt(POOL, sq[:, 0:nc_, :], a0[:, 0:nc_, :], a0[:, 0:nc_, :], ALU.mult, rd=[a0], wr=[sq])
                S.op(DVE, lambda e, o_=r_[:, 0:nc_ * 4], i_=sq[:, 0:nc_, :].rearrange("p c (h e) -> p (c h) e", h=4):
                     e.tensor_reduce(out=o_, in_=i_, axis=AX.X, op=ALU.add), reads=[sq], writes=[r_])
                act(r_[:, 0:nc_ * 4], r_[:, 0:nc_ * 4], AF.Sqrt, bias=EPS, scale=1.0 / 64, rd=[r_], wr=[r_])
                recip(r_[:, 16:16 + nc_ * 4], r_[:, 0:nc_ * 4], rd=[r_], wr=[r_])
                tt(DVE, a0[:, 0:nc_, :].rearrange("p c (h e) -> p (c h) e", h=4),
                   a0[:, 0:nc_, :].rearrange("p c (h e) -> p (c h) e", h=4),
                   bcast(r_[:, 16:16 + nc_ * 4], 2, 64), ALU.mult, rd=[a0, r_], wr=[a0])
                tt(POOL, a0[:, 0:nc_, :], a0[:, 0:nc_, :], bcast(nw[:, :], 1, nc_), ALU.mult, rd=[a0, nw], wr=[a0])
                act(z_[:, 0:nc_, :], z_[:, 0:nc_, :], AF.Silu, rd=[z_], wr=[z_])
                tt(DVE, a0[:, 0:nc_, :], a0[:, 0:nc_, :], z_[:, 0:nc_, :], ALU.mult, rd=[a0, z_], wr=[a0])
                for jj in range(2):
                    pp = P[(gi * 2 + jj) % 8]
                    for c in range(nc_):
                        tr(pp[:, c * 128:(c + 1) * 128], a0[:, c, jj * 128:(jj + 1) * 128], rd=[a0], wr=[pp])
                    copy(ys[:, jj, 0:n], pp[:, 0:n], rd=[pp], wr=[ys])
                dma_out(yT[yrow:yrow + 256, t0:t0 + n].rearrange("(j p) t -> p j t", p=128), ys[:, :, 0:n],
                        rd=[ys], wr=[yT])
                gi += 1

        def chunk_orders():
            fwd = list(range(NCH))
            bwd = list(range(NCC - 1, -1, -1)) + list(range(NCH - 1, NCC - 1, -1))
            return [fwd, bwd]

        def phaseB(l, b):
            S.barrier(); A.reset()
            cw = A.alloc("cw", [6, 5])
            dma_in(cw[:, :, :], convw_in[l, :, :, :], wr=[cw])
            xpad = [A.alloc(f"xpad{i}", [TL + 4]) for i in range(2)]
            acc = [A.alloc(f"acc{i}", [TL]) for i in range(2)]
            rn = [A.alloc(f"rn{i}", [512]) for i in range(2)]
            tst = [A.alloc(f"tst{i}", [4, 128]) for i in range(2)]
            it = 0
            ci = 0
            for blk in range(6):
                for (s0, n) in ((0, TC), (TC, TL)):
                    xp, ac = xpad[it % 2], acc[it % 2]
                    memset(POOL, xp[:, 0:2], 0.0, wr=[xp])
                    memset(POOL, xp[:, n + 2:n + 4], 0.0, wr=[xp])
                    dma_in(xp[:, 2:n + 2], qkvT[blk * 128:(blk + 1) * 128, s0:s0 + n], rd=[qkvT], wr=[xp])
                    ts(DVE, ac[:, 0:n], xp[:, 0:n], cw[:, blk, 0:1], None, ALU.mult, rd=[xp, cw], wr=[ac])
                    for j in range(1, 5):
                        stt(ac[:, 0:n], xp[:, j:j + n], cw[:, blk, j:j + 1], ac[:, 0:n], ALU.mult, ALU.add,
                            rd=[xp, cw, ac], wr=[ac])
                    act(ac[:, 0:n], ac[:, 0:n], AF.Silu, rd=[ac], wr=[ac])
                    if blk < 4:
                        sc = 64.0 if blk < 2 else 1.0
                        tt(POOL, xp[:, 0:n], ac[:, 0:n], ac[:, 0:n], ALU.mult, rd=[ac], wr=[xp])
                        for g0 in range(0, n, 512):
                            gn = min(512, n - g0)
                            pp = P[ci % 8]; r_ = rn[ci % 2]; ci += 1
                            mm(pp[:, 0:gn], C("bd"), xp[:, g0:g0 + gn], rd=[xp, cst], wr=[pp])
                            act(r_[:, 0:gn], pp[:, 0:gn], AF.Sqrt, bias=sc * EPS, scale=sc, rd=[pp], wr=[r_])
                            recip(r_[:, 0:gn], r_[:, 0:gn], rd=[r_], wr=[r_])
                            tt(DVE, ac[:, g0:g0 + gn], ac[:, g0:g0 + gn], r_[:, 0:gn], ALU.mult, rd=[ac, r_], wr=[ac])
                        dma_out(gqk[blk // 2, (blk % 2) * 128:(blk % 2 + 1) * 128, s0:s0 + n], ac[:, 0:n], rd=[ac], wr=[gqk])
                    if blk >= 2:
                        dst = ktm_d if blk < 4 else vtm_d
                        jj = blk % 2
                        for g0 in range(0, n, 512):
                            gn = min(512, n - g0)
                            pp = P[ci % 8]; ts_ = tst[ci % 2]; ci += 1
                            for c in range(gn // 128):
                                tr(pp[:, c * 128:(c + 1) * 128], ac[:, g0 + c * 128:g0 + (c + 1) * 128], rd=[ac], wr=[pp])
                            copy(ts_[:, 0:gn // 128, :], pp[:, 0:gn].rearrange("p (c f) -> p c f", f=128), rd=[pp], wr=[ts_])
                            dma_out(dst[s0 + g0:s0 + g0 + gn, jj * 128:(jj + 1) * 128].rearrange("(c p) f -> p c f", p=128),
                                    ts_[:, 0:gn // 128, :], rd=[ts_], wr=[dst])
                    it += 1
            S.barrier(); A.reset()
            ab = A.alloc("ab", [NCH, 16])
            dma_in(ab[:, :, :], PT[:, 256:272].rearrange("(c p) f -> p c f", p=128), rd=[PT], wr=[ab])
            prm = A.alloc("prm", [16])
            dma_in(prm[:, 0:8], alog_in[l, :, :], wr=[prm])
            dma_in(prm[:, 8:16], dtb_in[l, :, :], wr=[prm])
            act(prm[:, 0:8], prm[:, 0:8], AF.Exp, rd=[prm], wr=[prm])
            xg = A.alloc("xg", [NCH, 8]); t1 = A.alloc("t1", [NCH, 8]); t2 = A.alloc("t2", [NCH, 8])
            g_ = A.alloc("g", [2, NCH, 4]); beta = A.alloc("beta", [2, NCH, 4])
            gc = A.alloc("gc", [2, NCH, 4]); gtot = A.alloc("gtot", [2, NCH, 4])
            eg = A.alloc("eg", [2, NCH, 4]); edec = A.alloc("edec", [2, NCH, 4]); egtot = A.alloc("egtot", [2, NCH, 4])
            negegb = A.alloc("negegb", [2, NCH, 4])
            tt(DVE, xg[:, :, :], ab[:, :, 0:8], bcast(prm[:, 8:16], 1, NCH), ALU.add, rd=[ab, prm], wr=[xg])
            act(t1[:, :, :], xg[:, :, :], AF.Abs, rd=[xg], wr=[t1])
            act(t1[:, :, :], t1[:, :, :], AF.Exp, scale=-1.0, rd=[t1], wr=[t1])
            act(t1[:, :, :], t1[:, :, :], AF.Ln, bias=1.0, rd=[t1], wr=[t1])
            ts(DVE, t2[:, :, :], xg[:, :, :], 0.0, None, ALU.max, rd=[xg], wr=[t2])
            tt(DVE, t1[:, :, :], t1[:, :, :], t2[:, :, :], ALU.add, rd=[t1, t2], wr=[t1])
            stt(t1[:, :, :], t1[:, :, :], -1.0, bcast(prm[:, 0:8], 1, NCH), ALU.mult, ALU.mult, rd=[t1, prm], wr=[t1])
            act(ab[:, :, 8:16], ab[:, :, 8:16], AF.Sigmoid, rd=[ab], wr=[ab])
            for d in range(2):
                copy(g_[:, d, :, :], t1[:, :, d * 4:(d + 1) * 4], rd=[t1], wr=[g_], eng=DVE)
                copy(beta[:, d, :, :], ab[:, :, 8 + d * 4:8 + (d + 1) * 4], rd=[ab], wr=[beta], eng=DVE)
            for d in range(2):
                mm(P[0][:, d * NCH * 4:(d + 1) * NCH * 4], C("trif" if d == 0 else "trib"),
                   g_[:, d, :, :].rearrange("p c h -> p (c h)"), rd=[g_, cst], wr=[P[0]])
            mm(P[1][:, 0:2 * NCH * 4], C("ones"), g_[:, :, :, :].rearrange("p d c h -> p (d c h)"), rd=[g_, cst], wr=[P[1]])
            fl = "p d c h -> p (d c h)"
            copy(gc[:, :, :, :].rearrange(fl), P[0][:, 0:2 * NCH * 4], rd=[P[0]], wr=[gc], eng=DVE)
            copy(gtot[:, :, :, :].rearrange(fl), P[1][:, 0:2 * NCH * 4], rd=[P[1]], wr=[gtot], eng=DVE)
            act(eg[:, :, :, :].rearrange(fl), gc[:, :, :, :].rearrange(fl), AF.Exp, rd=[gc], wr=[eg])
            act(egtot[:, :, :, :].rearrange(fl), gtot[:, :, :, :].rearrange(fl), AF.Exp, rd=[gtot], wr=[egtot])
            tt(DVE, edec[:, :, :, :].rearrange(fl), gtot[:, :, :, :].rearrange(fl), gc[:, :, :, :].rearrange(fl),
               ALU.subtract, rd=[gtot, gc], wr=[edec])
            act(edec[:, :, :, :].rearrange(fl), edec[:, :, :, :].rearrange(fl), AF.Exp, rd=[edec], wr=[edec])
            stt(negegb[:, :, :, :].rearrange(fl), eg[:, :, :, :].rearrange(fl), -1.0, beta[:, :, :, :].rearrange(fl),
                ALU.mult, ALU.mult, rd=[eg, beta], wr=[negegb])
            betaT = A.alloc("betaT", [T], parts=8)
            dma_in(betaT[:, :], bT[:, :], rd=[bT], wr=[betaT])
            act(betaT[:, :], betaT[:, :], AF.Sigmoid, rd=[betaT], wr=[betaT])
            W = {}
            for d in range(2):
                w = {}
                for nm, shp, parts in (("qTc", [4, 128], 64), ("kTc", [4, 128], 64), ("ktmc", [256], 128), ("vtmc", [256], 128)):
                    w[nm] = [A.alloc(f"{nm}{d}{i}", shp, parts=parts) for i in range(2)]
                for nm, shp, parts in (("kbT", [4, 128], 64), ("dgx", [4, 128], 128), ("E", [4, 128], 128), ("V", [4, 512], 128),
                                       ("LMA", [4, 512], 128), ("ysb", [4, 64], 128), ("zsb", [4, 64], 128), ("Pa", [4, 128], 128), ("Pb", [4, 128], 128),
                                       ("La", [4, 128], 128), ("Lb", [4, 128], 128), ("Ma", [4, 128], 128), ("Mb", [4, 128], 128),
                                       ("vb", [4, 64], 128), ("rhs2", [4, 64], 128), ("vnew", [4, 64], 128),
                                       ("tmp", [4, 64], 128), ("kdec", [256], 128), ("Sst", [4, 64], 64), ("tmpS", [4, 64], 64)):
                    w[nm] = A.alloc(f"{nm}{d}", shp, parts=parts)
                w["osb"] = [A.alloc(f"osb{d}{i}", [4, 64]) for i in range(2)]
                W[d] = w
                memset(POOL, w["Sst"][:, :, :], 0.0, wr=[w["Sst"]])
            orders = chunk_orders()
            ident4 = bcast(C("ident"), 1, 4)
            o_sel = CONST_OFF["selb"][0]
            v4 = "p (h s) -> p h s"
            h4 = "p (h e) -> p h e"
            def chunk_body(step, d):
                c = orders[d][step]
                w = W[d]
                B0, B1, B2, B3 = P[d * 4], P[d * 4 + 1], P[d * 4 + 2], P[d * 4 + 3]
                tok = c * 128
                qTc, kTc, ktmc, vtmc = (w[nm][step % 2] for nm in ("qTc", "kTc", "ktmc", "vtmc"))
                dma_in(qTc[:, :, :], gqk[0, :, tok:tok + 128].rearrange("(h p) t -> p h t", p=64), rd=[gqk], wr=[qTc])
                dma_in(kTc[:, :, :], gqk[1, :, tok:tok + 128].rearrange("(h p) t -> p h t", p=64), rd=[gqk], wr=[kTc])
                dma_in(ktmc[:, :], ktm_d[tok:tok + 128, :], rd=[ktm_d], wr=[ktmc])
                dma_in(vtmc[:, :], vtm_d[tok:tok + 128, :], rd=[vtm_d], wr=[vtmc])
                for h in range(4):
                    sel = cst[0:8, o_sel + (d * 4 + h) * 64:o_sel + (d * 4 + h + 1) * 64]
                    mm(B3[0:64, h * 128:(h + 1) * 128], sel, betaT[:, tok:tok + 128], rd=[betaT, cst], wr=[B3])
                kbT = w["kbT"]
                tt(DVE, kbT[:, :, :], kTc[:, :, :], B3[0:64, :].rearrange(v4, h=4), ALU.mult, rd=[kTc, B3], wr=[kbT])
                yield
                for h in range(4):
                    hs = slice(h * 128, (h + 1) * 128)
                    mm(B0[:, hs], kbT[:, h, :], kTc[:, h, :], rd=[kbT, kTc], wr=[B0])
                    mm(B1[:, hs], kTc[:, h, :], kbT[:, h, :], rd=[kbT, kTc], wr=[B1])
                    mm(B2[:, hs], kTc[:, h, :], qTc[:, h, :], rd=[qTc, kTc], wr=[B2])
                gcc = gc[:, d, c, :]
                dgx, E, V, LMA = w["dgx"], w["E"], w["V"], w["LMA"]
                tt(DVE, dgx[:, :, :], ident4, bcast(gcc, 2, 128), ALU.mult, rd=[gc, cst], wr=[dgx])
                yield
                mm(B3[:, :], C("ones"), dgx[:, :, :].rearrange("p h s -> p (h s)"), rd=[dgx, cst], wr=[B3])
                tt(DVE, E[:, :, :], B3[:, :].rearrange(v4, h=4), bcast(gcc, 2, 128), ALU.subtract, rd=[B3, gc], wr=[E])
                Ef = E[:, :, :].rearrange("p h s -> p (h s)")
                stt(V[:, 0, :], Ef, -1.0, C(f"mL{d}"), ALU.mult, ALU.min, rd=[E, cst], wr=[V])
                tt(DVE, V[:, 1, :], Ef, C(f"mS{d}"), ALU.min, rd=[E, cst], wr=[V])
                tt(DVE, V[:, 2, :], Ef, C(f"mI{d}"), ALU.min, rd=[E, cst], wr=[V])
                tt(DVE, V[:, 3, :], Ef, C(f"mO{d}"), ALU.min, rd=[E, cst], wr=[V])
                act(V[:, :, :], V[:, :, :], AF.Exp, rd=[V], wr=[V])
                yield
                tt(DVE, LMA[:, 3, :], B1[:, :], V[:, 3, :], ALU.mult, rd=[B1, V], wr=[LMA])
                tt(DVE, LMA[:, 0, :], B0[:, :], V[:, 0, :], ALU.mult, rd=[B0, V], wr=[LMA])
                tt(DVE, LMA[:, 1, :], B1[:, :], V[:, 1, :], ALU.mult, rd=[B1, V], wr=[LMA])
                tt(DVE, LMA[:, 2, :], B2[:, :], V[:, 2, :], ALU.mult, rd=[B2, V], wr=[LMA])
                Lc = LMA[:, 0, :].rearrange(v4, h=4); Mc = LMA[:, 1, :].rearrange(v4, h=4)
                Lk, Mk = LMA, LMA
                ATm = LMA[:, 2, :].rearrange(v4, h=4)
                Pc, Pk = w["Pa"][:, :, :], w["Pa"]
                tt(DVE, Pc, ident4, Mc, ALU.subtract, rd=[LMA, cst], wr=[Pk])
                yield
                Moff = LMA[:, 3, :].rearrange(v4, h=4)
                for lvl in range(5):
                    last = lvl == 4
                    Ln_b = w["La"] if lvl % 2 == 0 else w["Lb"]
                    Mn_b = w["Ma"] if lvl % 2 == 0 else w["Mb"]
                    Pn_b = w["Pb"] if lvl % 2 == 0 else w["Pa"]
                    for h in range(4):
                        mm(B0[:, h * 128:(h + 1) * 128], Mc[:, h, :], Lc[:, h, :], rd=[Lk, Mk], wr=[B0])
                    if not last:
                        for h in range(4):
                            mm(B1[:, h * 128:(h + 1) * 128], Lc[:, h, :], Mc[:, h, :], rd=[Lk, Mk], wr=[B1])
                    yield
                    copy(Ln_b[:, :, :], B0[:, :].rearrange(v4, h=4), rd=[B0], wr=[Ln_b], eng=ACT)
                    if not last:
                        copy(Mn_b[:, :, :], B1[:, :].rearrange(v4, h=4), rd=[B1], wr=[Mn_b], eng=DVE)
                    for h in range(4):
                        mm(B3[:, h * 128:(h + 1) * 128], Ln_b[:, h, :], Pc[:, h, :], rd=[Ln_b, Pk], wr=[B3])
                    yield
                    tt(DVE, Pn_b[:, :, :], Pc, B3[:, :].rearrange(v4, h=4), ALU.add, rd=[Pk, B3], wr=[Pn_b])
                    Lc, Lk = Ln_b[:, :, :], Ln_b
                    if not last:
                        Mc, Mk = Mn_b[:, :, :], Mn_b
                    Pc, Pk = Pn_b[:, :, :], Pn_b
                Sst, tmpS = w["Sst"], w["tmpS"]
                for h in range(4):
                    mm(B2[:, h * 64:(h + 1) * 64], kTc[:, h, :], Sst[:, h, :], rd=[kTc, Sst], wr=[B2])
                    mm(B2[:, 256 + h * 64:256 + (h + 1) * 64], qTc[:, h, :], Sst[:, h, :], rd=[qTc, Sst], wr=[B2])
                vb, rhs2, vnew, tmp, kdec = w["vb"], w["rhs2"], w["vnew"], w["tmp"], w["kdec"]
                tt(POOL, vb[:, :, :], vtmc[:, :].rearrange(h4, h=4), bcast(beta[:, d, c, :], 2, 64), ALU.mult,
                   rd=[vtmc, beta], wr=[vb])
                tt(POOL, kdec[:, :].rearrange(h4, h=4), ktmc[:, :].rearrange(h4, h=4), bcast(edec[:, d, c, :], 2, 64),
                   ALU.mult, rd=[ktmc, edec], wr=[kdec])
                tt(DVE, rhs2[:, :, :], B2[:, 0:256].rearrange(h4, h=4), bcast(negegb[:, d, c, :], 2, 64), ALU.mult,
                   rd=[B2, negegb], wr=[rhs2])
                tt(DVE, rhs2[:, :, :], rhs2[:, :, :], vb[:, :, :], ALU.add, rd=[rhs2, vb], wr=[rhs2])
                yield
                ysb, zsb = w["ysb"], w["zsb"]
                for h in range(4):
                    mm(B3[:, h * 64:(h + 1) * 64], Pc[:, h, :], rhs2[:, h, :], rd=[Pk, rhs2], wr=[B3])
                copy(ysb[:, :, :], B3[:, 0:256].rearrange(h4, h=4), rd=[B3], wr=[ysb], eng=ACT)
                yield
                for h in range(4):
                    mm(B3[:, 256 + h * 64:256 + (h + 1) * 64], Moff[:, h, :], ysb[:, h, :], rd=[LMA, ysb], wr=[B3])
                tt(DVE, zsb[:, :, :], rhs2[:, :, :], B3[:, 256:512].rearrange(h4, h=4), ALU.subtract, rd=[rhs2, B3], wr=[zsb])
                yield
                for h in range(4):
                    mm(B3[:, h * 64:(h + 1) * 64], Pc[:, h, :], zsb[:, h, :], rd=[Pk, zsb], wr=[B3])
                copy(vnew[:, :, :], B3[:, 0:256].rearrange(h4, h=4), rd=[B3], wr=[vnew], eng=ACT)
                yield
                for h in range(4):
                    mm(B3[:, 256 + h * 64:256 + (h + 1) * 64], ATm[:, h, :], vnew[:, h, :], rd=[LMA, vnew], wr=[B3])
                osb = w["osb"][step % 2]
                tt(DVE, tmp[:, :, :], B2[:, 256:512].rearrange(h4, h=4), bcast(eg[:, d, c, :], 2, 64), ALU.mult,
                   rd=[B2, eg], wr=[tmp])
                tt(DVE, osb[:, :, :], tmp[:, :, :], B3[:, 256:512].rearrange(h4, h=4), ALU.add, rd=[tmp, B3], wr=[osb])
                yield
                if not (l == L - 1 and c < NCC):
                    dma_out(o_d[d][tok:tok + 128, :], osb[:, :, :].rearrange("p h e -> p (h e)"), rd=[osb], wr=[o_d[d]])
                for h in range(4):
                    mm(B0[0:64, h * 64:(h + 1) * 64], kdec[:, h * 64:(h + 1) * 64], vnew[:, h, :], rd=[kdec, vnew], wr=[B0])
                tt(POOL, tmpS[:, :, :], Sst[:, :, :], bcast(egtot[0:64, d, c, :], 2, 64), ALU.mult, rd=[Sst, egtot], wr=[tmpS])
                tt(DVE, Sst[:, :, :], tmpS[:, :, :], B0[0:64, 0:256].rearrange(h4, h=4), ALU.add, rd=[tmpS, B0], wr=[Sst])
            for step in range(NCH):
                gens = [chunk_body(step, 0), chunk_body(step, 1)]
                while gens:
                    for g_ in list(gens):
                        try:
                            next(g_)
                        except StopIteration:
                            gens.remove(g_)
            gating(l, gnw_in, 0, 0)

        def phaseC(l, b):
            S.barrier(); A.reset()
            wg = [A.alloc(f"wg{d}", [128], parts=32) for d in range(2)]
            nbg = A.alloc("nbg", [2])
            for d in range(2):
                memset(POOL, wg[d][:, :], 0.0, wr=[wg[d]])
                dma_in(wg[d][d * 16:(d + 1) * 16, :], w_gk[l, d, :, :], wr=[wg[d]])
            dma_in(nbg[:, :], bgk_in[l, :, :], wr=[nbg])
            ts(DVE, nbg[:, :], nbg[:, :], -1.0, None, ALU.mult, rd=[nbg], wr=[nbg])
            W = {}
            for d in range(2):
                w = {}
                for nm, shp, parts in (("gl", [512], 32), ("gq", [512], 128), ("gk", [512], 128), ("nl", [512], 128),
                                       ("cs", [512], 128), ("t", [512], 128), ("eq", [512], 128), ("ek", [512], 128),
                                       ("ekh", [512], 128), ("tot", [8], 128), ("qm", [4, 128], 128), ("sc", [4, 128], 128),
                                       ("khT", [128], 128), ("Sx", [256], 128)):
                    w[nm] = A.alloc(f"c{nm}{d}", shp, parts=parts)
                w["vt"] = [A.alloc(f"cvt{d}{i}", [256]) for i in range(2)]
                w["osb"] = [A.alloc(f"cosb{d}{i}", [256]) for i in range(2)]
                W[d] = w
                memset(POOL, w["Sx"][:, :], 0.0, wr=[w["Sx"]])
            gfwd = list(groups)
            gbwd = [groups[0]] + list(reversed(groups[1:]))
            gorder = [gfwd, gbwd]
            cnt = 0
            for gstep in range(len(groups)):
                for d in range(2):
                    w = W[d]
                    t0, n = gorder[d][gstep]
                    nc_ = n // 128
                    B0, B1, B2, B3 = P[d * 4], P[d * 4 + 1], P[d * 4 + 2], P[d * 4 + 3]
                    gl, gq, gk_, nl, cs, t_, eq, ek, ekh, tot = (w[k_] for k_ in
                                                                   ("gl", "gq", "gk", "nl", "cs", "t", "eq", "ek", "ekh", "tot"))
                    dma_in(gl[:, 0:n], glT[:, t0:t0 + n], rd=[glT], wr=[gl])
                    dma_in(gq[:, 0:n], gqT[:, t0:t0 + n], rd=[gqT], wr=[gq])
                    dma_in(gk_[:, 0:n], gkT[:, t0:t0 + n], rd=[gkT], wr=[gk_])
                    mm(B0[:, 0:n], wg[d][:, :], gl[:, 0:n], rd=[wg[d], gl], wr=[B0])
                    act(nl[:, 0:n], B0[:, 0:n], AF.Exp, bias=nbg[:, d:d + 1], scale=-1.0, rd=[B0, nbg], wr=[nl])
                    act(nl[:, 0:n], nl[:, 0:n], AF.Ln, bias=1.0, rd=[nl], wr=[nl])
                    S.op(DVE, lambda e, o_=cs[:, 0:n], a_=C("reset")[:, 0:n], b_=nl[:, 0:n]:
                         e.tensor_tensor_scan(out=o_, data0=a_, data1=b_, initial=0.0, op0=ALU.mult, op1=ALU.add),
                         reads=[nl, cst], writes=[cs])
                    c3 = "p (c s) -> p c s"
                    copy(tot[:, 0:nc_], cs[:, 0:n].rearrange(c3, s=128)[:, :, 127], rd=[cs], wr=[tot], eng=DVE)
                    if d == 1:
                        tt(DVE, t_[:, 0:n], nl[:, 0:n], cs[:, 0:n], ALU.subtract, rd=[nl, cs], wr=[t_])
                        tt(DVE, cs[:, 0:n].rearrange(c3, s=128), t_[:, 0:n].rearrange(c3, s=128),
                           bcast(tot[:, 0:nc_], 2, 128), ALU.add, rd=[t_, tot], wr=[cs])
                    act(eq[:, 0:n], cs[:, 0:n], AF.Exp, scale=-1.0 / 16, rd=[cs], wr=[eq])
                    act(ek[:, 0:n], cs[:, 0:n], AF.Exp, scale=1.0 / 16, rd=[cs], wr=[ek])
                    tt(DVE, t_[:, 0:n].rearrange(c3, s=128), cs[:, 0:n].rearrange(c3, s=128),
                       bcast(tot[:, 0:nc_], 2, 128), ALU.subtract, rd=[cs, tot], wr=[t_])
                    act(ekh[:, 0:n], t_[:, 0:n], AF.Exp, scale=1.0 / 16, rd=[t_], wr=[ekh])
                    act(tot[:, 4:4 + nc_], tot[:, 0:nc_], AF.Exp, scale=-1.0 / 16, rd=[tot], wr=[tot])
                    stt(eq[:, 0:n], gq[:, 0:n], 32 ** -0.5, eq[:, 0:n], ALU.mult, ALU.mult, rd=[gq, eq], wr=[eq])
                    tt(DVE, ek[:, 0:n], gk_[:, 0:n], ek[:, 0:n], ALU.mult, rd=[gk_, ek], wr=[ek])
                    tt(POOL, ekh[:, 0:n], gk_[:, 0:n], ekh[:, 0:n], ALU.mult, rd=[gk_, ekh], wr=[ekh])
                    clist = list(range(nc_)) if d == 0 else list(range(nc_ - 1, -1, -1))
                    for ci in clist:
                        tok = t0 + ci * 128
                        csl = slice(ci * 128, (ci + 1) * 128)
                        qm, sc, khT, Sx = w["qm"], w["sc"], w["khT"], w["Sx"]
                        vt = w["vt"][cnt % 2]; osb = w["osb"][cnt % 2]; cnt += 1
                        dma_in(vt[:, :], PT[tok:tok + 128, 272:528], rd=[PT], wr=[vt])
                        tt(DVE, qm[:, :, :], bcast(eq[:, csl], 1, 4), bcast(C("hm"), 2, 128), ALU.mult,
                           rd=[eq, cst], wr=[qm])
                        mm(B1[:, :], ek[:, csl], qm[:, :, :].rearrange("p h t -> p (h t)"), rd=[ek, qm], wr=[B1])
                        tt(DVE, sc[:, :, :].rearrange("p h t -> p (h t)"), B1[:, :], C(f"m01{d}"), ALU.mult,
                           rd=[B1, cst], wr=[sc])
                        tr(B2[:, 0:128], ekh[:, csl], rd=[ekh], wr=[B2])
                        copy(khT[:, :], B2[:, 0:128], rd=[B2], wr=[khT], eng=ACT)
                        for h in range(4):
                            mm(B3[:, h * 64:(h + 1) * 64], qm[:, h, :], Sx[:, h * 64:(h + 1) * 64], start=True, stop=False,
                               rd=[qm, Sx], wr=[B3])
                            mm(B3[:, h * 64:(h + 1) * 64], sc[:, h, :], vt[:, h * 64:(h + 1) * 64], start=False, stop=True,
                               rd=[sc, vt], wr=[B3])
                        copy(osb[:, :], B3[:, 0:256], rd=[B3], wr=[osb], eng=ACT)
                        if not (l == L - 1 and tok < TC):
                            dma_out(o_d[d][tok:tok + 128, :], osb[:, :], rd=[osb], wr=[o_d[d]])
                        mm(B2[:, 256:512], khT[:, :], vt[:, :], rd=[khT, vt], wr=[B2])
                        stt(Sx[:, :], Sx[:, :], tot[:, 4 + ci:5 + ci], B2[:, 256:512], ALU.mult, ALU.add,
                            rd=[Sx, tot, B2], wr=[Sx])
            gating(l, lnw_in, 528, 768)

        def phaseD(l, b):
            S.barrier(); A.reset()
            wq = A.alloc("wq", [3, 768]); wqs = A.alloc("wqs", [3, 768])
            wkn = A.alloc("wkn", [2, 8, 96]); wv = A.alloc("wv", [2, 8, 64])
            nq = A.alloc("nq", [3]); nkv = A.alloc("nkv", [2])
            rQ = A.alloc("rQ", [2, T], parts=96); rK = A.alloc("rK", [2, T], parts=32)
            dma_in(rQ[:, :, :], ropeQ_in[:, :, :], wr=[rQ])
            dma_in(rK[:, :, :], ropeK_in[:, :, :], wr=[rK])
            dma_in(nq[:, :], qnw_in[l, :, :], wr=[nq])
            dma_in(nkv[:, :], kvnw_in[l, :, :], wr=[nkv])
            memset(POOL, wqs[:, :, :], 0.0, wr=[wqs])
            memset(POOL, wkn[:, :, :, :], 0.0, wr=[wkn])
            for j in range(3):
                rows = slice(j * 128, (j + 1) * 128)
                dma_in(wq[:, j, :], w_uq[l, rows, :], wr=[wq])
                src = w_uq[l, rows, :].rearrange("k (h c) -> k h c", c=96)
                dst = wqs[:, j, :].rearrange("k (h c) -> k h c", c=96)
                dma_in(dst[:, :, 64:80], src[:, :, 80:96], wr=[wqs])
                dma_in(dst[:, :, 80:96], src[:, :, 64:80], wr=[wqs])
            for j in range(2):
                rows = slice(j * 128, (j + 1) * 128)
                src = w_ukv[l, rows, :].rearrange("k (h c) -> k h c", c=128)
                dma_in(wkn[:, j, :, 0:64], src[:, :, 0:64], wr=[wkn])
                dma_in(wv[:, j, :, :], src[:, :, 64:128], wr=[wv])
            for j in range(3):
                ts(DVE, wq[:, j, :], wq[:, j, :], nq[:, j:j + 1], None, ALU.mult, rd=[wq, nq], wr=[wq])
                ts(DVE, wqs[:, j, :], wqs[:, j, :], nq[:, j:j + 1], None, ALU.mult, rd=[wqs, nq], wr=[wqs])
            for j in range(2):
                ts(DVE, wkn[:, j, :, :].rearrange("p h c -> p (h c)"), wkn[:, j, :, :].rearrange("p h c -> p (h c)"),
                   nkv[:, j:j + 1], None, ALU.mult, rd=[wkn, nkv], wr=[wkn])
                ts(DVE, wv[:, j, :, :].rearrange("p h c -> p (h c)"), wv[:, j, :, :].rearrange("p h c -> p (h c)"),
                   nkv[:, j:j + 1], None, ALU.mult, rd=[wv, nkv], wr=[wv])
            cq = [A.alloc(f"cq{i}", [3, 512]) for i in range(2)]
            ckv = [A.alloc(f"ckv{i}", [2, 512]) for i in range(2)]
            kr = [A.alloc(f"kr{i}", [2, 512], parts=32) for i in range(2)]
            sq = A.alloc("sqd", [3, 512]); rr = A.alloc("rr", [512]); t1 = A.alloc("dt1", [512], parts=96)
            t2 = A.alloc("dt2", [512], parts=96)
            qst = [A.alloc(f"qst{i}", [512], parts=96) for i in range(2)]
            kst = [A.alloc(f"kst{i}", [512], parts=96) for i in range(2)]
            vst = [A.alloc(f"vst{i}", [512]) for i in range(2)]
            mx = A.alloc("mx", [16]); mt = A.alloc("mt", [2])
            memset(POOL, mx[:, :], 0.0, wr=[mx])
            pi = 0

            def nextP():
                nonlocal pi
                pi += 1
                return P[pi % 8]

            def norm_stat(src_buf, src_ap, rows, n, col):
                tt(POOL, t2[0:rows, 0:n], src_ap, src_ap, ALU.mult, rd=[src_buf], wr=[t2])
                pp = nextP()
                mm(pp[:, 0:n], C("ones")[0:rows, :], t2[0:rows, 0:n], rd=[t2, cst], wr=[pp])
                S.op(DVE, lambda e, o_=mt[:, 0:1], i_=pp[:, 0:n]: e.tensor_reduce(out=o_, in_=i_, axis=AX.X, op=ALU.max),
                     reads=[pp], writes=[mt])
                tt(DVE, mx[:, col:col + 1], mx[:, col:col + 1], mt[:, 0:1], ALU.max, rd=[mx, mt], wr=[mx])

            for gi, (t0, n) in enumerate(groups):
                cq_, ckv_, kr_ = cq[gi % 2], ckv[gi % 2], kr[gi % 2]
                dma_in(cq_[:, :, 0:n], cqT[:, t0:t0 + n].rearrange("(j p) t -> p j t", p=128), rd=[cqT], wr=[cq_])
                dma_in(ckv_[:, :, 0:n], ckvT[:, t0:t0 + n].rearrange("(j p) t -> p j t", p=128), rd=[ckvT], wr=[ckv_])
                dma_in(kr_[:, :, 0:n], krT[:, t0:t0 + n].rearrange("(j p) t -> p j t", p=32), rd=[krT], wr=[kr_])
                for (buf, nj, dim) in ((cq_, 3, 384.0), (ckv_, 2, 256.0)):
                    act(sq[:, 0:nj, 0:n], buf[:, 0:nj, 0:n], AF.Square, rd=[buf], wr=[sq])
                    pp = nextP()
                    for j in range(nj):
                        mm(pp[:, 0:n], C("ones"), sq[:, j, 0:n], start=(j == 0), stop=(j == nj - 1), rd=[sq, cst], wr=[pp])
                    act(rr[:, 0:n], pp[:, 0:n], AF.Sqrt, bias=EPS, scale=1.0 / dim, rd=[pp], wr=[rr])
                    recip(rr[:, 0:n], rr[:, 0:n], rd=[rr], wr=[rr])
                    tt(DVE, buf[:, 0:nj, 0:n], buf[:, 0:nj, 0:n], bcast(rr[:, 0:n], 1, nj), ALU.mult, rd=[buf, rr], wr=[buf])
                tt(DVE, kr_[:, 0, 0:n], kr_[:, 0, 0:n], rK[:, 0, t0:t0 + n], ALU.mult, rd=[kr_, rK], wr=[kr_])
                tt(POOL, kr_[:, 1, 0:n], kr_[:, 1, 0:n], rK[:, 1, t0:t0 + n], ALU.mult, rd=[kr_, rK], wr=[kr_])
                tt(DVE, kr_[:, 0, 0:n], kr_[:, 0, 0:n], kr_[:, 1, 0:n], ALU.add, rd=[kr_], wr=[kr_])
                need_q = (t0 >= TC) or (l < L - 1)
                for h in range(8):
                    pp = nextP(); ks = kst[h % 2]
                    for j in range(2):
                        mm(pp[0:96, 0:n], wkn[:, j, h, :], ckv_[:, j, 0:n], start=(j == 0), stop=False, rd=[wkn, ckv_], wr=[pp])
                    mm(pp[0:96, 0:n], C("esel", 32), kr_[:, 0, 0:n], start=False, stop=True, rd=[kr_, cst], wr=[pp])
                    copy(ks[:, 0:n], pp[0:96, 0:n], rd=[pp], wr=[ks])
                    dma_out(KT[h, :, t0:t0 + n], ks[:, 0:n], rd=[ks], wr=[KT])
                    norm_stat(ks, ks[:, 0:n], 96, n, 8 + h)
                    if not need_q:
                        continue
                    p1 = nextP(); p2 = nextP(); qs = qst[h % 2]
                    for j in range(3):
                        mm(p1[0:96, 0:n], wq[:, j, h * 96:(h + 1) * 96], cq_[:, j, 0:n], start=(j == 0), stop=(j == 2),
                           rd=[wq, cq_], wr=[p1])
                    for j in range(3):
                        mm(p2[0:96, 0:n], wqs[:, j, h * 96:(h + 1) * 96], cq_[:, j, 0:n], start=(j == 0), stop=(j == 2),
                           rd=[wqs, cq_], wr=[p2])
                    tt(DVE, t1[:, 0:n], p1[0:96, 0:n], rQ[:, 0, t0:t0 + n], ALU.mult, rd=[p1, rQ], wr=[t1])
                    tt(DVE, qs[:, 0:n], p2[0:96, 0:n], rQ[:, 1, t0:t0 + n], ALU.mult, rd=[p2, rQ], wr=[qs])
                    tt(POOL, qs[:, 0:n], qs[:, 0:n], t1[:, 0:n], ALU.add, rd=[qs, t1], wr=[qs])
                    dma_out(QT[h, :, t0:t0 + n], qs[:, 0:n], rd=[qs], wr=[QT])
                    norm_stat(qs, qs[:, 0:n], 96, n, h)
                for i in range(n // 128):
                    pp = nextP(); vs = vst[i % 2]
                    for j in range(2):
                        mm(pp[:, :], ckv_[:, j, i * 128:(i + 1) * 128], wv[:, j, :, :].rearrange("p h c -> p (h c)"),
                           start=(j == 0), stop=(j == 1), rd=[ckv_, wv], wr=[pp])
                    copy(vs[:, :], pp[:, :], rd=[pp], wr=[vs])
                    dma_out(Vd[t0 + i * 128:t0 + (i + 1) * 128, :], vs[:, :], rd=[vs], wr=[Vd])
            tt(DVE, small[:, 0:8], mx[:, 0:8], mx[:, 8:16], ALU.mult, rd=[mx], wr=[small])
            act(small[:, 0:8], small[:, 0:8], AF.Sqrt, rd=[small], wr=[small])
            ts(DVE, small[:, 0:8], small[:, 0:8], -MLA_SCALE, None, ALU.mult, rd=[small], wr=[small])
            S.barrier(); A.reset()
            kt = [A.alloc(f"kt{i}", [T], parts=96) for i in range(2)]
            vh = [A.alloc(f"vh{i}", [NCH, 65]) for i in range(2)]
            for i in range(2):
                memset(POOL, vh[i][:, :, 64:65], 1.0, wr=[vh[i]])
            qg = [A.alloc(f"qg{i}", [512], parts=96) for i in range(2)]
            zg = [A.alloc(f"zg{i}", [512], parts=64) for i in range(2)]
            pT = [A.alloc(f"pT{i}", [512]) for i in range(4)]
            ot = [A.alloc(f"ot{i}", [512], parts=65) for i in range(2)]
            rd_ = [A.alloc(f"rden{i}", [512], parts=64) for i in range(2)]
            ys = [A.alloc(f"ysd{i}", [512], parts=64) for i in range(2)]
            qi = 0
            si = 0
            for h in range(8):
                kt_, vh_ = kt[h % 2], vh[h % 2]
                dma_in(kt_[:, :], KT[h, :, :], rd=[KT], wr=[kt_])
                dma_in(vh_[:, :, 0:64], Vd[:, h * 64:(h + 1) * 64].rearrange("(c p) e -> p c e", p=128), rd=[Vd], wr=[vh_])
                for (t0, n) in groups:
                    if t0 < TC:
                        if l == L - 1:
                            continue
                        kchunks = list(range(NCC))
                    else:
                        kchunks = list(range(NCH))
                    q_, z_, o_, r_, y_ = qg[qi % 2], zg[qi % 2], ot[qi % 2], rd_[qi % 2], ys[qi % 2]
                    pacc = P[6 + qi % 2]
                    qi += 1
                    dma_in(q_[:, 0:n], QT[h, :, t0:t0 + n], rd=[QT], wr=[q_])
                    dma_in(z_[:, 0:n], zmT[h * 64:(h + 1) * 64, t0:t0 + n], rd=[zmT], wr=[z_])
                    act(z_[:, 0:n], z_[:, 0:n], AF.Silu, rd=[z_], wr=[z_])
                    pend = []
                    nk = len(kchunks)

                    def pv(ki_, c_, pp_):
                        mm(pacc[0:65, 0:n], vh_[:, c_, :], pp_[:, 0:n], start=(ki_ == 0), stop=(ki_ == nk - 1),
                           rd=[vh_, pp_], wr=[pacc])

                    for ki, c in enumerate(kchunks):
                        ps_ = P[si % 4]; p_ = pT[si % 4]; si += 1
                        mm(ps_[:, 0:n], kt_[:, c * 128:(c + 1) * 128], q_[:, 0:n], rd=[kt_, q_], wr=[ps_])
                        act(p_[:, 0:n], ps_[:, 0:n], AF.Exp, bias=small[:, h:h + 1], scale=MLA_SCALE, rd=[ps_, small], wr=[p_])
                        pend.append((ki, c, p_))
                        if len(pend) > 2:
                            pv(*pend.pop(0))
                    for it_ in pend:
                        pv(*it_)
                    copy(o_[:, 0:n], pacc[0:65, 0:n], rd=[pacc], wr=[o_], eng=DVE)
                    pd = P[4 + qi % 2]
                    mm(pd[0:64, 0:n], C("selrow", 65), o_[:, 0:n], rd=[o_, cst], wr=[pd])
                    recip(r_[:, 0:n], pd[0:64, 0:n], rd=[pd], wr=[r_])
                    tt(DVE, y_[:, 0:n], o_[0:64, 0:n], r_[:, 0:n], ALU.mult, rd=[o_, r_], wr=[y_])
                    tt(POOL, y_[:, 0:n], y_[:, 0:n], z_[:, 0:n], ALU.mult, rd=[y_, z_], wr=[y_])
                    dma_out(yT[256 + h * 64:256 + (h + 1) * 64, t0:t0 + n], y_[:, 0:n], rd=[y_], wr=[yT])

        out_events = []

        def phaseE(l, b):
            S.barrier(); A.reset()
            wo = A.alloc("wo", [8, D])
            for k in range(8):
                dma_in(wo[:, k, :], w_out[l, k * 128:(k + 1) * 128, :], wr=[wo])
            fn = A.alloc("fn", [D])
            if l == L - 1:
                dma_in(fn[:, :], fnw_in[:, :], wr=[fn])
            yt = [A.alloc(f"yt{i}", [8, 128]) for i in range(2)]
            xt = [A.alloc(f"ext{i}", [D]) for i in range(2)]
            tm = [A.alloc(f"etm{i}", [D]) for i in range(2)]
            junk = A.alloc("ejunk", [D])
            ss = [A.alloc(f"ess{i}", [2]) for i in range(2)]
            ti = 0
            for c in range(NCH):
                tok = c * 128
                if tok < TC and l == L - 1:
                    continue
                gb = gate_bc[NB if tok < TC else b]
                y_, x_, t_, s_ = yt[ti % 2], xt[ti % 2], tm[ti % 2], ss[ti % 2]
                pa, pb = P[(ti % 4) * 2], P[(ti % 4) * 2 + 1]
                ti += 1
                dma_in(y_[:, :, :], yT[:, tok:tok + 128].rearrange("(k p) t -> p k t", p=128), rd=[yT], wr=[y_])
                src, srd = x_src(l, b, tok)
                dma_in(x_[:, :], src, rd=srd, wr=[x_])
                for half, pp in ((0, pa), (1, pb)):
                    for k in range(8):
                        mm(pp[:, :], y_[:, k, :], wo[:, k, half * 512:(half + 1) * 512], start=(k == 0), stop=(k == 7),
                           rd=[y_, wo], wr=[pp])
                    hs = slice(half * 512, (half + 1) * 512)
                    tt(DVE, t_[:, hs], pp[:, :], gb[:, hs], ALU.mult, rd=[pp, gb], wr=[t_])
                tt(POOL, t_[:, :], t_[:, :], x_[:, :], ALU.add, rd=[t_, x_], wr=[t_])
                if l < L - 1:
                    dma_out(xs[b][tok:tok + 128, :], t_[:, :], rd=[t_], wr=[xs[b]])
                else:
                    act(junk[:, :], t_[:, :], AF.Square, accum=s_[:, 0:1], rd=[t_], wr=[junk, s_])
                    act(s_[:, 0:1], s_[:, 0:1], AF.Sqrt, bias=EPS, scale=1.0 / D, rd=[s_], wr=[s_])
                    recip(s_[:, 1:2], s_[:, 0:1], rd=[s_], wr=[s_])
                    stt(t_[:, :], t_[:, :], s_[:, 1:2], fn[:, :], ALU.mult, ALU.mult, rd=[t_, s_, fn], wr=[t_])
                    ev = dma_out(out[b, tok - TC:tok - TC + 128, :], t_[:, :], rd=[t_], wr=[("out", b, c)])
                    out_events.append(ev)

        for l in range(L):
            if "0" in cfg.phases:
                phase0(l)
            for b in range(NB):
                if "A" in cfg.phases:
                    phaseA(l, b)
                if "B" in cfg.phases:
                    phaseB(l, b)
                if "C" in cfg.phases:
                    phaseC(l, b)
                if "D" in cfg.phases:
                    phaseD(l, b)
                if "E" in cfg.phases:
                    phaseE(l, b)
        S.barrier()
        fin = list(S.floor.items())
        S.emit(final_events=fin)
        build.stats = {k: len(v) for k, v in S.ops.items()}
    return nc


def _col(v, nchunk):
    return np.ascontiguousarray(np.asarray(v, np.float32).reshape(nchunk, 128).T)


def make_in_maps(inp, n_cores, NB, TL, L):
    f = lambda a: np.ascontiguousarray(np.asarray(a, dtype=np.float32))
    rq, rk = _rope_tables(TL)
    shared = {
        "w_ada": f(inp["w_ada"]), "w_in": f(inp["w_in"]), "w_uq": f(inp["mla_w_uq"]), "w_ukv": f(inp["mla_w_ukv"]),
        "w_gk": f(inp["gla_w_gk"]), "w_out": f(inp["w_out"]),
        "b_adaT": np.stack([_col(inp["b_ada"][l], 24) for l in range(L)]),
        "convw": np.ascontiguousarray(np.asarray(inp["gdn_conv_w"], np.float32).reshape(L, 5, 6, 128).transpose(0, 3, 2, 1)),
        "alog": np.ascontiguousarray(np.broadcast_to(np.asarray(inp["gdn_a_log"], np.float32).reshape(L, 1, 8), (L, 128, 8))),
        "dtb": np.ascontiguousarray(np.broadcast_to(np.asarray(inp["gdn_dt_bias"], np.float32).reshape(L, 1, 8), (L, 128, 8))),
        "gnw": np.ascontiguousarray(np.broadcast_to(np.tile(np.asarray(inp["gdn_norm_w"], np.float32), (1, 4))[:, None, :], (L, 128, 256))),
        "lnw": np.ascontiguousarray(np.broadcast_to(np.tile(np.asarray(inp["gla_norm_w"], np.float32), (1, 4))[:, None, :], (L, 128, 256))),
        "qnw": np.stack([_col(inp["mla_q_norm_w"][l], 3) for l in range(L)]),
        "kvnw": np.stack([_col(inp["mla_kv_norm_w"][l], 2) for l in range(L)]),
        "b_gkT": np.stack([np.ascontiguousarray(np.asarray(inp["gla_b_gk"][l], np.float32).T) for l in range(L)]),
        "fnw": np.ascontiguousarray(np.broadcast_to(np.asarray(inp["final_norm_w"], np.float32)[None, :], (128, D))),
        "consts": CONST_ARR, "ropeQ": rq, "ropeK": rk,
    }
    maps = []
    x = np.asarray(inp["x"], np.float32); ctx = np.asarray(inp["ctx"], np.float32)
    c = np.asarray(inp["c"], np.float32); c_ctx = np.asarray(inp["c_ctx"], np.float32)
    for i in range(n_cores):
        bs = slice(i * NB, (i + 1) * NB)
        vecs = np.concatenate([c[bs], c_ctx[None, :]], axis=0)
        cT = np.ascontiguousarray(vecs.reshape(NB + 1, 8, 128).transpose(2, 1, 0))
        m = dict(shared)
        m["x"] = np.ascontiguousarray(x[bs]); m["ctx"] = np.ascontiguousarray(ctx[bs]); m["cT"] = cT
        maps.append(m)
    return maps


_NC_CACHE = {}


def kernel(**inputs):
    B, TL, _ = inputs["x"].shape
    L = inputs["w_in"].shape[0]
    n_cores = 8
    NB = B // n_cores
    key = (TL, NB, L)
    if key not in _NC_CACHE:
        _NC_CACHE[key] = build(Cfg(TL=TL, NB=NB, L=L))
    nc = _NC_CACHE[key]
    maps = make_in_maps(inputs, n_cores, NB, TL, L)
    res = run_bass_kernel_spmd(nc, maps, core_ids=list(range(n_cores)))
    return np.concatenate([np.asarray(r["out"], np.float32) for r in res.results], axis=0)
```
